# Optimizing a Trainium2 kernel written in Bass

```python
import math
import jax, jax.numpy as jnp
from jax import lax
import numpy as np

D_MODEL = 1024
BATCH = 4
SEQ = 8192
DEPTH = 2
DEC_BATCH = 16
DEC_SEQ = 16
PAST_LEN = 4096

CHUNK = 64
N_META = 16
N_MIXERS = 2
N_LAYERS_A = (DEPTH + N_MIXERS - 1) // N_MIXERS
N_LAYERS_B = DEPTH // N_MIXERS
HG_HEADS = 8
HG_DK = 128
HG_DV = D_MODEL // HG_HEADS
HG_DIM = HG_HEADS * HG_DK
HG_VDIM = HG_HEADS * HG_DV
GLA_BLOCK = 16
D_RNN = D_MODEL
RG_BLOCKS = 8
RG_BW = D_RNN // RG_BLOCKS
CONV_W = 4
RG_C = 8.0
D_FF = 2816
FFN_RES = 0.5
EPS = 1e-6

kernel_name = 'hgrn2_rglru_macaron_stream_step'


def rmsnorm(x, g):
    xf = x.astype(jnp.float32)
    y = xf * lax.rsqrt(jnp.mean(xf * xf, axis=-1, keepdims=True) + EPS)
    return (y * g.astype(jnp.float32)).astype(x.dtype)


def swiglu_half(x, g, w_in, w_out):
    a, b = jnp.split(rmsnorm(x, g) @ w_in, 2, axis=-1)
    return (jax.nn.silu(a) * b) @ w_out


def gla_blocked(q, k, v, logf, s0, block):
    B, T, H, DK = q.shape
    DV = v.shape[-1]
    n = T // block

    def to_blocks(t):
        return t.reshape(B, n, block, H, t.shape[-1]).transpose(1, 0, 3, 2, 4)

    mask = jnp.tril(jnp.ones((block, block), dtype=bool))

    def step(S, inp):
        qc, kc, vc, gc = inp
        b = jnp.cumsum(gc, axis=-2)
        b_last = b[:, :, -1:, :]
        q_d = qc * jnp.exp(b)
        k_d = kc * jnp.exp(-b)
        scores = jnp.where(mask, jnp.einsum('bhtk,bhsk->bhts', q_d, k_d), 0.0)
        o = (jnp.einsum('bhts,bhsv->bhtv', scores, vc)
             + jnp.einsum('bhtk,bhkv->bhtv', q_d, S))
        S_new = (jnp.exp(b_last[:, :, 0, :])[..., None] * S
                 + jnp.einsum('bhsk,bhsv->bhkv', kc * jnp.exp(b_last - b), vc))
        return S_new, o

    S, o = lax.scan(step, s0.astype(jnp.float32),
                    (to_blocks(q), to_blocks(k), to_blocks(v), to_blocks(logf)))
    o = o.transpose(1, 0, 3, 2, 4).reshape(B, T, H, DV)
    return o, S


def hgrn2_mixer(x, s0, w_in, lb, onorm, w_out):
    B, T, _ = x.shape
    proj = (x @ w_in).astype(jnp.float32)
    q, fr, i, g = jnp.split(proj, [HG_DIM, 2 * HG_DIM, 2 * HG_DIM + HG_VDIM], axis=-1)
    f = lb + (1.0 - lb) * jax.nn.sigmoid(fr)
    logf = jnp.log(f)
    k = 1.0 - f
    q = jax.nn.silu(q)
    o, S = gla_blocked(q.reshape(B, T, HG_HEADS, HG_DK), k.reshape(B, T, HG_HEADS, HG_DK),
                       i.reshape(B, T, HG_HEADS, HG_DV), logf.reshape(B, T, HG_HEADS, HG_DK),
                       s0, math.gcd(T, GLA_BLOCK))
    o = o * lax.rsqrt(jnp.mean(o * o, axis=-1, keepdims=True) + EPS)
    o = o * onorm.astype(jnp.float32).reshape(HG_HEADS, HG_DV)
    o = o.reshape(B, T, HG_VDIM) * jax.nn.silu(g)
    return o.astype(x.dtype) @ w_out, S


def _lin_combine(left, right):
    a1, b1 = left
    a2, b2 = right
    return a1 * a2, a2 * b1 + b2


def rglru_mixer(x, h0, conv0, w_in, conv_w, conv_b, wa, ba, wx, bx, lam, w_out):
    B, T, _ = x.shape
    xb, gb = jnp.split(x @ w_in, 2, axis=-1)
    xpad = jnp.concatenate([conv0.astype(xb.dtype), xb], axis=1)
    conv = conv_b + xpad[:, 0:T] * conv_w[0]
    for j in range(1, CONV_W):
        conv = conv + xpad[:, j:j + T] * conv_w[j]
    new_conv = xpad[:, T:]
    cf = conv.astype(jnp.float32)
    cb = cf.reshape(B, T, RG_BLOCKS, RG_BW)
    r = jax.nn.sigmoid(jnp.einsum('btnc,ncd->btnd', cb, wa.astype(jnp.float32)).reshape(B, T, D_RNN) + ba)
    ig = jax.nn.sigmoid(jnp.einsum('btnc,ncd->btnd', cb, wx.astype(jnp.float32)).reshape(B, T, D_RNN) + bx)
    log_a = -RG_C * r * jax.nn.softplus(-lam.astype(jnp.float32))
    a = jnp.exp(log_a)
    u = jnp.sqrt(-jnp.expm1(2.0 * log_a)) * (ig * cf)
    a_cum, b_cum = lax.associative_scan(_lin_combine, (a, u), axis=1)
    h = a_cum * h0.astype(jnp.float32)[:, None, :] + b_cum
    y = (h * jax.nn.gelu(gb.astype(jnp.float32))).astype(x.dtype)
    return y @ w_out, h[:, -1], new_conv


def run_trunk(h, s_hgrn, s_h, s_conv, ffn_norm, ffn_w_in, ffn_w_out, mix_norm,
              a_w_in, a_lb, a_onorm, a_w_out, b_w_in, b_conv_w, b_conv_b,
              b_wa, b_ba, b_wx, b_bx, b_lambda, b_w_out):
    lb_all = jnp.cumsum(jax.nn.softmax(a_lb.astype(jnp.float32), axis=0), axis=0)
    new_S, new_h, new_c = [], [], []
    for layer in range(DEPTH):
        h = h + FFN_RES * swiglu_half(h, ffn_norm[layer, 0], ffn_w_in[layer, 0], ffn_w_out[layer, 0])
        hn = rmsnorm(h, mix_norm[layer])
        j = layer // N_MIXERS
        if layer % N_MIXERS == 0:
            out, S = hgrn2_mixer(hn, s_hgrn[j], a_w_in[j], lb_all[j], a_onorm[j], a_w_out[j])
            new_S.append(S)
        else:
            out, hl, cv = rglru_mixer(hn, s_h[j], s_conv[j], b_w_in[j], b_conv_w[j], b_conv_b[j],
                                      b_wa[j], b_ba[j], b_wx[j], b_bx[j], b_lambda[j], b_w_out[j])
            new_h.append(hl)
            new_c.append(cv)
        h = h + out
        h = h + FFN_RES * swiglu_half(h, ffn_norm[layer, 1], ffn_w_in[layer, 1], ffn_w_out[layer, 1])
    return h, jnp.stack(new_S), jnp.stack(new_h), jnp.stack(new_c)


def setup_inputs(seed: int = 0) -> dict:
    key = jax.random.key(seed)
    ks = jax.random.split(key, 24)
    f32 = jnp.float32

    def nrm(k, shape, scale):
        return jax.random.normal(k, shape, f32) * scale

    u = jax.random.uniform(ks[21], (N_LAYERS_B, D_RNN), f32, 0.9, 0.999)
    s = u ** (1.0 / RG_C)
    lam = jnp.log(s) - jnp.log1p(-s)
    return {
        'x_prompt': nrm(ks[0], (BATCH, SEQ, D_MODEL), 1.0),
        'x_sample': nrm(ks[1], (DEC_BATCH, DEC_SEQ, D_MODEL), 1.0),
        'state_hgrn': nrm(ks[2], (N_LAYERS_A, DEC_BATCH, HG_HEADS, HG_DK, HG_DV), 0.5),
        'state_rglru': nrm(ks[3], (N_LAYERS_B, DEC_BATCH, D_RNN), 0.5),
        'state_conv': nrm(ks[4], (N_LAYERS_B, DEC_BATCH, CONV_W - 1, D_RNN), 1.0),
        'meta_tokens': nrm(ks[5], (N_META, D_MODEL), 1.0),
        'ffn_norm': 1.0 + nrm(ks[6], (DEPTH, 2, D_MODEL), 0.05),
        'ffn_w_in': nrm(ks[7], (DEPTH, 2, D_MODEL, 2 * D_FF), D_MODEL ** -0.5),
        'ffn_w_out': nrm(ks[8], (DEPTH, 2, D_FF, D_MODEL), D_FF ** -0.5),
        'mix_norm': 1.0 + nrm(ks[9], (DEPTH, D_MODEL), 0.05),
        'a_w_in': nrm(ks[10], (N_LAYERS_A, D_MODEL, 2 * HG_DIM + 2 * HG_VDIM), D_MODEL ** -0.5),
        'a_lb': nrm(ks[11], (N_LAYERS_A + 1, HG_DIM), 0.5),
        'a_onorm': 1.0 + nrm(ks[12], (N_LAYERS_A, HG_VDIM), 0.05),
        'a_w_out': nrm(ks[13], (N_LAYERS_A, HG_VDIM, D_MODEL), HG_VDIM ** -0.5),
        'b_w_in': nrm(ks[14], (N_LAYERS_B, D_MODEL, 2 * D_RNN), D_MODEL ** -0.5),
        'b_conv_w': nrm(ks[15], (N_LAYERS_B, CONV_W, D_RNN), CONV_W ** -0.5),
        'b_conv_b': nrm(ks[16], (N_LAYERS_B, D_RNN), 0.02),
        'b_wa': nrm(ks[17], (N_LAYERS_B, RG_BLOCKS, RG_BW, RG_BW), RG_BW ** -0.5),
        'b_ba': nrm(ks[18], (N_LAYERS_B, D_RNN), 0.02),
        'b_wx': nrm(ks[19], (N_LAYERS_B, RG_BLOCKS, RG_BW, RG_BW), RG_BW ** -0.5),
        'b_bx': nrm(ks[20], (N_LAYERS_B, D_RNN), 0.02),
        'b_lambda': lam,
        'b_w_out': nrm(ks[22], (N_LAYERS_B, D_RNN, D_MODEL), D_RNN ** -0.5),
        'final_norm': 1.0 + nrm(ks[23], (D_MODEL,), 0.05),
    }


def reference(x_prompt, x_sample, state_hgrn, state_rglru, state_conv, meta_tokens,
              ffn_norm, ffn_w_in, ffn_w_out, mix_norm, a_w_in, a_lb, a_onorm, a_w_out,
              b_w_in, b_conv_w, b_conv_b, b_wa, b_ba, b_wx, b_bx, b_lambda, b_w_out, final_norm):
    dt = x_prompt.dtype
    B = x_prompt.shape[0]
    meta = jnp.broadcast_to(meta_tokens.astype(dt)[None], (B, N_META, D_MODEL))
    xp = jnp.concatenate([meta, x_prompt], axis=1)
    zS = jnp.zeros((N_LAYERS_A, B, HG_HEADS, HG_DK, HG_DV), jnp.float32)
    zh = jnp.zeros((N_LAYERS_B, B, D_RNN), jnp.float32)
    zc = jnp.zeros((N_LAYERS_B, B, CONV_W - 1, D_RNN), dt)
    hp, hgrn_p, rglru_p, conv_p = run_trunk(
        xp, zS, zh, zc, ffn_norm, ffn_w_in, ffn_w_out, mix_norm, a_w_in, a_lb, a_onorm, a_w_out,
        b_w_in, b_conv_w, b_conv_b, b_wa, b_ba, b_wx, b_bx, b_lambda, b_w_out)
    y_prompt = rmsnorm(hp[:, N_META:], final_norm)
    hs, hgrn_s, rglru_s, conv_s = run_trunk(
        x_sample, state_hgrn, state_rglru, state_conv, ffn_norm, ffn_w_in, ffn_w_out, mix_norm,
        a_w_in, a_lb, a_onorm, a_w_out, b_w_in, b_conv_w, b_conv_b, b_wa, b_ba, b_wx, b_bx,
        b_lambda, b_w_out)
    y_sample = rmsnorm(hs, final_norm)
    return (y_prompt, y_sample, hgrn_p, hgrn_s, rglru_p, rglru_s, conv_p, conv_s)
```

```python
import numpy as np
import concourse.bass as bass
import concourse.mybir as mybir
from concourse.bass_utils import run_bass_kernel_spmd

F32 = mybir.dt.float32
BF16 = mybir.dt.bfloat16
AF = mybir.ActivationFunctionType
ALU = mybir.AluOpType

D = 1024
NK = 8
DFF = 2816
NF = 22
NMETA = 16
EPS = 1e-6
NSLOT = 5
USZ = 4096


class Buf:
    __slots__ = ("name", "w", "r", "dsem", "dcnt")

    def __init__(self, name):
        self.name = name
        self.w = None
        self.r = {}
        self.dsem = None
        self.dcnt = 0


class FW:
    def __init__(self, nc):
        self.nc = nc
        self.eng = {"pe": nc.tensor, "act": nc.scalar, "dve": nc.vector,
                    "pool": nc.gpsimd, "sp": nc.sync}
        self.sem = {}
        self.cnt = {}
        self.waited = {}
        self.semobj = {}
        for e in self.eng:
            self.sem[e] = nc.alloc_semaphore(name="prog_" + e)
            self.semobj[e] = self.sem[e]
            self.cnt[e] = 0
            self.waited[e] = {}
        self.ninst = 0

    def _need(self, reads, writes):
        need = {}

        def add(clk):
            if clk is None:
                return
            k, v = clk
            if v > need.get(k, 0):
                need[k] = v
        for b in reads:
            add(b.w)
        for b in writes:
            add(b.w)
            for k, v in b.r.items():
                add((k, v))
        return need

    def _emit_waits(self, e, need, skip_self=False):
        eng = self.eng[e]
        wd = self.waited[e]
        for k, v in need.items():
            if skip_self and k == e:
                continue
            if wd.get(k, 0) >= v:
                continue
            eng.wait_ge(self.semobj[k], v)
            wd[k] = v

    def op(self, e, fn, reads=(), writes=()):
        need = self._need(reads, writes)
        self._emit_waits(e, need, skip_self=(e == "pe"))
        ins = fn()
        self.cnt[e] += 1
        c = self.cnt[e]
        ins.then_inc(self.sem[e], 1)
        for b in writes:
            b.w = (e, c)
            b.r = {}
        for b in reads:
            if b not in writes:
                b.r[e] = c
        self.ninst += 1
        return ins

    def dma(self, q, out_ap, in_ap, owner, reads=(), writes=(), **kw):
        need = self._need(reads, writes)
        self._emit_waits(q, need)
        k = "d_" + owner.name
        if owner.dsem is None:
            owner.dsem = self.nc.alloc_semaphore(name=k)
            self.semobj[k] = owner.dsem
        owner.dcnt += 1
        v = 16 * owner.dcnt
        ins = self.eng[q].dma_start(out=out_ap, in_=in_ap, **kw)
        ins.then_inc(owner.dsem, 16)
        for b in writes:
            b.w = (k, v)
            b.r = {}
        for b in reads:
            if b not in writes:
                b.r[k] = v
        self.ninst += 1
        return ins

    def handoff(self, olds, news):
        merged = {}
        for b in olds:
            if b.w is not None and b.w[1] > merged.get(b.w[0], 0):
                merged[b.w[0]] = b.w[1]
            for k, v in b.r.items():
                if v > merged.get(k, 0):
                    merged[k] = v
        for b in news:
            b.w = None
            b.r = dict(merged)

    def final_wait(self, e, bufs):
        self._emit_waits(e, self._need([], bufs))


def unit_table():
    units = []

    def ffn(fi):
        for j in range(11):
            units.append(("ffn_in", (fi, j), 4096))
        for mo in range(8):
            units.append(("ffn_out", (fi, mo), NF * 128))
    ffn(0)
    for u in range(8):
        units.append(("a_in", (u,), 4096))
    for u in range(2):
        units.append(("a_out", (u,), 4096))
    ffn(1)
    ffn(2)
    for u in range(4):
        units.append(("b_in", (u,), 4096))
    units.append(("b_gate", (), 2048))
    for u in range(2):
        units.append(("b_out", (u,), 4096))
    ffn(3)
    offs = []
    o = 0
    for _, _, sz in units:
        offs.append(o)
        o += sz
    return units, offs, o


UNITS, UOFF, WTOT = unit_table()
NUNIT = len(UNITS)

NPAR = 18
(P_FFN0, P_MIX0, P_FIN, P_LB0, P_LB1, P_ONORM, P_CW0, P_CB, P_BA, P_BX, P_LAM) = (0, 4, 6, 7, 8, 9, 10, 14, 15, 16, 17)


def build_program(T, NS=2, NSMP=16, dbg=None):
    nc = bass.Bass("TRN2", target_bir_lowering=False)
    fw = FW(nc)
    V = nc.vector
    A = nc.scalar
    PE = nc.tensor

    wall = nc.dram_tensor("wall", [128, WTOT], F32, kind="ExternalInput").ap()
    par_d = nc.dram_tensor("par", [128, NPAR * 8], F32, kind="ExternalInput").ap()
    cst_d = nc.dram_tensor("cst", [128, 384], F32, kind="ExternalInput").ap()
    xseq = nc.dram_tensor("xseq", [128, NK, T], F32, kind="ExternalInput").ap()
    xsmp = nc.dram_tensor("xsmp", [128, NK, NS * NSMP], F32, kind="ExternalInput").ap()
    s0_d = nc.dram_tensor("s0", [NS, 128, 8, 128], F32, kind="ExternalInput").ap()
    h0_d = nc.dram_tensor("h0", [128, NS, 8], F32, kind="ExternalInput").ap()
    c0_d = nc.dram_tensor("c0", [128, NS, 8, 3], F32, kind="ExternalInput").ap()
    yseq = nc.dram_tensor("yseq", [128, NK, T], F32, kind="ExternalOutput").ap()
    ysmp = nc.dram_tensor("ysmp", [128, NK, NS * NSMP], F32, kind="ExternalOutput").ap()
    sout = nc.dram_tensor("sout", [1 + NS, 128, 8, 128], F32, kind="ExternalOutput").ap()
    hout = nc.dram_tensor("hout", [128, 1 + NS, 8], F32, kind="ExternalOutput").ap()
    cout = nc.dram_tensor("cout", [128, 1 + NS, 8, 3], F32, kind="ExternalOutput").ap()
    wb = nc.dram_tensor("wb", [128, WTOT], BF16, kind="Internal").ap()

    cur = [16512]

    def alloc(name, shape, dt, at=None):
        nbytes = int(np.prod(shape[1:])) * (4 if dt == F32 else 2)
        if at is None:
            off = cur[0]
            cur[0] += (nbytes + 31) // 32 * 32
        else:
            off = at
        return nc.alloc_sbuf_tensor_at(name, list(shape), dt, offset=off)

    X = alloc("X", [128, NK, 512], F32)
    XN = alloc("XN", [128, NK, 512], BF16)
    RING = alloc("RING", [128, NSLOT, USZ], BF16)
    A0 = alloc("A0", [128, 8, 520], F32)
    a1_off = cur[0]
    A1 = alloc("A1", [128, 8, 520], F32)
    ON = alloc("ON", [128, 8, 512], BF16, at=a1_off)
    A2 = alloc("A2", [128, 8, 520], F32)
    VB = alloc("VB", [128, 4096], BF16)
    SQ = alloc("SQ", [128, NK, 512], BF16)
    RSTD = alloc("RSTD", [128, 512], F32)
    SS = [alloc(f"S{i}", [128, 8, 128], F32) for i in range(2)]
    SBF = [alloc(f"SBF{i}", [128, 8, 128], BF16) for i in range(2)]
    HST = alloc("HST", [128, 8], F32)
    CVS = alloc("CVS", [128, 8, 3], F32)
    PAR = alloc("PAR", [128, NPAR, 8], F32)
    DER = alloc("DER", [128, 8, 8], F32)
    CST = alloc("CST", [128, 384], F32)
    IDB = alloc("IDB", [128, 128], BF16)
    ONESB = alloc("ONESB", [128, 128], BF16)
    PM = alloc("PM", [128, 4], F32)
    su0 = cur[0]
    Hh = alloc("H", [128, NF, 512], BF16)
    Tt = [alloc(f"T{i}", [128, 512], F32) for i in range(2)]
    su_ffn_end = cur[0]
    cur[0] = su0
    G2 = [alloc(f"G{i}", [128, 8, 129], F32) for i in range(2)]
    W2 = [alloc(f"W{i}", [128, 8, 128], F32) for i in range(2)]
    E12 = [alloc(f"E1{i}", [128, 8, 128], F32) for i in range(2)]
    QD2 = [alloc(f"QD{i}", [128, 8, 128], BF16) for i in range(2)]
    KD2 = [alloc(f"KD{i}", [128, 8, 128], BF16) for i in range(2)]
    KPT2 = [[alloc(f"KPT{i}_{m}", [128, 1024], BF16) for m in range(4)] for i in range(2)]
    OO = alloc("OO", [128, 8, 128], F32)
    RSO = alloc("RSO", [128, 8, 128], F32)
    AT = alloc("AT", [128, 8, 128], BF16)
    SQO = alloc("SQO", [128, 8, 128], BF16)
    su_a_end = cur[0]
    cur[0] = su0
    CBt = [alloc(f"CB{i}", [128, 512], BF16) for i in range(2)]
    RA = alloc("RA", [128, 8, 512], F32)
    IGA = alloc("IGA", [128, 8, 512], F32)
    AA = alloc("AA", [128, 8, 512], F32)
    Ut = [alloc(f"U{i}", [128, 512], F32) for i in range(2)]
    HNt = [alloc(f"HN{i}", [128, 512], F32) for i in range(2)]
    su_b_end = cur[0]
    cur[0] = max(su_ffn_end, su_a_end, su_b_end)
    assert cur[0] <= 229344, cur[0]

    PS = [nc.alloc_psum_tensor(f"ps{i}", [128, 512], F32) for i in range(8)]

    Xb = [Buf(f"X{k}") for k in range(NK)]
    XNb = [Buf(f"XN{k}") for k in range(NK)]
    RSb = [Buf(f"RS{i}") for i in range(NSLOT)]
    A0b = [Buf(f"A0_{h}") for h in range(8)]
    A1b = [Buf(f"A1_{h}") for h in range(8)]
    A2b = [Buf(f"A2_{h}") for h in range(8)]
    VBb = Buf("VB")
    ONb = Buf("ON")
    SQb = Buf("SQ")
    RSTDb = Buf("RSTD")
    SSb = [[Buf(f"S{i}_{h}") for h in range(8)] for i in range(2)]
    SBFb = [Buf("SBF0"), Buf("SBF1")]
    HSTb = Buf("HST")
    HSTnb = [Buf(f"HST{n}") for n in range(8)]
    Ynb = [Buf(f"Y{n}") for n in range(8)]
    CVSb = Buf("CVS")
    PARb = Buf("PAR")
    DERb = Buf("DER")
    CSTb = Buf("CST")
    Hb = [Buf(f"H{m}") for m in range(NF)]
    Tb = [Buf("T0"), Buf("T1")]
    FFNB = Hb + Tb
    G2b = [Buf("G0"), Buf("G1")]
    W2b = [Buf("W0"), Buf("W1")]
    E12b = [Buf("E10"), Buf("E11")]
    QD2b = [Buf("QD0"), Buf("QD1")]
    KD2b = [Buf("KD0"), Buf("KD1")]
    KPT2b = [[Buf(f"KPT{i}_{m}") for m in range(4)] for i in range(2)]
    OOb, RSOb, ATb, SQOb = Buf("OO"), Buf("RSO"), Buf("AT"), Buf("SQO")
    TILEB = G2b + W2b + E12b + QD2b + KD2b + KPT2b[0] + KPT2b[1] + [OOb, RSOb, ATb, SQOb]
    CBb = [Buf("CB0"), Buf("CB1")]
    Rb = [Buf(f"R{n}") for n in range(8)]
    IGb = [Buf(f"IG{n}") for n in range(8)]
    Ab = [Buf(f"AA{n}") for n in range(8)]
    Ub = [Buf("U0"), Buf("U1")]
    HNb = [Buf("HN0"), Buf("HN1")]
    BPHB = CBb + Rb + IGb + Ab + Ub + HNb
    PB = [Buf(f"PB{i}") for i in range(8)]
    WBb = [Buf(f"WB{u}") for u in range(NUNIT)]
    CGb = [Buf(f"CG{i}") for i in range(8)]
    YOUTb = Buf("YOUT")
    XLDb = Buf("XLD")
    STb = Buf("STIO")

    fw.dma("pool", PAR[:].rearrange("p a b -> p (a b)"), par_d, owner=PARb, writes=[PARb])
    fw.dma("pool", CST[:], cst_d, owner=CSTb, writes=[CSTb])

    IDF = CST[:, 0:128]
    MASK = CST[:, 128:256]
    ONESF = CST[:, 256:384]

    def par(i, h=None):
        if h is None:
            return PAR[:, i, :]
        return PAR[:, i, h:h + 1]

    def der(i, h=None):
        if h is None:
            return DER[:, i, :]
        return DER[:, i, h:h + 1]
    D_LB, D_OML, D_NOML, D_CL, D_CL2, D_T0, D_T1, D_T2 = range(8)

    st = {"next_dma": 0, "next_use": 0, "sbf": 0, "s": 0}
    total_uses = [0]

    def emit_cast():
        for u in range(NUNIT):
            o, sz = UOFF[u], UNITS[u][2]
            cg = CGb[u % 8]
            fw.dma("pool", wb[:, o:o + sz], wall[:, o:o + sz], owner=cg, writes=[cg, WBb[u]])

    def emit_wload(s):
        u = s % NUNIT
        o, sz = UOFF[u], UNITS[u][2]
        slot = s % NSLOT
        fw.dma("sp", RING[:, slot, 0:sz], wb[:, o:o + sz], owner=RSb[slot],
               reads=[WBb[u]], writes=[RSb[slot]])

    def use_unit(kind):
        s = st["next_use"]
        assert UNITS[s % NUNIT][0] == kind, (UNITS[s % NUNIT], kind)
        while st["next_dma"] <= min(s + NSLOT - 1, total_uses[0] - 1):
            emit_wload(st["next_dma"])
            st["next_dma"] += 1
        st["next_use"] += 1
        slot = s % NSLOT
        return RING[:, slot, :], RSb[slot]

    def mm(out, lhsT, rhs, start, stop, reads, writes):
        fw.op("pe", lambda: PE.matmul(out, lhsT=lhsT, rhs=rhs, start=start, stop=stop,
                                      skip_group_check=True), reads, writes)

    def act(out, in_, func, reads, writes, **kw):
        fw.op("act", lambda: A.activation(out=out, in_=in_, func=func, **kw), reads, writes)

    SQkb = [Buf(f"SQ{k}") for k in range(NK)]
    nst = {"presq": False}

    def norm_sq(kc, N):
        act(SQ[:, kc, :N], X[:, kc, :N], AF.Square, [Xb[kc]], [SQkb[kc]])

    def norm_mm(kc, N):
        mm(PS[6][:, :N], ONESB[:], SQ[:, kc, :N], kc == 0, kc == NK - 1, [SQkb[kc], CSTb], [PB[6]])

    def rmsnorm(gidx, N, out_f32=None, out_bufs=None):
        if not nst["presq"]:
            for kc in range(NK):
                norm_sq(kc, N)
                norm_mm(kc, N)
        nst["presq"] = False
        act(RSTD[:, :N], PS[6][:, :N], AF.Sqrt, [PB[6]], [RSTDb], scale=1.0 / D, bias=EPS)
        fw.op("dve", lambda: V.reciprocal(out=RSTD[:, :N], in_=RSTD[:, :N]), [RSTDb], [RSTDb])
        for kc in range(NK):
            if out_f32 is None:
                o, ob = XN[:, kc, :N], [XNb[kc]]
            else:
                o, ob = out_f32[:, kc, :N], [out_bufs[kc]]
            fw.op("dve", lambda o=o, kc=kc: V.scalar_tensor_tensor(
                out=o, in0=X[:, kc, :N], scalar=par(gidx, kc), in1=RSTD[:, :N],
                op0=ALU.mult, op1=ALU.mult), [Xb[kc], RSTDb, PARb], ob)

    def ffn(fi, N):
        rmsnorm(P_FFN0 + fi, N)
        for j in range(11):
            u, ub = use_unit("ffn_in")
            uv = u.rearrange("p (s k c) -> p s k c", s=2, k=NK)
            for mm_ in range(2):
                m = 2 * j + mm_
                pa, pab = PS[m % 2], PB[m % 2]
                pb, pbb = PS[2 + m % 2], PB[2 + m % 2]
                for kc in range(NK):
                    mm(pa[:, :N], uv[:, 0, kc, mm_ * 128:(mm_ + 1) * 128], XN[:, kc, :N],
                       kc == 0, kc == NK - 1, [ub, XNb[kc]], [pab])
                for kc in range(NK):
                    mm(pb[:, :N], uv[:, 1, kc, mm_ * 128:(mm_ + 1) * 128], XN[:, kc, :N],
                       kc == 0, kc == NK - 1, [ub, XNb[kc]], [pbb])
                t, tb = Tt[m % 2], Tb[m % 2]
                act(t[:, :N], pa[:, :N], AF.Silu, [pab], [tb])
                fw.op("dve", lambda t=t, pb=pb, m=m: V.tensor_tensor(
                    out=Hh[:, m, :N], in0=t[:, :N], in1=pb[:, :N], op=ALU.mult), [tb, pbb], [Hb[m]])
        for mo in range(NK):
            u, ub = use_unit("ffn_out")
            uv = u[:, 0:NF * 128].rearrange("p (k c) -> p k c", k=NF)
            py, pyb = PS[4 + mo % 2], PB[4 + mo % 2]
            for kf in range(NF):
                mm(py[:, :N], uv[:, kf, :], Hh[:, kf, :N], kf == 0, kf == NF - 1, [ub, Hb[kf]], [pyb])
            if mo >= 1:
                norm_mm(mo - 1, N)
            fw.op("dve", lambda py=py, mo=mo: V.scalar_tensor_tensor(
                out=X[:, mo, :N], in0=py[:, :N], scalar=0.5, in1=X[:, mo, :N],
                op0=ALU.mult, op1=ALU.add), [pyb, Xb[mo]], [Xb[mo]])
            norm_sq(mo, N)
        norm_mm(NK - 1, N)
        nst["presq"] = True

    def proj_fm(kind, nblk, N, evac):
        u = ub = uv = None
        for blk in range(nblk):
            if blk % 4 == 0:
                u, ub = use_unit(kind)
                uv = u.rearrange("p (k c) -> p k c", k=NK)
            p, pbuf = PS[blk % 2], PB[blk % 2]
            for kc in range(NK):
                mm(p[:, :N], uv[:, kc, (blk % 4) * 128:(blk % 4 + 1) * 128], XN[:, kc, :N],
                   kc == 0, kc == NK - 1, [ub, XNb[kc]], [pbuf])
            evac(blk, p, pbuf)

    def out_proj(kind, N, src, src_bufs):
        u = ub = uv = None
        for mo in range(NK):
            if mo % 4 == 0:
                u, ub = use_unit(kind)
                uv = u.rearrange("p (k c) -> p k c", k=NK)
            py, pyb = PS[4 + mo % 2], PB[4 + mo % 2]
            for kh in range(NK):
                mm(py[:, :N], uv[:, kh, (mo % 4) * 128:(mo % 4 + 1) * 128], src[:, kh, :N],
                   kh == 0, kh == NK - 1, [ub] + src_bufs, [pyb])
            if mo >= 1:
                norm_mm(mo - 1, N)
            fw.op("dve", lambda py=py, mo=mo: V.tensor_tensor(
                out=X[:, mo, :N], in0=py[:, :N], in1=X[:, mo, :N], op=ALU.add), [pyb, Xb[mo]], [Xb[mo]])
            norm_sq(mo, N)
        norm_mm(NK - 1, N)
        nst["presq"] = True

    def mixer_a(N):
        TN = min(N, 128)
        ntile = N // TN
        nb = TN // 16
        rmsnorm(P_MIX0 + 0, N)
        Q, K, LF = A0, A1, A2
        Vv = VB[:].rearrange("p (t c) -> p t c", t=4)

        def ev_q(h, p, pbuf):
            act(Q[:, h, :N], p[:, :N], AF.Silu, [pbuf], [A0b[h]])
        import os
        kstop = os.environ.get("KSTOP", "")
        proj_fm("a_in", 8, N, ev_q)
        if kstop == "q":
            skip(8)
            return

        def ev_f(h, p, pbuf):
            act(K[:, h, :N], p[:, :N], AF.Sigmoid, [pbuf], [A1b[h]])
        proj_fm("a_in", 8, N, ev_f)
        for h in range(8):
            act(LF[:, h, :N], K[:, h, :N], AF.Ln, [A1b[h], DERb], [A2b[h]],
                scale=der(D_OML, h), bias=der(D_LB, h))
            fw.op("dve", lambda h=h: V.tensor_scalar(
                out=K[:, h, :N], in0=K[:, h, :N], scalar1=der(D_NOML, h), scalar2=der(D_OML, h),
                op0=ALU.mult, op1=ALU.add), [A1b[h], DERb], [A1b[h]])
        if kstop == "f":
            skip(6)
            return

        idx = 0
        for c2 in range(2):
            u, ub = use_unit("a_in")
            uv = u.rearrange("p (k c) -> p k c", k=NK)
            for tt in range(ntile):
                p, pbuf = PS[2 + idx % 2], PB[2 + idx % 2]
                for kc in range(NK):
                    mm(p[:TN, :], XN[:, kc, tt * TN:(tt + 1) * TN], uv[:, kc, :],
                       kc == 0, kc == NK - 1, [ub, XNb[kc]], [pbuf])
                if idx % 2 == 0:
                    act(Vv[:TN, tt, c2 * 512:(c2 + 1) * 512], p[:TN, :], AF.Copy, [pbuf], [VBb])
                else:
                    fw.op("dve", lambda p=p, tt=tt, c2=c2: V.tensor_copy(
                        out=Vv[:TN, tt, c2 * 512:(c2 + 1) * 512], in_=p[:TN, :]), [pbuf], [VBb])
                idx += 1

        if kstop == "v":
            skip(4)
            return
        fw.handoff(FFNB, TILEB)
        import os
        sub = int(os.environ.get("KSUB", "99"))
        GP = nc.gpsimd
        for i in range(2 if ntile > 1 else 1):
            fw.op("pool", lambda i=i: GP.memset(G2[i][:], 0.0), [], [G2b[i]])
        opb = [PB[1], PB[2]]

        def prep_stages(tt):
            pr = tt % 2
            G, W, E1, QD, KD, KPT = G2[pr], W2[pr], E12[pr], QD2[pr], KD2[pr], KPT2[pr]
            Gb, Wb, E1b, QDb, KDb, KPTb = G2b[pr], W2b[pr], E12b[pr], QD2b[pr], KD2b[pr], KPT2b[pr]
            tsl = slice(tt * TN, (tt + 1) * TN)

            def v4(ap):
                return ap.rearrange("p h (b j) -> p h b j", j=16)

            def s0():
                for h in range(8):
                    fw.op("dve", lambda h=h: V.tensor_tensor_scan(
                        out=G[:, h, 1:TN + 1], data0=ONESF[:, :TN], data1=LF[:, h, tsl], initial=0.0,
                        op0=ALU.mult, op1=ALU.add), [A2b[h], CSTb], [Gb])

            def s1():
                g1 = v4(G[:, :, 1:TN + 1])
                g0 = v4(G[:, :, 0:TN])[:, :, :, 0:1].to_broadcast([128, 8, nb, 16])
                fw.op("pool", lambda: GP.tensor_tensor(out=v4(W[:, :, :TN]), in0=g1, in1=g0, op=ALU.subtract),
                      [Gb], [Wb])

            def s2():
                act(E1[:, :, :TN], W[:, :, :TN], AF.Exp, [Wb], [E1b])
                act(W[:, :, :TN], W[:, :, :TN], AF.Exp, [Wb], [Wb], scale=-1.0)

            def s3():
                fw.op("pool", lambda: GP.tensor_tensor(out=QD[:, :, :TN], in0=Q[:, :, tsl], in1=E1[:, :, :TN],
                                                       op=ALU.mult), A0b + [E1b], [QDb])
                fw.op("pool", lambda: GP.tensor_tensor(out=W[:, :, :TN], in0=K[:, :, tsl], in1=W[:, :, :TN],
                                                       op=ALU.mult), A1b + [Wb], [Wb])

            def s4():
                act(KD[:, :, :TN], W[:, :, :TN], AF.Copy, [Wb], [KDb])
                e1l = v4(E1[:, :, :TN])[:, :, :, 15:16].to_broadcast([128, 8, nb, 16])
                fw.op("pool", lambda: GP.tensor_tensor(out=v4(W[:, :, :TN]), in0=v4(W[:, :, :TN]), in1=e1l,
                                                       op=ALU.mult), [Wb, E1b], [Wb])

            def s5(hf):
                def f():
                    bank = 7 if hf == 0 else 0
                    for h in range(4 * hf, 4 * hf + 4):
                        fw.op("pe", lambda h=h: PE.transpose(PS[bank][:TN, (h % 4) * 128:(h % 4 + 1) * 128],
                                                             W[:, h, :TN], IDF), [Wb, CSTb], [PB[bank]])
                    nmask = 4 if TN == 128 else 1
                    for m4 in range(nmask):
                        fw.op("dve", lambda m4=m4: V.tensor_scalar(
                            out=KPT[m4][:TN, hf * 512:(hf + 1) * 512], in0=PS[bank][:TN, :],
                            scalar1=PM[:TN, m4:m4 + 1], scalar2=None, op0=ALU.mult),
                            [PB[bank], CSTb], [KPTb[m4]])
                return f
            return [s0, s1, s2, s3, s4, s5(0), s5(1)]

        SLOTS = {0: [0, 1], 2: [2], 3: [3], 5: [4], 6: [5], 7: [6]}

        def front_scores(tt):
            pr = tt % 2
            QD, KD, QDb, KDb = QD2[pr], KD2[pr], QD2b[pr], KD2b[pr]
            for hg in range(2):
                sb = 0 if hg == 0 else 7
                for h in range(hg * 4, hg * 4 + 4):
                    mm(PS[sb][:TN, (h % 4) * TN:(h % 4 + 1) * TN], KD[:, h, :TN], QD[:, h, :TN],
                       h % 4 == 0, h % 4 == 3, [KDb, QDb], [PB[sb]])
                scv = PS[sb][:TN, 0:4 * TN].rearrange("p (h t) -> p h t", h=4)
                mk = MASK[:TN, :TN].unsqueeze(1).to_broadcast([TN, 4, TN])
                fw.op("dve", lambda scv=scv, mk=mk, hg=hg: V.tensor_tensor(
                    out=AT[:TN, hg * 4:hg * 4 + 4, :TN], in0=scv, in1=mk, op=ALU.mult), [PB[sb], CSTb], [ATb])

        def emit_u(tt, n):
            KPT, KPTb = KPT2[tt % 2], KPT2b[tt % 2]
            if TN == 128:
                r0, kr, kp, kpb = 64 * (n // 4), 64, KPT[n % 4], KPTb[n % 4]
            else:
                r0, kr, kp, kpb = 0, 16, KPT[0], KPTb[0]
            bk = 3 + 2 * (n % 2)
            for h in range(8):
                mm(PS[bk + h // 4][:, (h % 4) * 128:(h % 4 + 1) * 128],
                   kp[r0:r0 + kr, h * 128:(h + 1) * 128], Vv[r0:r0 + kr, tt, h * 128:(h + 1) * 128],
                   h % 4 == 0, h % 4 == 3, [kpb, VBb], [PB[bk + h // 4]])

        def tile_body(tt, nxt):
            pr = tt % 2
            E1, QD, KD, KPT = E12[pr], QD2[pr], KD2[pr], KPT2[pr]
            E1b, QDb, KDb, KPTb = E12b[pr], QD2b[pr], KD2b[pr], KPT2b[pr]
            tsl = slice(tt * TN, (tt + 1) * TN)
            for hg in range(2):
                for h in range(hg * 4, hg * 4 + 4):
                    mm(PS[1 + hg][:, (h % 4) * TN:(h % 4 + 1) * TN], Vv[:TN, tt, h * 128:(h + 1) * 128],
                       AT[:TN, h, :TN], h % 4 == 0, False, [VBb, ATb], [opb[hg]])

            if tt == 0:
                emit_u(tt, 0)
            for n in range(nb):
                if n + 1 < nb:
                    emit_u(tt, n + 1)
                cs = st["sbf"]
                for h in range(8):
                    mm(PS[1 + h // 4][:, (h % 4) * TN + 16 * n:(h % 4) * TN + 16 * n + 16],
                       SBF[cs][:, h, :], QD[:, h, 16 * n:16 * n + 16], False, False,
                       [SBFb[cs], QDb], [opb[h // 4]])
                bk = 3 + 2 * (n % 2)
                si = st["s"]
                so = 1 - si
                for h in range(8):
                    fw.op("dve", lambda h=h, bk=bk, n=n, si=si, so=so: V.scalar_tensor_tensor(
                        out=SS[so][:, h, :], in0=SS[si][:, h, :], scalar=E1[:, h, 16 * n + 15:16 * n + 16],
                        in1=PS[bk + h // 4][:, (h % 4) * 128:(h % 4 + 1) * 128],
                        op0=ALU.mult, op1=ALU.add), [SSb[si][h], E1b, PB[bk + h // 4]], [SSb[so][h]])
                st["s"] = so
                ns = 1 - cs
                act(SBF[ns][:], SS[so][:], AF.Copy, SSb[so], [SBFb[ns]])
                st["sbf"] = ns
                if nxt:
                    for si_ in SLOTS.get(n, []):
                        nxt[si_]()
            for hg in range(2):
                pv = PS[1 + hg][:, 0:4 * TN].rearrange("p (h t) -> p h t", h=4)
                act(OO[:, hg * 4:hg * 4 + 4, :TN], pv, AF.Copy, [opb[hg]], [OOb])
            if tt + 1 < ntile:
                front_scores(tt + 1)
                emit_u(tt + 1, 0)
            fw.op("pool", lambda: GP.tensor_tensor(out=SQO[:, :, :TN], in0=OO[:, :, :TN], in1=OO[:, :, :TN],
                                                   op=ALU.mult), [OOb], [SQOb])
            for h in range(8):
                mm(PS[5 + h // 4][:, (h % 4) * TN:(h % 4 + 1) * TN], ONESB[:], SQO[:, h, :TN],
                   h % 4 == 0, h % 4 == 3, [SQOb, CSTb], [PB[5 + h // 4]])
            for hg in range(2):
                pv = PS[5 + hg][:, 0:4 * TN].rearrange("p (h t) -> p h t", h=4)
                act(RSO[:, hg * 4:hg * 4 + 4, :TN], pv, AF.Ln, [PB[5 + hg]], [RSOb], scale=1.0 / 128, bias=EPS)
            act(RSO[:, :, :TN], RSO[:, :, :TN], AF.Exp, [RSOb], [RSOb], scale=-0.5)
            fw.op("pool", lambda: GP.tensor_tensor(out=Q[:, :, tsl], in0=OO[:, :, :TN], in1=RSO[:, :, :TN],
                                                   op=ALU.mult), [OOb, RSOb], A0b)

        for f in prep_stages(0):
            f()
        front_scores(0)
        for tt in range(ntile):
            nxt = prep_stages(tt + 1) if tt + 1 < ntile else []
            tile_body(tt, nxt)

        fw.handoff(TILEB, FFNB)
        if sub < 7:
            skip(4)
            return

        def ev_g(h, p, pbuf):
            act(LF[:, h, :N], p[:, :N], AF.Silu, [pbuf], [A2b[h]])
            fw.op("dve", lambda: V.scalar_tensor_tensor(
                out=ON[:, h, :N], in0=Q[:, h, :N], scalar=par(P_ONORM, h), in1=LF[:, h, :N],
                op0=ALU.mult, op1=ALU.mult), [A0b[h], A2b[h], PARb], A1b)
        proj_fm("a_in", 8, N, ev_g)
        out_proj("a_out", N, ON, A1b)

    def mixer_b(N):
        rmsnorm(P_MIX0 + 1, N)
        XBr, GG, CV = A0, A1, A2
        Y = VB[:].rearrange("p (k t) -> p k t", k=8)

        fw.op("dve", lambda: V.tensor_copy(out=XBr[:, :, 0:3], in_=CVS[:]), [CVSb], A0b)

        def ev_x(n, p, pbuf):
            act(XBr[:, n, 3:3 + N], p[:, :N], AF.Copy, [pbuf], [A0b[n]])
        proj_fm("b_in", 8, N, ev_x)

        def ev_g(n, p, pbuf):
            act(GG[:, n, :N], p[:, :N], AF.Gelu, [pbuf], [A1b[n]])
        proj_fm("b_in", 8, N, ev_g)

        fw.handoff(FFNB, BPHB)
        u, ub = use_unit("b_gate")
        uv = u[:, 0:2048].rearrange("p (s n d) -> p s n d", s=2, n=8)
        fw.handoff([VBb, HSTb], Ynb + HSTnb)

        def stage_a(n):
            i2 = n % 2
            fw.op("pool", lambda n=n: nc.gpsimd.tensor_scalar(
                out=CV[:, n, :N], in0=XBr[:, n, 0:N], scalar1=par(P_CW0 + 0, n), scalar2=par(P_CB, n),
                op0=ALU.mult, op1=ALU.add), [A0b[n], PARb], [A2b[n]])
            for j in range(1, 4):
                fw.op("dve", lambda n=n, j=j: V.scalar_tensor_tensor(
                    out=CV[:, n, :N], in0=XBr[:, n, j:j + N], scalar=par(P_CW0 + j, n), in1=CV[:, n, :N],
                    op0=ALU.mult, op1=ALU.add), [A0b[n], A2b[n], PARb], [A2b[n]])
            act(CBt[i2][:, :N], CV[:, n, :N], AF.Copy, [A2b[n]], [CBb[i2]])
            pr, prb = PS[2 + 2 * i2], PB[2 + 2 * i2]
            pi, pib = PS[3 + 2 * i2], PB[3 + 2 * i2]
            mm(pr[:, :N], uv[:, 0, n, :], CBt[i2][:, :N], True, True, [ub, CBb[i2]], [prb])
            mm(pi[:, :N], uv[:, 1, n, :], CBt[i2][:, :N], True, True, [ub, CBb[i2]], [pib])
            act(RA[:, n, :N], pr[:, :N], AF.Sigmoid, [prb, PARb], [Rb[n]], bias=par(P_BA, n))
            act(IGA[:, n, :N], pi[:, :N], AF.Sigmoid, [pib, PARb], [IGb[n]], bias=par(P_BX, n))

        def stage_b(n):
            i2 = n % 2
            fw.op("dve", lambda i2=i2, n=n: V.tensor_tensor(
                out=Ut[i2][:, :N], in0=IGA[:, n, :N], in1=CV[:, n, :N], op=ALU.mult), [IGb[n], A2b[n]], [Ub[i2]])
            fw.op("dve", lambda i2=i2, n=n: V.tensor_tensor(
                out=Ut[i2][:, :N], in0=Ut[i2][:, :N], in1=RA[:, n, :N], op=ALU.mult), [Ub[i2], Rb[n]], [Ub[i2]])
            fw.op("dve", lambda i2=i2, n=n: V.tensor_tensor_scan(
                out=HNt[i2][:, :N], data0=AA[:, n, :N], data1=Ut[i2][:, :N], initial=HST[:, n:n + 1],
                op0=ALU.mult, op1=ALU.add), [Ab[n], Ub[i2], HSTnb[n]], [HNb[i2]])
            fw.op("dve", lambda i2=i2, n=n: V.tensor_copy(out=HST[:, n:n + 1], in_=HNt[i2][:, N - 1:N]),
                  [HNb[i2]], [HSTnb[n]])
            fw.op("dve", lambda i2=i2, n=n: V.tensor_tensor(
                out=Y[:, n, :N], in0=HNt[i2][:, :N], in1=GG[:, n, :N], op=ALU.mult), [HNb[i2], A1b[n]], [Ynb[n]])

        for n in range(8):
            stage_a(n)
        for half in range(2):
            ns_ = range(4 * half, 4 * half + 4)
            for n in ns_:
                act(AA[:, n, :N], RA[:, n, :N], AF.Exp, [Rb[n], DERb], [Ab[n]], scale=der(D_CL, n))
                act(RA[:, n, :N], RA[:, n, :N], AF.Exp, [Rb[n], DERb], [Rb[n]], scale=der(D_CL2, n))
            for n in ns_:
                act(RA[:, n, :N], RA[:, n, :N], AF.Sqrt, [Rb[n]], [Rb[n]], scale=-1.0, bias=1.0)
        for n in range(8):
            stage_b(n)
        fw.op("dve", lambda: V.tensor_copy(out=CVS[:], in_=XBr[:, :, N:N + 3]), A0b, [CVSb])
        fw.handoff(BPHB, FFNB)
        out_proj("b_out", N, Y, Ynb)
        fw.handoff(Ynb + HSTnb, [VBb, HSTb])

    LEVELS = ["io", "ffn0", "mixa", "ffn1", "ffn2", "mixb", "ffn3", "all"]
    lvl = LEVELS.index(dbg) if dbg else len(LEVELS) - 1

    def skip(k):
        st["next_use"] += k

    def prefetch(src, N):
        fw.dma("pool", A0[:, :, :N], src, owner=XLDb, writes=A0b)

    def chunk(src, dst, N, first, nxt):
        if first:
            prefetch(src, N)
            emit_cast()
        for kc in range(NK):
            act(X[:, kc, :N], A0[:, kc, :N], AF.Copy, [A0b[kc]], [Xb[kc]])
        nst["presq"] = False
        if lvl >= 1:
            ffn(0, N)
        else:
            skip(19)
        if lvl >= 2:
            mixer_a(N)
        else:
            skip(10)
        if lvl >= 3:
            ffn(1, N)
        else:
            skip(19)
        if lvl >= 4:
            ffn(2, N)
        else:
            skip(19)
        if lvl >= 5:
            mixer_b(N)
        else:
            skip(7)
        if nxt is not None:
            prefetch(nxt[0], nxt[1])
        if lvl >= 6:
            ffn(3, N)
        else:
            skip(19)
        if lvl >= 7:
            rmsnorm(P_FIN, N, out_f32=A1, out_bufs=A1b)
        else:
            fw.op("dve", lambda: V.tensor_copy(out=A1[:, :, :N], in_=X[:, :, :N]), Xb, A1b)
        fw.dma("pool", dst, A1[:, :, :N], owner=YOUTb, reads=A1b, writes=[YOUTb])

    def write_states(idx):
        fw.dma("pool", sout[idx], SS[st["s"]][:], owner=STb, reads=SSb[st["s"]], writes=[STb])
        fw.dma("pool", hout[:, idx, :], HST[:], owner=STb, reads=[HSTb], writes=[STb])
        fw.dma("pool", cout[:, idx, :, :], CVS[:], owner=STb, reads=[CVSb], writes=[STb])

    chunks = []
    pos = 0
    while pos < T:
        n = min(512, T - pos)
        chunks.append((pos, n))
        pos += n
    total_uses[0] = NUNIT * (len(chunks) + NS)

    pm_d = nc.dram_tensor("pm", [128, 4], F32, kind="ExternalInput").ap()
    fw.dma("pool", PM[:], pm_d, owner=CSTb, writes=[CSTb])
    r = [PARb]
    w = [DERb]
    act(der(D_T0), par(P_LB0), AF.Exp, r, w)
    act(der(D_T1), par(P_LB1), AF.Exp, r, w)
    fw.op("dve", lambda: V.tensor_tensor(out=der(D_T2), in0=der(D_T0), in1=der(D_T1), op=ALU.add), [DERb], w)
    fw.op("dve", lambda: V.reciprocal(out=der(D_T2), in_=der(D_T2)), [DERb], w)
    fw.op("dve", lambda: V.tensor_tensor(out=der(D_LB), in0=der(D_T0), in1=der(D_T2), op=ALU.mult), [DERb], w)
    fw.op("dve", lambda: V.tensor_tensor(out=der(D_OML), in0=der(D_T1), in1=der(D_T2), op=ALU.mult), [DERb], w)
    fw.op("dve", lambda: V.tensor_scalar(out=der(D_NOML), in0=der(D_OML), scalar1=-1.0, scalar2=None, op0=ALU.mult), [DERb], w)
    act(der(D_T0), par(P_LAM), AF.Exp, r + [DERb], w, scale=-1.0)
    act(der(D_T0), der(D_T0), AF.Ln, [DERb], w, bias=1.0)
    fw.op("dve", lambda: V.tensor_scalar(out=der(D_CL), in0=der(D_T0), scalar1=-8.0, scalar2=None, op0=ALU.mult), [DERb], w)
    fw.op("dve", lambda: V.tensor_scalar(out=der(D_CL2), in0=der(D_T0), scalar1=-16.0, scalar2=None, op0=ALU.mult), [DERb], w)
    fw.op("dve", lambda: V.tensor_copy(out=IDB[:], in_=IDF), [CSTb], [CSTb])
    fw.op("dve", lambda: V.tensor_copy(out=ONESB[:], in_=ONESF), [CSTb], [CSTb])

    fw.op("dve", lambda: V.memset(SS[0][:], 0.0), [], SSb[0])
    fw.op("dve", lambda: V.memset(SBF[0][:], 0.0), [], [SBFb[0]])
    fw.op("dve", lambda: V.memset(HST[:], 0.0), [], [HSTb])
    fw.op("dve", lambda: V.memset(CVS[:], 0.0), [], [CVSb])
    st["sbf"] = 0
    jobs = [(xseq[:, :, pos:pos + n], yseq[:, :, pos:pos + n], n) for (pos, n) in chunks]
    for s in range(NS):
        jobs.append((xsmp[:, :, s * NSMP:(s + 1) * NSMP], ysmp[:, :, s * NSMP:(s + 1) * NSMP], NSMP))
    npr = len(chunks)
    for ji, (src, dst, n) in enumerate(jobs):
        nxt = (jobs[ji + 1][0], jobs[ji + 1][2]) if ji + 1 < len(jobs) else None
        if ji >= npr:
            sidx = ji - npr
            fw.dma("pool", SS[st["s"]][:], s0_d[sidx], owner=XLDb, reads=[], writes=SSb[st["s"]])
            fw.dma("pool", HST[:], h0_d[:, sidx, :], owner=XLDb, writes=[HSTb])
            fw.dma("pool", CVS[:], c0_d[:, sidx, :, :], owner=XLDb, writes=[CVSb])
            cs = st["sbf"]
            act(SBF[cs][:], SS[st["s"]][:], AF.Copy, SSb[st["s"]], [SBFb[cs]])
        chunk(src, dst, n, ji == 0, nxt)
        if ji == npr - 1:
            write_states(0)
        elif ji >= npr:
            write_states(1 + ji - npr)
    fw.final_wait("pool", [YOUTb, STb])
    assert st["next_use"] == total_uses[0], (st["next_use"], total_uses[0])
    return nc, fw


def fm(v):
    return np.ascontiguousarray(np.asarray(v, np.float32).reshape(8, 128).T)


def build_wall(ffn_w_in, ffn_w_out, a_w_in, a_w_out, b_w_in, b_wa, b_wx, b_w_out):
    wall = np.empty((128, WTOT), np.float32)

    def std(wm, u):
        blk = wm[:, u * 512:(u + 1) * 512].reshape(NK, 128, 512)
        return blk.transpose(1, 0, 2).reshape(128, -1)
    for ui, (kind, args, sz) in enumerate(UNITS):
        o = UOFF[ui]
        if kind == "ffn_in":
            fi, j = args
            wm = ffn_w_in[fi // 2, fi % 2].reshape(NK, 128, 2, DFF)[:, :, :, j * 256:(j + 1) * 256]
            blk = wm.transpose(1, 2, 0, 3).reshape(128, -1)
        elif kind == "ffn_out":
            fi, mo = args
            wm = ffn_w_out[fi // 2, fi % 2].reshape(NF, 128, 8, 128)[:, :, mo, :]
            blk = wm.transpose(1, 0, 2).reshape(128, -1)
        elif kind == "a_in":
            blk = std(a_w_in[0], args[0])
        elif kind == "a_out":
            blk = std(a_w_out[0], args[0])
        elif kind == "b_in":
            blk = std(b_w_in[0], args[0])
        elif kind == "b_out":
            blk = std(b_w_out[0], args[0])
        elif kind == "b_gate":
            blk = np.stack([b_wa[0], b_wx[0]], 0).transpose(2, 0, 1, 3).reshape(128, -1)
        assert blk.shape[1] == sz, (kind, blk.shape, sz)
        wall[:, o:o + sz] = blk
    return wall


def build_consts():
    cst = np.zeros((128, 384), np.float32)
    cst[:, 0:128] = np.eye(128, dtype=np.float32)
    s = np.arange(128)[:, None]
    t = np.arange(128)[None, :]
    cst[:, 128:256] = ((s // 16 == t // 16) & (s <= t)).astype(np.float32)
    cst[:, 256:384] = 1.0
    pm = np.zeros((128, 4), np.float32)
    blk = np.arange(128) // 16
    for m4 in range(4):
        pm[:, m4] = (blk % 4 == m4)
    return cst, pm


_CACHE = {}


def kernel(x_prompt, x_sample, state_hgrn, state_rglru, state_conv, meta_tokens,
           ffn_norm, ffn_w_in, ffn_w_out, mix_norm, a_w_in, a_lb, a_onorm, a_w_out,
           b_w_in, b_conv_w, b_conv_b, b_wa, b_ba, b_wx, b_bx, b_lambda, b_w_out, final_norm,
           _ncores=8):
    f32 = np.float32
    x_prompt = np.asarray(x_prompt, f32)
    x_sample = np.asarray(x_sample, f32)
    B, SEQ, _ = x_prompt.shape
    DB, DS, _ = x_sample.shape
    T = SEQ + NMETA
    NS = DB // _ncores
    import os
    dbg = os.environ.get("KDBG") or None
    key = (T, NS, DS, dbg)
    if key not in _CACHE:
        _CACHE[key] = build_program(T, NS, DS, dbg)
    nc, fw = _CACHE[key]

    wall = build_wall(np.asarray(ffn_w_in, f32), np.asarray(ffn_w_out, f32), np.asarray(a_w_in, f32),
                      np.asarray(a_w_out, f32), np.asarray(b_w_in, f32), np.asarray(b_wa, f32),
                      np.asarray(b_wx, f32), np.asarray(b_w_out, f32))
    pv = [None] * NPAR
    fn = np.asarray(ffn_norm, f32)
    for l in range(2):
        for i in range(2):
            pv[P_FFN0 + l * 2 + i] = fm(fn[l, i])
    mn = np.asarray(mix_norm, f32)
    pv[P_MIX0], pv[P_MIX0 + 1] = fm(mn[0]), fm(mn[1])
    pv[P_FIN] = fm(final_norm)
    alb = np.asarray(a_lb, f32)
    pv[P_LB0], pv[P_LB1] = fm(alb[0]), fm(alb[1])
    pv[P_ONORM] = fm(np.asarray(a_onorm, f32)[0])
    cw = np.asarray(b_conv_w, f32)[0]
    for j in range(4):
        pv[P_CW0 + j] = fm(cw[j])
    pv[P_CB] = fm(np.asarray(b_conv_b, f32)[0])
    pv[P_BA] = fm(np.asarray(b_ba, f32)[0])
    pv[P_BX] = fm(np.asarray(b_bx, f32)[0])
    pv[P_LAM] = fm(np.asarray(b_lambda, f32)[0])
    par = np.ascontiguousarray(np.stack(pv, 1).reshape(128, NPAR * 8))
    cst, pm = build_consts()

    meta = np.asarray(meta_tokens, f32)
    sh = np.asarray(state_hgrn, f32)[0]
    sr = np.asarray(state_rglru, f32)[0]
    sc = np.asarray(state_conv, f32)[0]

    owner = {0: 0, 1: 1, 4: 2, 5: 3} if _ncores == 8 else {i: i for i in range(min(B, _ncores))}
    in_maps = []
    for c in range(_ncores):
        if c in owner:
            xs = np.concatenate([meta, x_prompt[owner[c]]], 0)
            xseq = np.ascontiguousarray(xs.reshape(T, NK, 128).transpose(2, 1, 0))
        else:
            xseq = np.zeros((128, NK, T), f32)
        ss = slice(c * NS, (c + 1) * NS)
        xm = x_sample[ss].reshape(NS * DS, NK, 128).transpose(2, 1, 0)
        in_maps.append({
            "wall": wall, "par": par, "cst": cst, "pm": pm,
            "xseq": xseq, "xsmp": np.ascontiguousarray(xm),
            "s0": np.ascontiguousarray(sh[ss].transpose(0, 2, 1, 3)),
            "h0": np.ascontiguousarray(sr[ss].reshape(NS, 8, 128).transpose(2, 0, 1)),
            "c0": np.ascontiguousarray(sc[ss].reshape(NS, 3, 8, 128).transpose(3, 0, 2, 1)),
        })
    res = run_bass_kernel_spmd(nc, in_maps, core_ids=list(range(_ncores)))
    R = res.results

    y_prompt = np.empty((B, SEQ, D), f32)
    y_sample = np.empty((DB, DS, D), f32)
    hg_p = np.empty((1, B, 8, 128, 128), f32)
    hg_s = np.empty((1, DB, 8, 128, 128), f32)
    rg_p = np.empty((1, B, D), f32)
    rg_s = np.empty((1, DB, D), f32)
    cv_p = np.empty((1, B, 3, D), f32)
    cv_s = np.empty((1, DB, 3, D), f32)
    for c in range(_ncores):
        r = R[c]
        so, ho, co = r["sout"], r["hout"], r["cout"]
        if c in owner:
            b = owner[c]
            y_prompt[b] = r["yseq"].transpose(2, 1, 0).reshape(T, D)[NMETA:]
            hg_p[0, b] = so[0].transpose(1, 0, 2)
            rg_p[0, b] = ho[:, 0, :].T.reshape(D)
            cv_p[0, b] = co[:, 0].transpose(2, 1, 0).reshape(3, D)
        ym = r["ysmp"].transpose(2, 1, 0).reshape(NS, DS, D)
        for s in range(NS):
            g = c * NS + s
            y_sample[g] = ym[s]
            hg_s[0, g] = so[1 + s].transpose(1, 0, 2)
            rg_s[0, g] = ho[:, 1 + s, :].T.reshape(D)
            cv_s[0, g] = co[:, 1 + s].transpose(2, 1, 0).reshape(3, D)
    return (y_prompt, y_sample, hg_p, hg_s, rg_p, rg_s, cv_p, cv_s)
```

```python
import numpy as np
import concourse.bass as bass
import concourse.mybir as mybir
from concourse.bass_utils import run_bass_kernel_spmd

F32 = mybir.dt.float32
BF16 = mybir.dt.bfloat16
AF = mybir.ActivationFunctionType
ALU = mybir.AluOpType

D = 1024
NK = 8
DFF = 2816
NF = 22
NMETA = 16
EPS = 1e-6
NSLOT = 5
USZ = 4096


class Buf:
    __slots__ = ("name", "w", "r", "dsem", "dcnt")

    def __init__(self, name):
        self.name = name
        self.w = None
        self.r = {}
        self.dsem = None
        self.dcnt = 0


class FW:
    def __init__(self, nc):
        self.nc = nc
        self.eng = {"pe": nc.tensor, "act": nc.scalar, "dve": nc.vector,
                    "pool": nc.gpsimd, "sp": nc.sync}
        self.sem = {}
        self.cnt = {}
        self.waited = {}
        self.semobj = {}
        for e in self.eng:
            self.sem[e] = nc.alloc_semaphore(name="prog_" + e)
            self.semobj[e] = self.sem[e]
            self.cnt[e] = 0
            self.waited[e] = {}
        self.ninst = 0

    def _need(self, reads, writes):
        need = {}

        def add(clk):
            if clk is None:
                return
            k, v = clk
            if v > need.get(k, 0):
                need[k] = v
        for b in reads:
            add(b.w)
        for b in writes:
            add(b.w)
            for k, v in b.r.items():
                add((k, v))
        return need

    def _emit_waits(self, e, need, skip_self=False):
        eng = self.eng[e]
        wd = self.waited[e]
        for k, v in need.items():
            if skip_self and k == e:
                continue
            if wd.get(k, 0) >= v:
                continue
            eng.wait_ge(self.semobj[k], v)
            wd[k] = v

    def op(self, e, fn, reads=(), writes=()):
        need = self._need(reads, writes)
        self._emit_waits(e, need, skip_self=(e == "pe"))
        ins = fn()
        self.cnt[e] += 1
        c = self.cnt[e]
        ins.then_inc(self.sem[e], 1)
        for b in writes:
            b.w = (e, c)
            b.r = {}
        for b in reads:
            if b not in writes:
                b.r[e] = c
        self.ninst += 1
        return ins

    def dma(self, q, out_ap, in_ap, owner, reads=(), writes=(), **kw):
        need = self._need(reads, writes)
        self._emit_waits(q, need)
        k = "d_" + owner.name
        if owner.dsem is None:
            owner.dsem = self.nc.alloc_semaphore(name=k)
            self.semobj[k] = owner.dsem
        owner.dcnt += 1
        v = 16 * owner.dcnt
        ins = self.eng[q].dma_start(out=out_ap, in_=in_ap, **kw)
        ins.then_inc(owner.dsem, 16)
        for b in writes:
            b.w = (k, v)
            b.r = {}
        for b in reads:
            if b not in writes:
                b.r[k] = v
        self.ninst += 1
        return ins

    def handoff(self, olds, news):
        merged = {}
        for b in olds:
            if b.w is not None and b.w[1] > merged.get(b.w[0], 0):
                merged[b.w[0]] = b.w[1]
            for k, v in b.r.items():
                if v > merged.get(k, 0):
                    merged[k] = v
        for b in news:
            b.w = None
            b.r = dict(merged)

    def final_wait(self, e, bufs):
        self._emit_waits(e, self._need([], bufs))


def unit_table():
    units = []

    def ffn(fi):
        for j in range(11):
            units.append(("ffn_in", (fi, j), 4096))
        for mo in range(8):
            units.append(("ffn_out", (fi, mo), NF * 128))
    ffn(0)
    for u in range(8):
        units.append(("a_in", (u,), 4096))
    for u in range(2):
        units.append(("a_out", (u,), 4096))
    ffn(1)
    ffn(2)
    for u in range(4):
        units.append(("b_in", (u,), 4096))
    units.append(("b_gate", (), 2048))
    for u in range(2):
        units.append(("b_out", (u,), 4096))
    ffn(3)
    offs = []
    o = 0
    for _, _, sz in units:
        offs.append(o)
        o += sz
    return units, offs, o


UNITS, UOFF, WTOT = unit_table()
NUNIT = len(UNITS)

NPAR = 18
(P_FFN0, P_MIX0, P_FIN, P_LB0, P_LB1, P_ONORM, P_CW0, P_CB, P_BA, P_BX, P_LAM) = (0, 4, 6, 7, 8, 9, 10, 14, 15, 16, 17)


def build_program(T, NS=2, NSMP=16, dbg=None):
    nc = bass.Bass("TRN2", target_bir_lowering=False)
    fw = FW(nc)
    V = nc.vector
    A = nc.scalar
    PE = nc.tensor

    wall = nc.dram_tensor("wall", [128, WTOT], F32, kind="ExternalInput").ap()
    par_d = nc.dram_tensor("par", [128, NPAR * 8], F32, kind="ExternalInput").ap()
    cst_d = nc.dram_tensor("cst", [128, 384], F32, kind="ExternalInput").ap()
    xseq = nc.dram_tensor("xseq", [128, NK, T], F32, kind="ExternalInput").ap()
    xsmp = nc.dram_tensor("xsmp", [128, NK, NS * NSMP], F32, kind="ExternalInput").ap()
    s0_d = nc.dram_tensor("s0", [NS, 128, 8, 128], F32, kind="ExternalInput").ap()
    h0_d = nc.dram_tensor("h0", [128, NS, 8], F32, kind="ExternalInput").ap()
    c0_d = nc.dram_tensor("c0", [128, NS, 8, 3], F32, kind="ExternalInput").ap()
    yseq = nc.dram_tensor("yseq", [128, NK, T], F32, kind="ExternalOutput").ap()
    ysmp = nc.dram_tensor("ysmp", [128, NK, NS * NSMP], F32, kind="ExternalOutput").ap()
    sout = nc.dram_tensor("sout", [1 + NS, 128, 8, 128], F32, kind="ExternalOutput").ap()
    hout = nc.dram_tensor("hout", [128, 1 + NS, 8], F32, kind="ExternalOutput").ap()
    cout = nc.dram_tensor("cout", [128, 1 + NS, 8, 3], F32, kind="ExternalOutput").ap()
    wb = nc.dram_tensor("wb", [128, WTOT], BF16, kind="Internal").ap()

    cur = [16512]

    def alloc(name, shape, dt, at=None):
        nbytes = int(np.prod(shape[1:])) * (4 if dt == F32 else 2)
        if at is None:
            off = cur[0]
            cur[0] += (nbytes + 31) // 32 * 32
        else:
            off = at
        return nc.alloc_sbuf_tensor_at(name, list(shape), dt, offset=off)

    X = alloc("X", [128, NK, 512], F32)
    XN = alloc("XN", [128, NK, 512], BF16)
    RING = alloc("RING", [128, NSLOT, USZ], BF16)
    A0 = alloc("A0", [128, 8, 520], F32)
    a1_off = cur[0]
    A1 = alloc("A1", [128, 8, 520], F32)
    ON = alloc("ON", [128, 8, 512], BF16, at=a1_off)
    A2 = alloc("A2", [128, 8, 520], F32)
    VB = alloc("VB", [128, 4096], BF16)
    SQ = alloc("SQ", [128, NK, 512], BF16)
    RSTD = alloc("RSTD", [128, 512], F32)
    SS = [alloc(f"S{i}", [128, 8, 128], F32) for i in range(2)]
    SBF = [alloc(f"SBF{i}", [128, 8, 128], BF16) for i in range(2)]
    HST = alloc("HST", [128, 8], F32)
    CVS = alloc("CVS", [128, 8, 3], F32)
    PAR = alloc("PAR", [128, NPAR, 8], F32)
    DER = alloc("DER", [128, 8, 8], F32)
    CST = alloc("CST", [128, 384], F32)
    IDB = alloc("IDB", [128, 128], BF16)
    ONESB = alloc("ONESB", [128, 128], BF16)
    PM = alloc("PM", [128, 4], F32)
    su0 = cur[0]
    Hh = alloc("H", [128, NF, 512], BF16)
    Tt = [alloc(f"T{i}", [128, 512], F32) for i in range(2)]
    su_ffn_end = cur[0]
    cur[0] = su0
    G2 = [alloc(f"G{i}", [128, 8, 129], F32) for i in range(2)]
    W2 = [alloc(f"W{i}", [128, 8, 128], F32) for i in range(2)]
    E12 = [alloc(f"E1{i}", [128, 8, 128], F32) for i in range(2)]
    QD2 = [alloc(f"QD{i}", [128, 8, 128], BF16) for i in range(2)]
    KD2 = [alloc(f"KD{i}", [128, 8, 128], BF16) for i in range(2)]
    KPT2 = [[alloc(f"KPT{i}_{m}", [128, 1024], BF16) for m in range(4)] for i in range(2)]
    OO = alloc("OO", [128, 8, 128], F32)
    RSO = alloc("RSO", [128, 8, 128], F32)
    AT = alloc("AT", [128, 8, 128], BF16)
    SQO = alloc("SQO", [128, 8, 128], BF16)
    su_a_end = cur[0]
    cur[0] = su0
    CBt = [alloc(f"CB{i}", [128, 512], BF16) for i in range(2)]
    RA = alloc("RA", [128, 8, 512], F32)
    IGA = alloc("IGA", [128, 8, 512], F32)
    AA = alloc("AA", [128, 8, 512], F32)
    Ut = [alloc(f"U{i}", [128, 512], F32) for i in range(2)]
    HNt = [alloc(f"HN{i}", [128, 512], F32) for i in range(2)]
    su_b_end = cur[0]
    cur[0] = max(su_ffn_end, su_a_end, su_b_end)
    assert cur[0] <= 229344, cur[0]

    PS = [nc.alloc_psum_tensor(f"ps{i}", [128, 512], F32) for i in range(8)]

    Xb = [Buf(f"X{k}") for k in range(NK)]
    XNb = [Buf(f"XN{k}") for k in range(NK)]
    RSb = [Buf(f"RS{i}") for i in range(NSLOT)]
    A0b = [Buf(f"A0_{h}") for h in range(8)]
    A1b = [Buf(f"A1_{h}") for h in range(8)]
    A2b = [Buf(f"A2_{h}") for h in range(8)]
    VBb = Buf("VB")
    ONb = Buf("ON")
    SQb = Buf("SQ")
    RSTDb = Buf("RSTD")
    SSb = [[Buf(f"S{i}_{h}") for h in range(8)] for i in range(2)]
    SBFb = [Buf("SBF0"), Buf("SBF1")]
    HSTb = Buf("HST")
    HSTnb = [Buf(f"HST{n}") for n in range(8)]
    Ynb = [Buf(f"Y{n}") for n in range(8)]
    CVSb = Buf("CVS")
    PARb = Buf("PAR")
    DERb = Buf("DER")
    CSTb = Buf("CST")
    Hb = [Buf(f"H{m}") for m in range(NF)]
    Tb = [Buf("T0"), Buf("T1")]
    FFNB = Hb + Tb
    G2b = [Buf("G0"), Buf("G1")]
    W2b = [Buf("W0"), Buf("W1")]
    E12b = [Buf("E10"), Buf("E11")]
    QD2b = [Buf("QD0"), Buf("QD1")]
    KD2b = [Buf("KD0"), Buf("KD1")]
    KPT2b = [[Buf(f"KPT{i}_{m}") for m in range(4)] for i in range(2)]
    OOb, RSOb, ATb, SQOb = Buf("OO"), Buf("RSO"), Buf("AT"), Buf("SQO")
    TILEB = G2b + W2b + E12b + QD2b + KD2b + KPT2b[0] + KPT2b[1] + [OOb, RSOb, ATb, SQOb]
    CBb = [Buf("CB0"), Buf("CB1")]
    Rb = [Buf(f"R{n}") for n in range(8)]
    IGb = [Buf(f"IG{n}") for n in range(8)]
    Ab = [Buf(f"AA{n}") for n in range(8)]
    Ub = [Buf("U0"), Buf("U1")]
    HNb = [Buf("HN0"), Buf("HN1")]
    BPHB = CBb + Rb + IGb + Ab + Ub + HNb
    PB = [Buf(f"PB{i}") for i in range(8)]
    WBb = [Buf(f"WB{u}") for u in range(NUNIT)]
    CGb = [Buf(f"CG{i}") for i in range(8)]
    YOUTb = Buf("YOUT")
    XLDb = Buf("XLD")
    STb = Buf("STIO")
    STHb = Buf("STH")
    STCb = Buf("STC")
    LDSb = Buf("LDS")
    LDHb = Buf("LDH")
    LDCb = Buf("LDC")

    fw.dma("pool", PAR[:].rearrange("p a b -> p (a b)"), par_d, owner=PARb, writes=[PARb])
    fw.dma("pool", CST[:], cst_d, owner=CSTb, writes=[CSTb])

    IDF = CST[:, 0:128]
    MASK = CST[:, 128:256]
    ONESF = CST[:, 256:384]

    def par(i, h=None):
        if h is None:
            return PAR[:, i, :]
        return PAR[:, i, h:h + 1]

    def der(i, h=None):
        if h is None:
            return DER[:, i, :]
        return DER[:, i, h:h + 1]
    D_LB, D_OML, D_NOML, D_CL, D_CL2, D_T0, D_T1, D_T2 = range(8)

    st = {"next_dma": 0, "next_use": 0, "sbf": 0, "s": 0}
    total_uses = [0]

    def emit_cast():
        for u in range(NUNIT):
            o, sz = UOFF[u], UNITS[u][2]
            cg = CGb[u % 8]
            fw.dma("pool", wb[:, o:o + sz], wall[:, o:o + sz], owner=cg, writes=[cg, WBb[u]])

    def emit_wload(s):
        u = s % NUNIT
        o, sz = UOFF[u], UNITS[u][2]
        slot = s % NSLOT
        fw.dma("sp", RING[:, slot, 0:sz], wb[:, o:o + sz], owner=RSb[slot],
               reads=[WBb[u]], writes=[RSb[slot]])

    def use_unit(kind):
        s = st["next_use"]
        assert UNITS[s % NUNIT][0] == kind, (UNITS[s % NUNIT], kind)
        while st["next_dma"] <= min(s + NSLOT - 1, total_uses[0] - 1):
            emit_wload(st["next_dma"])
            st["next_dma"] += 1
        st["next_use"] += 1
        slot = s % NSLOT
        return RING[:, slot, :], RSb[slot]

    def mm(out, lhsT, rhs, start, stop, reads, writes):
        fw.op("pe", lambda: PE.matmul(out, lhsT=lhsT, rhs=rhs, start=start, stop=stop,
                                      skip_group_check=True), reads, writes)

    def act(out, in_, func, reads, writes, **kw):
        fw.op("act", lambda: A.activation(out=out, in_=in_, func=func, **kw), reads, writes)

    SQkb = [Buf(f"SQ{k}") for k in range(NK)]
    nst = {"presq": False}

    def norm_sq(kc, N):
        act(SQ[:, kc, :N], X[:, kc, :N], AF.Square, [Xb[kc]], [SQkb[kc]])

    def norm_mm(kc, N):
        mm(PS[6][:, :N], ONESB[:], SQ[:, kc, :N], kc == 0, kc == NK - 1, [SQkb[kc], CSTb], [PB[6]])

    def rmsnorm(gidx, N, out_f32=None, out_bufs=None):
        if not nst["presq"]:
            for kc in range(NK):
                norm_sq(kc, N)
                norm_mm(kc, N)
        nst["presq"] = False
        act(RSTD[:, :N], PS[6][:, :N], AF.Sqrt, [PB[6]], [RSTDb], scale=1.0 / D, bias=EPS)
        fw.op("dve", lambda: V.reciprocal(out=RSTD[:, :N], in_=RSTD[:, :N]), [RSTDb], [RSTDb])
        for kc in range(NK):
            if out_f32 is None:
                o, ob = XN[:, kc, :N], [XNb[kc]]
            else:
                o, ob = out_f32[:, kc, :N], [out_bufs[kc]]
            fw.op("dve", lambda o=o, kc=kc: V.scalar_tensor_tensor(
                out=o, in0=X[:, kc, :N], scalar=par(gidx, kc), in1=RSTD[:, :N],
                op0=ALU.mult, op1=ALU.mult), [Xb[kc], RSTDb, PARb], ob)

    def ffn(fi, N):
        rmsnorm(P_FFN0 + fi, N)
        for j in range(11):
            u, ub = use_unit("ffn_in")
            uv = u.rearrange("p (s k c) -> p s k c", s=2, k=NK)
            for mm_ in range(2):
                m = 2 * j + mm_
                pa, pab = PS[m % 2], PB[m % 2]
                pb, pbb = PS[2 + m % 2], PB[2 + m % 2]
                for kc in range(NK):
                    mm(pa[:, :N], uv[:, 0, kc, mm_ * 128:(mm_ + 1) * 128], XN[:, kc, :N],
                       kc == 0, kc == NK - 1, [ub, XNb[kc]], [pab])
                for kc in range(NK):
                    mm(pb[:, :N], uv[:, 1, kc, mm_ * 128:(mm_ + 1) * 128], XN[:, kc, :N],
                       kc == 0, kc == NK - 1, [ub, XNb[kc]], [pbb])
                t, tb = Tt[m % 2], Tb[m % 2]
                act(t[:, :N], pa[:, :N], AF.Silu, [pab], [tb])
                fw.op("dve", lambda t=t, pb=pb, m=m: V.tensor_tensor(
                    out=Hh[:, m, :N], in0=t[:, :N], in1=pb[:, :N], op=ALU.mult), [tb, pbb], [Hb[m]])
        for mo in range(NK):
            u, ub = use_unit("ffn_out")
            uv = u[:, 0:NF * 128].rearrange("p (k c) -> p k c", k=NF)
            py, pyb = PS[4 + mo % 2], PB[4 + mo % 2]
            for kf in range(NF):
                mm(py[:, :N], uv[:, kf, :], Hh[:, kf, :N], kf == 0, kf == NF - 1, [ub, Hb[kf]], [pyb])
            if mo >= 1:
                norm_mm(mo - 1, N)
            fw.op("dve", lambda py=py, mo=mo: V.scalar_tensor_tensor(
                out=X[:, mo, :N], in0=py[:, :N], scalar=0.5, in1=X[:, mo, :N],
                op0=ALU.mult, op1=ALU.add), [pyb, Xb[mo]], [Xb[mo]])
            norm_sq(mo, N)
        norm_mm(NK - 1, N)
        nst["presq"] = True

    def proj_fm(kind, nblk, N, evac):
        u = ub = uv = None
        for blk in range(nblk):
            if blk % 4 == 0:
                u, ub = use_unit(kind)
                uv = u.rearrange("p (k c) -> p k c", k=NK)
            p, pbuf = PS[blk % 2], PB[blk % 2]
            for kc in range(NK):
                mm(p[:, :N], uv[:, kc, (blk % 4) * 128:(blk % 4 + 1) * 128], XN[:, kc, :N],
                   kc == 0, kc == NK - 1, [ub, XNb[kc]], [pbuf])
            evac(blk, p, pbuf)

    def out_proj(kind, N, src, src_bufs):
        u = ub = uv = None
        for mo in range(NK):
            if mo % 4 == 0:
                u, ub = use_unit(kind)
                uv = u.rearrange("p (k c) -> p k c", k=NK)
            py, pyb = PS[4 + mo % 2], PB[4 + mo % 2]
            for kh in range(NK):
                mm(py[:, :N], uv[:, kh, (mo % 4) * 128:(mo % 4 + 1) * 128], src[:, kh, :N],
                   kh == 0, kh == NK - 1, [ub] + src_bufs, [pyb])
            if mo >= 1:
                norm_mm(mo - 1, N)
            fw.op("dve", lambda py=py, mo=mo: V.tensor_tensor(
                out=X[:, mo, :N], in0=py[:, :N], in1=X[:, mo, :N], op=ALU.add), [pyb, Xb[mo]], [Xb[mo]])
            norm_sq(mo, N)
        norm_mm(NK - 1, N)
        nst["presq"] = True

    def mixer_a(N):
        TN = min(N, 128)
        ntile = N // TN
        nb = TN // 16
        rmsnorm(P_MIX0 + 0, N)
        Q, K, LF = A0, A1, A2
        Vv = VB[:].rearrange("p (t c) -> p t c", t=4)

        def ev_q(h, p, pbuf):
            act(Q[:, h, :N], p[:, :N], AF.Silu, [pbuf], [A0b[h]])
        import os
        kstop = os.environ.get("KSTOP", "")
        proj_fm("a_in", 8, N, ev_q)
        if kstop == "q":
            skip(8)
            return

        def ev_f(h, p, pbuf):
            act(K[:, h, :N], p[:, :N], AF.Sigmoid, [pbuf], [A1b[h]])
        proj_fm("a_in", 8, N, ev_f)
        for h in range(8):
            act(LF[:, h, :N], K[:, h, :N], AF.Ln, [A1b[h], DERb], [A2b[h]],
                scale=der(D_OML, h), bias=der(D_LB, h))
            fw.op("dve", lambda h=h: V.tensor_scalar(
                out=K[:, h, :N], in0=K[:, h, :N], scalar1=der(D_NOML, h), scalar2=der(D_OML, h),
                op0=ALU.mult, op1=ALU.add), [A1b[h], DERb], [A1b[h]])
        if kstop == "f":
            skip(6)
            return

        idx = 0
        for c2 in range(2):
            u, ub = use_unit("a_in")
            uv = u.rearrange("p (k c) -> p k c", k=NK)
            for tt in range(ntile):
                p, pbuf = PS[2 + idx % 2], PB[2 + idx % 2]
                for kc in range(NK):
                    mm(p[:TN, :], XN[:, kc, tt * TN:(tt + 1) * TN], uv[:, kc, :],
                       kc == 0, kc == NK - 1, [ub, XNb[kc]], [pbuf])
                if idx % 2 == 0:
                    act(Vv[:TN, tt, c2 * 512:(c2 + 1) * 512], p[:TN, :], AF.Copy, [pbuf], [VBb])
                else:
                    fw.op("dve", lambda p=p, tt=tt, c2=c2: V.tensor_copy(
                        out=Vv[:TN, tt, c2 * 512:(c2 + 1) * 512], in_=p[:TN, :]), [pbuf], [VBb])
                idx += 1

        if kstop == "v":
            skip(4)
            return
        fw.handoff(FFNB, TILEB)
        import os
        sub = int(os.environ.get("KSUB", "99"))
        GP = nc.gpsimd
        for i in range(2 if ntile > 1 else 1):
            fw.op("pool", lambda i=i: GP.memset(G2[i][:], 0.0), [], [G2b[i]])
        opb = [PB[1], PB[2]]

        def prep_stages(tt):
            pr = tt % 2
            G, W, E1, QD, KD, KPT = G2[pr], W2[pr], E12[pr], QD2[pr], KD2[pr], KPT2[pr]
            Gb, Wb, E1b, QDb, KDb, KPTb = G2b[pr], W2b[pr], E12b[pr], QD2b[pr], KD2b[pr], KPT2b[pr]
            tsl = slice(tt * TN, (tt + 1) * TN)

            def v4(ap):
                return ap.rearrange("p h (b j) -> p h b j", j=16)

            def s0():
                for h in range(8):
                    fw.op("dve", lambda h=h: V.tensor_tensor_scan(
                        out=G[:, h, 1:TN + 1], data0=ONESF[:, :TN], data1=LF[:, h, tsl], initial=0.0,
                        op0=ALU.mult, op1=ALU.add), [A2b[h], CSTb], [Gb])

            def s1():
                g1 = v4(G[:, :, 1:TN + 1])
                g0 = v4(G[:, :, 0:TN])[:, :, :, 0:1].to_broadcast([128, 8, nb, 16])
                fw.op("pool", lambda: GP.tensor_tensor(out=v4(W[:, :, :TN]), in0=g1, in1=g0, op=ALU.subtract),
                      [Gb], [Wb])

            def s2():
                act(E1[:, :, :TN], W[:, :, :TN], AF.Exp, [Wb], [E1b])
                act(W[:, :, :TN], W[:, :, :TN], AF.Exp, [Wb], [Wb], scale=-1.0)

            def s3():
                fw.op("pool", lambda: GP.tensor_tensor(out=QD[:, :, :TN], in0=Q[:, :, tsl], in1=E1[:, :, :TN],
                                                       op=ALU.mult), A0b + [E1b], [QDb])
                fw.op("pool", lambda: GP.tensor_tensor(out=W[:, :, :TN], in0=K[:, :, tsl], in1=W[:, :, :TN],
                                                       op=ALU.mult), A1b + [Wb], [Wb])

            def s4():
                act(KD[:, :, :TN], W[:, :, :TN], AF.Copy, [Wb], [KDb])
                e1l = v4(E1[:, :, :TN])[:, :, :, 15:16].to_broadcast([128, 8, nb, 16])
                fw.op("pool", lambda: GP.tensor_tensor(out=v4(W[:, :, :TN]), in0=v4(W[:, :, :TN]), in1=e1l,
                                                       op=ALU.mult), [Wb, E1b], [Wb])

            def s5(hf):
                def f():
                    bank = 7 if hf == 0 else 0
                    for h in range(4 * hf, 4 * hf + 4):
                        fw.op("pe", lambda h=h: PE.transpose(PS[bank][:TN, (h % 4) * 128:(h % 4 + 1) * 128],
                                                             W[:, h, :TN], IDF), [Wb, CSTb], [PB[bank]])
                    nmask = 4 if TN == 128 else 1
                    for m4 in range(nmask):
                        fw.op("dve", lambda m4=m4: V.tensor_scalar(
                            out=KPT[m4][:TN, hf * 512:(hf + 1) * 512], in0=PS[bank][:TN, :],
                            scalar1=PM[:TN, m4:m4 + 1], scalar2=None, op0=ALU.mult),
                            [PB[bank], CSTb], [KPTb[m4]])
                return f
            return [s0, s1, s2, s3, s4, s5(0), s5(1)]

        SLOTS = {0: [0, 1], 2: [2], 3: [3], 5: [4], 6: [5], 7: [6]}

        def front_scores(tt):
            pr = tt % 2
            QD, KD, QDb, KDb = QD2[pr], KD2[pr], QD2b[pr], KD2b[pr]
            for hg in range(2):
                sb = 0 if hg == 0 else 7
                for h in range(hg * 4, hg * 4 + 4):
                    mm(PS[sb][:TN, (h % 4) * TN:(h % 4 + 1) * TN], KD[:, h, :TN], QD[:, h, :TN],
                       h % 4 == 0, h % 4 == 3, [KDb, QDb], [PB[sb]])
                scv = PS[sb][:TN, 0:4 * TN].rearrange("p (h t) -> p h t", h=4)
                mk = MASK[:TN, :TN].unsqueeze(1).to_broadcast([TN, 4, TN])
                fw.op("dve", lambda scv=scv, mk=mk, hg=hg: V.tensor_tensor(
                    out=AT[:TN, hg * 4:hg * 4 + 4, :TN], in0=scv, in1=mk, op=ALU.mult), [PB[sb], CSTb], [ATb])

        def emit_u(tt, n):
            KPT, KPTb = KPT2[tt % 2], KPT2b[tt % 2]
            if TN == 128:
                r0, kr, kp, kpb = 64 * (n // 4), 64, KPT[n % 4], KPTb[n % 4]
            else:
                r0, kr, kp, kpb = 0, 16, KPT[0], KPTb[0]
            bk = 3 + 2 * (n % 2)
            for h in range(8):
                mm(PS[bk + h // 4][:, (h % 4) * 128:(h % 4 + 1) * 128],
                   kp[r0:r0 + kr, h * 128:(h + 1) * 128], Vv[r0:r0 + kr, tt, h * 128:(h + 1) * 128],
                   h % 4 == 0, h % 4 == 3, [kpb, VBb], [PB[bk + h // 4]])

        def tile_body(tt, nxt):
            pr = tt % 2
            E1, QD, KD, KPT = E12[pr], QD2[pr], KD2[pr], KPT2[pr]
            E1b, QDb, KDb, KPTb = E12b[pr], QD2b[pr], KD2b[pr], KPT2b[pr]
            tsl = slice(tt * TN, (tt + 1) * TN)
            for hg in range(2):
                for h in range(hg * 4, hg * 4 + 4):
                    mm(PS[1 + hg][:, (h % 4) * TN:(h % 4 + 1) * TN], Vv[:TN, tt, h * 128:(h + 1) * 128],
                       AT[:TN, h, :TN], h % 4 == 0, False, [VBb, ATb], [opb[hg]])

            if tt == 0:
                emit_u(tt, 0)
            for n in range(nb):
                if n + 1 < nb:
                    emit_u(tt, n + 1)
                cs = st["sbf"]
                for h in range(8):
                    mm(PS[1 + h // 4][:, (h % 4) * TN + 16 * n:(h % 4) * TN + 16 * n + 16],
                       SBF[cs][:, h, :], QD[:, h, 16 * n:16 * n + 16], False, False,
                       [SBFb[cs], QDb], [opb[h // 4]])
                bk = 3 + 2 * (n % 2)
                si = st["s"]
                so = 1 - si
                for h in range(8):
                    fw.op("dve", lambda h=h, bk=bk, n=n, si=si, so=so: V.scalar_tensor_tensor(
                        out=SS[so][:, h, :], in0=SS[si][:, h, :], scalar=E1[:, h, 16 * n + 15:16 * n + 16],
                        in1=PS[bk + h // 4][:, (h % 4) * 128:(h % 4 + 1) * 128],
                        op0=ALU.mult, op1=ALU.add), [SSb[si][h], E1b, PB[bk + h // 4]], [SSb[so][h]])
                st["s"] = so
                ns = 1 - cs
                act(SBF[ns][:], SS[so][:], AF.Copy, SSb[so], [SBFb[ns]])
                st["sbf"] = ns
                if nxt:
                    for si_ in SLOTS.get(n, []):
                        nxt[si_]()
            for hg in range(2):
                pv = PS[1 + hg][:, 0:4 * TN].rearrange("p (h t) -> p h t", h=4)
                act(OO[:, hg * 4:hg * 4 + 4, :TN], pv, AF.Copy, [opb[hg]], [OOb])
            if tt + 1 < ntile:
                front_scores(tt + 1)
                emit_u(tt + 1, 0)
            fw.op("pool", lambda: GP.tensor_tensor(out=SQO[:, :, :TN], in0=OO[:, :, :TN], in1=OO[:, :, :TN],
                                                   op=ALU.mult), [OOb], [SQOb])
            for h in range(8):
                mm(PS[5 + h // 4][:, (h % 4) * TN:(h % 4 + 1) * TN], ONESB[:], SQO[:, h, :TN],
                   h % 4 == 0, h % 4 == 3, [SQOb, CSTb], [PB[5 + h // 4]])
            for hg in range(2):
                pv = PS[5 + hg][:, 0:4 * TN].rearrange("p (h t) -> p h t", h=4)
                act(RSO[:, hg * 4:hg * 4 + 4, :TN], pv, AF.Ln, [PB[5 + hg]], [RSOb], scale=1.0 / 128, bias=EPS)
            act(RSO[:, :, :TN], RSO[:, :, :TN], AF.Exp, [RSOb], [RSOb], scale=-0.5)
            fw.op("pool", lambda: GP.tensor_tensor(out=Q[:, :, tsl], in0=OO[:, :, :TN], in1=RSO[:, :, :TN],
                                                   op=ALU.mult), [OOb, RSOb], A0b)

        for f in prep_stages(0):
            f()
        front_scores(0)
        for tt in range(ntile):
            nxt = prep_stages(tt + 1) if tt + 1 < ntile else []
            tile_body(tt, nxt)

        fw.handoff(TILEB, FFNB)
        if sub < 7:
            skip(4)
            return

        def ev_g(h, p, pbuf):
            act(LF[:, h, :N], p[:, :N], AF.Silu, [pbuf], [A2b[h]])
            fw.op("dve", lambda: V.scalar_tensor_tensor(
                out=ON[:, h, :N], in0=Q[:, h, :N], scalar=par(P_ONORM, h), in1=LF[:, h, :N],
                op0=ALU.mult, op1=ALU.mult), [A0b[h], A2b[h], PARb], A1b)
        proj_fm("a_in", 8, N, ev_g)
        out_proj("a_out", N, ON, A1b)

    def mixer_b(N):
        rmsnorm(P_MIX0 + 1, N)
        XBr, GG, CV = A0, A1, A2
        Y = VB[:].rearrange("p (k t) -> p k t", k=8)

        fw.op("dve", lambda: V.tensor_copy(out=XBr[:, :, 0:3], in_=CVS[:]), [CVSb], A0b)

        def ev_x(n, p, pbuf):
            act(XBr[:, n, 3:3 + N], p[:, :N], AF.Copy, [pbuf], [A0b[n]])
        proj_fm("b_in", 8, N, ev_x)

        def ev_g(n, p, pbuf):
            act(GG[:, n, :N], p[:, :N], AF.Gelu, [pbuf], [A1b[n]])
        proj_fm("b_in", 8, N, ev_g)

        fw.handoff(FFNB, BPHB)
        u, ub = use_unit("b_gate")
        uv = u[:, 0:2048].rearrange("p (s n d) -> p s n d", s=2, n=8)
        fw.handoff([VBb, HSTb], Ynb + HSTnb)

        def stage_a(n):
            i2 = n % 2
            fw.op("pool", lambda n=n: nc.gpsimd.tensor_scalar(
                out=CV[:, n, :N], in0=XBr[:, n, 0:N], scalar1=par(P_CW0 + 0, n), scalar2=par(P_CB, n),
                op0=ALU.mult, op1=ALU.add), [A0b[n], PARb], [A2b[n]])
            for j in range(1, 4):
                fw.op("dve", lambda n=n, j=j: V.scalar_tensor_tensor(
                    out=CV[:, n, :N], in0=XBr[:, n, j:j + N], scalar=par(P_CW0 + j, n), in1=CV[:, n, :N],
                    op0=ALU.mult, op1=ALU.add), [A0b[n], A2b[n], PARb], [A2b[n]])
            act(CBt[i2][:, :N], CV[:, n, :N], AF.Copy, [A2b[n]], [CBb[i2]])
            pr, prb = PS[2 + 2 * i2], PB[2 + 2 * i2]
            pi, pib = PS[3 + 2 * i2], PB[3 + 2 * i2]
            mm(pr[:, :N], uv[:, 0, n, :], CBt[i2][:, :N], True, True, [ub, CBb[i2]], [prb])
            mm(pi[:, :N], uv[:, 1, n, :], CBt[i2][:, :N], True, True, [ub, CBb[i2]], [pib])
            act(RA[:, n, :N], pr[:, :N], AF.Sigmoid, [prb, PARb], [Rb[n]], bias=par(P_BA, n))
            act(IGA[:, n, :N], pi[:, :N], AF.Sigmoid, [pib, PARb], [IGb[n]], bias=par(P_BX, n))

        def stage_b(n):
            i2 = n % 2
            fw.op("dve", lambda i2=i2, n=n: V.tensor_tensor(
                out=Ut[i2][:, :N], in0=IGA[:, n, :N], in1=CV[:, n, :N], op=ALU.mult), [IGb[n], A2b[n]], [Ub[i2]])
            fw.op("dve", lambda i2=i2, n=n: V.tensor_tensor(
                out=Ut[i2][:, :N], in0=Ut[i2][:, :N], in1=RA[:, n, :N], op=ALU.mult), [Ub[i2], Rb[n]], [Ub[i2]])
            fw.op("dve", lambda i2=i2, n=n: V.tensor_tensor_scan(
                out=HNt[i2][:, :N], data0=AA[:, n, :N], data1=Ut[i2][:, :N], initial=HST[:, n:n + 1],
                op0=ALU.mult, op1=ALU.add), [Ab[n], Ub[i2], HSTnb[n]], [HNb[i2]])
            fw.op("dve", lambda i2=i2, n=n: V.tensor_copy(out=HST[:, n:n + 1], in_=HNt[i2][:, N - 1:N]),
                  [HNb[i2]], [HSTnb[n]])
            fw.op("dve", lambda i2=i2, n=n: V.tensor_tensor(
                out=Y[:, n, :N], in0=HNt[i2][:, :N], in1=GG[:, n, :N], op=ALU.mult), [HNb[i2], A1b[n]], [Ynb[n]])

        for n in range(8):
            stage_a(n)
        for half in range(2):
            ns_ = range(4 * half, 4 * half + 4)
            for n in ns_:
                act(AA[:, n, :N], RA[:, n, :N], AF.Exp, [Rb[n], DERb], [Ab[n]], scale=der(D_CL, n))
                act(RA[:, n, :N], RA[:, n, :N], AF.Exp, [Rb[n], DERb], [Rb[n]], scale=der(D_CL2, n))
            for n in ns_:
                act(RA[:, n, :N], RA[:, n, :N], AF.Sqrt, [Rb[n]], [Rb[n]], scale=-1.0, bias=1.0)
        for n in range(8):
            stage_b(n)
        fw.op("dve", lambda: V.tensor_copy(out=CVS[:], in_=XBr[:, :, N:N + 3]), A0b, [CVSb])
        fw.handoff(BPHB, FFNB)
        out_proj("b_out", N, Y, Ynb)
        fw.handoff(Ynb + HSTnb, [VBb, HSTb])

    LEVELS = ["io", "ffn0", "mixa", "ffn1", "ffn2", "mixb", "ffn3", "all"]
    lvl = LEVELS.index(dbg) if dbg else len(LEVELS) - 1

    def skip(k):
        st["next_use"] += k

    def prefetch(src, N):
        fw.dma("pool", A0[:, :, :N], src, owner=XLDb, writes=A0b)

    def chunk(src, dst, N, first, nxt):
        if first:
            prefetch(src, N)
            emit_cast()
        for kc in range(NK):
            act(X[:, kc, :N], A0[:, kc, :N], AF.Copy, [A0b[kc]], [Xb[kc]])
        nst["presq"] = False
        if lvl >= 1:
            ffn(0, N)
        else:
            skip(19)
        if lvl >= 2:
            mixer_a(N)
        else:
            skip(10)
        if lvl >= 3:
            ffn(1, N)
        else:
            skip(19)
        if lvl >= 4:
            ffn(2, N)
        else:
            skip(19)
        if lvl >= 5:
            mixer_b(N)
        else:
            skip(7)
        if nxt is not None:
            prefetch(nxt[0], nxt[1])
        if lvl >= 6:
            ffn(3, N)
        else:
            skip(19)
        if lvl >= 7:
            rmsnorm(P_FIN, N, out_f32=A1, out_bufs=A1b)
        else:
            fw.op("dve", lambda: V.tensor_copy(out=A1[:, :, :N], in_=X[:, :, :N]), Xb, A1b)
        fw.dma("pool", dst, A1[:, :, :N], owner=YOUTb, reads=A1b, writes=[YOUTb])

    def write_states(idx):
        fw.dma("pool", sout[idx], SS[st["s"]][:], owner=STb, reads=SSb[st["s"]], writes=[STb])
        fw.dma("pool", hout[:, idx, :], HST[:], owner=STHb, reads=[HSTb], writes=[STHb])
        fw.dma("pool", cout[:, idx, :, :], CVS[:], owner=STCb, reads=[CVSb], writes=[STCb])

    chunks = []
    pos = 0
    while pos < T:
        n = min(512, T - pos)
        chunks.append((pos, n))
        pos += n
    total_uses[0] = NUNIT * (len(chunks) + NS)

    pm_d = nc.dram_tensor("pm", [128, 4], F32, kind="ExternalInput").ap()
    fw.dma("pool", PM[:], pm_d, owner=CSTb, writes=[CSTb])
    r = [PARb]
    w = [DERb]
    act(der(D_T0), par(P_LB0), AF.Exp, r, w)
    act(der(D_T1), par(P_LB1), AF.Exp, r, w)
    fw.op("dve", lambda: V.tensor_tensor(out=der(D_T2), in0=der(D_T0), in1=der(D_T1), op=ALU.add), [DERb], w)
    fw.op("dve", lambda: V.reciprocal(out=der(D_T2), in_=der(D_T2)), [DERb], w)
    fw.op("dve", lambda: V.tensor_tensor(out=der(D_LB), in0=der(D_T0), in1=der(D_T2), op=ALU.mult), [DERb], w)
    fw.op("dve", lambda: V.tensor_tensor(out=der(D_OML), in0=der(D_T1), in1=der(D_T2), op=ALU.mult), [DERb], w)
    fw.op("dve", lambda: V.tensor_scalar(out=der(D_NOML), in0=der(D_OML), scalar1=-1.0, scalar2=None, op0=ALU.mult), [DERb], w)
    act(der(D_T0), par(P_LAM), AF.Exp, r + [DERb], w, scale=-1.0)
    act(der(D_T0), der(D_T0), AF.Ln, [DERb], w, bias=1.0)
    fw.op("dve", lambda: V.tensor_scalar(out=der(D_CL), in0=der(D_T0), scalar1=-8.0, scalar2=None, op0=ALU.mult), [DERb], w)
    fw.op("dve", lambda: V.tensor_scalar(out=der(D_CL2), in0=der(D_T0), scalar1=-16.0, scalar2=None, op0=ALU.mult), [DERb], w)
    fw.op("dve", lambda: V.tensor_copy(out=IDB[:], in_=IDF), [CSTb], [CSTb])
    fw.op("dve", lambda: V.tensor_copy(out=ONESB[:], in_=ONESF), [CSTb], [CSTb])

    fw.op("dve", lambda: V.memset(SS[0][:], 0.0), [], SSb[0])
    fw.op("dve", lambda: V.memset(SBF[0][:], 0.0), [], [SBFb[0]])
    fw.op("dve", lambda: V.memset(HST[:], 0.0), [], [HSTb])
    fw.op("dve", lambda: V.memset(CVS[:], 0.0), [], [CVSb])
    st["sbf"] = 0
    jobs = [(xseq[:, :, pos:pos + n], yseq[:, :, pos:pos + n], n) for (pos, n) in chunks]
    for s in range(NS):
        jobs.append((xsmp[:, :, s * NSMP:(s + 1) * NSMP], ysmp[:, :, s * NSMP:(s + 1) * NSMP], NSMP))
    npr = len(chunks)
    for ji, (src, dst, n) in enumerate(jobs):
        nxt = (jobs[ji + 1][0], jobs[ji + 1][2]) if ji + 1 < len(jobs) else None
        if ji >= npr:
            sidx = ji - npr
            fw.dma("pool", SS[st["s"]][:], s0_d[sidx], owner=LDSb, reads=[], writes=SSb[st["s"]])
            fw.dma("pool", HST[:], h0_d[:, sidx, :], owner=LDHb, writes=[HSTb])
            fw.dma("pool", CVS[:], c0_d[:, sidx, :, :], owner=LDCb, writes=[CVSb])
            cs = st["sbf"]
            act(SBF[cs][:], SS[st["s"]][:], AF.Copy, SSb[st["s"]], [SBFb[cs]])
        chunk(src, dst, n, ji == 0, nxt)
        if ji == npr - 1:
            write_states(0)
        elif ji >= npr:
            write_states(1 + ji - npr)
    fw.final_wait("pool", [YOUTb, STb, STHb, STCb])
    assert st["next_use"] == total_uses[0], (st["next_use"], total_uses[0])
    return nc, fw


def fm(v):
    return np.ascontiguousarray(np.asarray(v, np.float32).reshape(8, 128).T)


def build_wall(ffn_w_in, ffn_w_out, a_w_in, a_w_out, b_w_in, b_wa, b_wx, b_w_out):
    wall = np.empty((128, WTOT), np.float32)

    def std(wm, u):
        blk = wm[:, u * 512:(u + 1) * 512].reshape(NK, 128, 512)
        return blk.transpose(1, 0, 2).reshape(128, -1)
    for ui, (kind, args, sz) in enumerate(UNITS):
        o = UOFF[ui]
        if kind == "ffn_in":
            fi, j = args
            wm = ffn_w_in[fi // 2, fi % 2].reshape(NK, 128, 2, DFF)[:, :, :, j * 256:(j + 1) * 256]
            blk = wm.transpose(1, 2, 0, 3).reshape(128, -1)
        elif kind == "ffn_out":
            fi, mo = args
            wm = ffn_w_out[fi // 2, fi % 2].reshape(NF, 128, 8, 128)[:, :, mo, :]
            blk = wm.transpose(1, 0, 2).reshape(128, -1)
        elif kind == "a_in":
            blk = std(a_w_in[0], args[0])
        elif kind == "a_out":
            blk = std(a_w_out[0], args[0])
        elif kind == "b_in":
            blk = std(b_w_in[0], args[0])
        elif kind == "b_out":
            blk = std(b_w_out[0], args[0])
        elif kind == "b_gate":
            blk = np.stack([b_wa[0], b_wx[0]], 0).transpose(2, 0, 1, 3).reshape(128, -1)
        assert blk.shape[1] == sz, (kind, blk.shape, sz)
        wall[:, o:o + sz] = blk
    return wall


def build_consts():
    cst = np.zeros((128, 384), np.float32)
    cst[:, 0:128] = np.eye(128, dtype=np.float32)
    s = np.arange(128)[:, None]
    t = np.arange(128)[None, :]
    cst[:, 128:256] = ((s // 16 == t // 16) & (s <= t)).astype(np.float32)
    cst[:, 256:384] = 1.0
    pm = np.zeros((128, 4), np.float32)
    blk = np.arange(128) // 16
    for m4 in range(4):
        pm[:, m4] = (blk % 4 == m4)
    return cst, pm


_CACHE = {}


def kernel(x_prompt, x_sample, state_hgrn, state_rglru, state_conv, meta_tokens,
           ffn_norm, ffn_w_in, ffn_w_out, mix_norm, a_w_in, a_lb, a_onorm, a_w_out,
           b_w_in, b_conv_w, b_conv_b, b_wa, b_ba, b_wx, b_bx, b_lambda, b_w_out, final_norm,
           _ncores=8):
    f32 = np.float32
    x_prompt = np.asarray(x_prompt, f32)
    x_sample = np.asarray(x_sample, f32)
    B, SEQ, _ = x_prompt.shape
    DB, DS, _ = x_sample.shape
    T = SEQ + NMETA
    NS = DB // _ncores
    import os
    dbg = os.environ.get("KDBG") or None
    key = (T, NS, DS, dbg)
    if key not in _CACHE:
        _CACHE[key] = build_program(T, NS, DS, dbg)
    nc, fw = _CACHE[key]

    wall = build_wall(np.asarray(ffn_w_in, f32), np.asarray(ffn_w_out, f32), np.asarray(a_w_in, f32),
                      np.asarray(a_w_out, f32), np.asarray(b_w_in, f32), np.asarray(b_wa, f32),
                      np.asarray(b_wx, f32), np.asarray(b_w_out, f32))
    pv = [None] * NPAR
    fn = np.asarray(ffn_norm, f32)
    for l in range(2):
        for i in range(2):
            pv[P_FFN0 + l * 2 + i] = fm(fn[l, i])
    mn = np.asarray(mix_norm, f32)
    pv[P_MIX0], pv[P_MIX0 + 1] = fm(mn[0]), fm(mn[1])
    pv[P_FIN] = fm(final_norm)
    alb = np.asarray(a_lb, f32)
    pv[P_LB0], pv[P_LB1] = fm(alb[0]), fm(alb[1])
    pv[P_ONORM] = fm(np.asarray(a_onorm, f32)[0])
    cw = np.asarray(b_conv_w, f32)[0]
    for j in range(4):
        pv[P_CW0 + j] = fm(cw[j])
    pv[P_CB] = fm(np.asarray(b_conv_b, f32)[0])
    pv[P_BA] = fm(np.asarray(b_ba, f32)[0])
    pv[P_BX] = fm(np.asarray(b_bx, f32)[0])
    pv[P_LAM] = fm(np.asarray(b_lambda, f32)[0])
    par = np.ascontiguousarray(np.stack(pv, 1).reshape(128, NPAR * 8))
    cst, pm = build_consts()

    meta = np.asarray(meta_tokens, f32)
    sh = np.asarray(state_hgrn, f32)[0]
    sr = np.asarray(state_rglru, f32)[0]
    sc = np.asarray(state_conv, f32)[0]

    owner = {0: 0, 1: 1, 4: 2, 5: 3} if _ncores == 8 else {i: i for i in range(min(B, _ncores))}
    in_maps = []
    for c in range(_ncores):
        if c in owner:
            xs = np.concatenate([meta, x_prompt[owner[c]]], 0)
            xseq = np.ascontiguousarray(xs.reshape(T, NK, 128).transpose(2, 1, 0))
        else:
            xseq = np.zeros((128, NK, T), f32)
        ss = slice(c * NS, (c + 1) * NS)
        xm = x_sample[ss].reshape(NS * DS, NK, 128).transpose(2, 1, 0)
        in_maps.append({
            "wall": wall, "par": par, "cst": cst, "pm": pm,
            "xseq": xseq, "xsmp": np.ascontiguousarray(xm),
            "s0": np.ascontiguousarray(sh[ss].transpose(0, 2, 1, 3)),
            "h0": np.ascontiguousarray(sr[ss].reshape(NS, 8, 128).transpose(2, 0, 1)),
            "c0": np.ascontiguousarray(sc[ss].reshape(NS, 3, 8, 128).transpose(3, 0, 2, 1)),
        })
    res = run_bass_kernel_spmd(nc, in_maps, core_ids=list(range(_ncores)))
    R = res.results

    y_prompt = np.empty((B, SEQ, D), f32)
    y_sample = np.empty((DB, DS, D), f32)
    hg_p = np.empty((1, B, 8, 128, 128), f32)
    hg_s = np.empty((1, DB, 8, 128, 128), f32)
    rg_p = np.empty((1, B, D), f32)
    rg_s = np.empty((1, DB, D), f32)
    cv_p = np.empty((1, B, 3, D), f32)
    cv_s = np.empty((1, DB, 3, D), f32)
    for c in range(_ncores):
        r = R[c]
        so, ho, co = r["sout"], r["hout"], r["cout"]
        if c in owner:
            b = owner[c]
            y_prompt[b] = r["yseq"].transpose(2, 1, 0).reshape(T, D)[NMETA:]
            hg_p[0, b] = so[0].transpose(1, 0, 2)
            rg_p[0, b] = ho[:, 0, :].T.reshape(D)
            cv_p[0, b] = co[:, 0].transpose(2, 1, 0).reshape(3, D)
        ym = r["ysmp"].transpose(2, 1, 0).reshape(NS, DS, D)
        for s in range(NS):
            g = c * NS + s
            y_sample[g] = ym[s]
            hg_s[0, g] = so[1 + s].transpose(1, 0, 2)
            rg_s[0, g] = ho[:, 1 + s, :].T.reshape(D)
            cv_s[0, g] = co[:, 1 + s].transpose(2, 1, 0).reshape(3, D)
    return (y_prompt, y_sample, hg_p, hg_s, rg_p, rg_s, cv_p, cv_s)
```

```python
import numpy as np
import concourse.bass as bass
import concourse.mybir as mybir
from concourse.bass_utils import run_bass_kernel_spmd

F32 = mybir.dt.float32
BF16 = mybir.dt.bfloat16
AF = mybir.ActivationFunctionType
ALU = mybir.AluOpType

D = 1024
NK = 8
DFF = 2816
NF = 22
NMETA = 16
EPS = 1e-6
NSLOT = 5
USZ = 4096


class Buf:
    __slots__ = ("name", "w", "r", "dsem", "dcnt")

    def __init__(self, name):
        self.name = name
        self.w = None
        self.r = {}
        self.dsem = None
        self.dcnt = 0


class FW:
    def __init__(self, nc):
        self.nc = nc
        self.eng = {"pe": nc.tensor, "act": nc.scalar, "dve": nc.vector,
                    "pool": nc.gpsimd, "sp": nc.sync}
        self.sem = {}
        self.cnt = {}
        self.waited = {}
        self.semobj = {}
        for e in self.eng:
            self.sem[e] = nc.alloc_semaphore(name="prog_" + e)
            self.semobj[e] = self.sem[e]
            self.cnt[e] = 0
            self.waited[e] = {}
        self.ninst = 0

    def _need(self, reads, writes):
        need = {}

        def add(clk):
            if clk is None:
                return
            k, v = clk
            if v > need.get(k, 0):
                need[k] = v
        for b in reads:
            add(b.w)
        for b in writes:
            add(b.w)
            for k, v in b.r.items():
                add((k, v))
        return need

    def _emit_waits(self, e, need, skip_self=False):
        eng = self.eng[e]
        wd = self.waited[e]
        for k, v in need.items():
            if skip_self and k == e:
                continue
            if wd.get(k, 0) >= v:
                continue
            eng.wait_ge(self.semobj[k], v)
            wd[k] = v

    def op(self, e, fn, reads=(), writes=()):
        need = self._need(reads, writes)
        self._emit_waits(e, need, skip_self=(e == "pe"))
        ins = fn()
        self.cnt[e] += 1
        c = self.cnt[e]
        ins.then_inc(self.sem[e], 1)
        for b in writes:
            b.w = (e, c)
            b.r = {}
        for b in reads:
            if b not in writes:
                b.r[e] = c
        self.ninst += 1
        return ins

    def dma(self, q, out_ap, in_ap, owner, reads=(), writes=(), **kw):
        need = self._need(reads, writes)
        self._emit_waits(q, need)
        k = "d_" + owner.name
        if owner.dsem is None:
            owner.dsem = self.nc.alloc_semaphore(name=k)
            self.semobj[k] = owner.dsem
        owner.dcnt += 1
        v = 16 * owner.dcnt
        ins = self.eng[q].dma_start(out=out_ap, in_=in_ap, **kw)
        ins.then_inc(owner.dsem, 16)
        for b in writes:
            b.w = (k, v)
            b.r = {}
        for b in reads:
            if b not in writes:
                b.r[k] = v
        self.ninst += 1
        return ins

    def handoff(self, olds, news):
        merged = {}
        for b in olds:
            if b.w is not None and b.w[1] > merged.get(b.w[0], 0):
                merged[b.w[0]] = b.w[1]
            for k, v in b.r.items():
                if v > merged.get(k, 0):
                    merged[k] = v
        for b in news:
            b.w = None
            b.r = dict(merged)

    def final_wait(self, e, bufs):
        self._emit_waits(e, self._need([], bufs))


def unit_table():
    units = []

    def ffn(fi):
        for j in range(11):
            units.append(("ffn_in", (fi, j), 4096))
        for mo in range(8):
            units.append(("ffn_out", (fi, mo), NF * 128))
    ffn(0)
    for u in range(8):
        units.append(("a_in", (u,), 4096))
    for u in range(2):
        units.append(("a_out", (u,), 4096))
    ffn(1)
    ffn(2)
    for u in range(4):
        units.append(("b_in", (u,), 4096))
    units.append(("b_gate", (), 2048))
    for u in range(2):
        units.append(("b_out", (u,), 4096))
    ffn(3)
    offs = []
    o = 0
    for _, _, sz in units:
        offs.append(o)
        o += sz
    return units, offs, o


UNITS, UOFF, WTOT = unit_table()
NUNIT = len(UNITS)

NPAR = 18
(P_FFN0, P_MIX0, P_FIN, P_LB0, P_LB1, P_ONORM, P_CW0, P_CB, P_BA, P_BX, P_LAM) = (0, 4, 6, 7, 8, 9, 10, 14, 15, 16, 17)


def build_program(T, NS=2, NSMP=16, dbg=None):
    nc = bass.Bass("TRN2", target_bir_lowering=False)
    fw = FW(nc)
    V = nc.vector
    A = nc.scalar
    PE = nc.tensor

    wall = nc.dram_tensor("wall", [128, WTOT], F32, kind="ExternalInput").ap()
    par_d = nc.dram_tensor("par", [128, NPAR * 8], F32, kind="ExternalInput").ap()
    cst_d = nc.dram_tensor("cst", [128, 384], F32, kind="ExternalInput").ap()
    xseq = nc.dram_tensor("xseq", [128, NK, T], F32, kind="ExternalInput").ap()
    xsmp = nc.dram_tensor("xsmp", [128, NK, NS * NSMP], F32, kind="ExternalInput").ap()
    s0_d = nc.dram_tensor("s0", [NS, 128, 8, 128], F32, kind="ExternalInput").ap()
    h0_d = nc.dram_tensor("h0", [128, NS, 8], F32, kind="ExternalInput").ap()
    c0_d = nc.dram_tensor("c0", [128, NS, 8, 3], F32, kind="ExternalInput").ap()
    yseq = nc.dram_tensor("yseq", [128, NK, T], F32, kind="ExternalOutput").ap()
    ysmp = nc.dram_tensor("ysmp", [128, NK, NS * NSMP], F32, kind="ExternalOutput").ap()
    sout = nc.dram_tensor("sout", [1 + NS, 128, 8, 128], F32, kind="ExternalOutput").ap()
    hout = nc.dram_tensor("hout", [128, 1 + NS, 8], F32, kind="ExternalOutput").ap()
    cout = nc.dram_tensor("cout", [128, 1 + NS, 8, 3], F32, kind="ExternalOutput").ap()
    wb = nc.dram_tensor("wb", [128, WTOT], BF16, kind="Internal").ap()

    cur = [16512]

    def alloc(name, shape, dt, at=None):
        nbytes = int(np.prod(shape[1:])) * (4 if dt == F32 else 2)
        if at is None:
            off = cur[0]
            cur[0] += (nbytes + 31) // 32 * 32
        else:
            off = at
        return nc.alloc_sbuf_tensor_at(name, list(shape), dt, offset=off)

    X = alloc("X", [128, NK, 512], F32)
    XN = alloc("XN", [128, NK, 512], BF16)
    RING = alloc("RING", [128, NSLOT, USZ], BF16)
    A0 = alloc("A0", [128, 8, 520], F32)
    a1_off = cur[0]
    A1 = alloc("A1", [128, 8, 520], F32)
    ON = alloc("ON", [128, 8, 512], BF16, at=a1_off)
    A2 = alloc("A2", [128, 8, 520], F32)
    VB = alloc("VB", [128, 4096], BF16)
    SQ = alloc("SQ", [128, NK, 512], BF16)
    RSTD = alloc("RSTD", [128, 512], F32)
    SS = [alloc(f"S{i}", [128, 8, 128], F32) for i in range(2)]
    SBF = [alloc(f"SBF{i}", [128, 8, 128], BF16) for i in range(2)]
    HST = alloc("HST", [128, 8], F32)
    CVS = alloc("CVS", [128, 8, 3], F32)
    PAR = alloc("PAR", [128, NPAR, 8], F32)
    DER = alloc("DER", [128, 8, 8], F32)
    CST = alloc("CST", [128, 384], F32)
    IDB = alloc("IDB", [128, 128], BF16)
    ONESB = alloc("ONESB", [128, 128], BF16)
    PM = alloc("PM", [128, 4], F32)
    su0 = cur[0]
    Hh = alloc("H", [128, NF, 512], BF16)
    Tt = [alloc(f"T{i}", [128, 512], F32) for i in range(2)]
    su_ffn_end = cur[0]
    cur[0] = su0
    G2 = [alloc(f"G{i}", [128, 8, 129], F32) for i in range(2)]
    W2 = [alloc(f"W{i}", [128, 8, 128], F32) for i in range(2)]
    E12 = [alloc(f"E1{i}", [128, 8, 128], F32) for i in range(2)]
    QD2 = [alloc(f"QD{i}", [128, 8, 128], BF16) for i in range(2)]
    KD2 = [alloc(f"KD{i}", [128, 8, 128], BF16) for i in range(2)]
    KPT2 = [[alloc(f"KPT{i}_{m}", [128, 1024], BF16) for m in range(4)] for i in range(2)]
    OO = alloc("OO", [128, 8, 128], F32)
    RSO = alloc("RSO", [128, 8, 128], F32)
    AT = alloc("AT", [128, 8, 128], BF16)
    SQO = alloc("SQO", [128, 8, 128], BF16)
    su_a_end = cur[0]
    cur[0] = su0
    CBt = [alloc(f"CB{i}", [128, 512], BF16) for i in range(2)]
    RA = alloc("RA", [128, 8, 512], F32)
    IGA = alloc("IGA", [128, 8, 512], F32)
    AA = alloc("AA", [128, 8, 512], F32)
    Ut = [alloc(f"U{i}", [128, 512], F32) for i in range(2)]
    HNt = [alloc(f"HN{i}", [128, 512], F32) for i in range(2)]
    su_b_end = cur[0]
    cur[0] = max(su_ffn_end, su_a_end, su_b_end)
    assert cur[0] <= 229344, cur[0]

    PS = [nc.alloc_psum_tensor(f"ps{i}", [128, 512], F32) for i in range(8)]

    Xb = [Buf(f"X{k}") for k in range(NK)]
    XNb = [Buf(f"XN{k}") for k in range(NK)]
    RSb = [Buf(f"RS{i}") for i in range(NSLOT)]
    A0b = [Buf(f"A0_{h}") for h in range(8)]
    A1b = [Buf(f"A1_{h}") for h in range(8)]
    A2b = [Buf(f"A2_{h}") for h in range(8)]
    VBb = Buf("VB")
    ONb = Buf("ON")
    SQb = Buf("SQ")
    RSTDb = Buf("RSTD")
    SSb = [[Buf(f"S{i}_{h}") for h in range(8)] for i in range(2)]
    SBFb = [Buf("SBF0"), Buf("SBF1")]
    HSTb = Buf("HST")
    HSTnb = [Buf(f"HST{n}") for n in range(8)]
    Ynb = [Buf(f"Y{n}") for n in range(8)]
    CVSb = Buf("CVS")
    PARb = Buf("PAR")
    DERb = Buf("DER")
    CSTb = Buf("CST")
    Hb = [Buf(f"H{m}") for m in range(NF)]
    Tb = [Buf("T0"), Buf("T1")]
    FFNB = Hb + Tb
    G2b = [Buf("G0"), Buf("G1")]
    W2b = [Buf("W0"), Buf("W1")]
    E12b = [Buf("E10"), Buf("E11")]
    QD2b = [Buf("QD0"), Buf("QD1")]
    KD2b = [Buf("KD0"), Buf("KD1")]
    KPT2b = [[Buf(f"KPT{i}_{m}") for m in range(4)] for i in range(2)]
    OOb, RSOb, ATb, SQOb = Buf("OO"), Buf("RSO"), Buf("AT"), Buf("SQO")
    TILEB = G2b + W2b + E12b + QD2b + KD2b + KPT2b[0] + KPT2b[1] + [OOb, RSOb, ATb, SQOb]
    CBb = [Buf("CB0"), Buf("CB1")]
    Rb = [Buf(f"R{n}") for n in range(8)]
    IGb = [Buf(f"IG{n}") for n in range(8)]
    Ab = [Buf(f"AA{n}") for n in range(8)]
    Ub = [Buf("U0"), Buf("U1")]
    HNb = [Buf("HN0"), Buf("HN1")]
    BPHB = CBb + Rb + IGb + Ab + Ub + HNb
    PB = [Buf(f"PB{i}") for i in range(8)]
    WBb = [Buf(f"WB{u}") for u in range(NUNIT)]
    CGb = [Buf(f"CG{i}") for i in range(8)]
    YOUTb = Buf("YOUT")
    XLDb = Buf("XLD")
    STb = Buf("STIO")
    STHb = Buf("STH")
    STCb = Buf("STC")
    LDSb = Buf("LDS")
    LDHb = Buf("LDH")
    LDCb = Buf("LDC")

    fw.dma("pool", PAR[:].rearrange("p a b -> p (a b)"), par_d, owner=PARb, writes=[PARb])
    fw.dma("pool", CST[:], cst_d, owner=CSTb, writes=[CSTb])

    IDF = CST[:, 0:128]
    MASK = CST[:, 128:256]
    ONESF = CST[:, 256:384]

    def par(i, h=None):
        if h is None:
            return PAR[:, i, :]
        return PAR[:, i, h:h + 1]

    def der(i, h=None):
        if h is None:
            return DER[:, i, :]
        return DER[:, i, h:h + 1]
    D_LB, D_OML, D_NOML, D_CL, D_CL2, D_T0, D_T1, D_T2 = range(8)

    st = {"next_dma": 0, "next_use": 0, "sbf": 0, "s": 0}
    total_uses = [0]

    def emit_cast():
        for u in range(NUNIT):
            o, sz = UOFF[u], UNITS[u][2]
            cg = CGb[u % 8]
            fw.dma("pool", wb[:, o:o + sz], wall[:, o:o + sz], owner=cg, writes=[cg, WBb[u]])

    def emit_wload(s):
        u = s % NUNIT
        o, sz = UOFF[u], UNITS[u][2]
        slot = s % NSLOT
        fw.dma("sp", RING[:, slot, 0:sz], wb[:, o:o + sz], owner=RSb[slot],
               reads=[WBb[u]], writes=[RSb[slot]])

    def use_unit(kind):
        s = st["next_use"]
        assert UNITS[s % NUNIT][0] == kind, (UNITS[s % NUNIT], kind)
        while st["next_dma"] <= min(s + NSLOT - 1, total_uses[0] - 1):
            emit_wload(st["next_dma"])
            st["next_dma"] += 1
        st["next_use"] += 1
        slot = s % NSLOT
        return RING[:, slot, :], RSb[slot]

    def mm(out, lhsT, rhs, start, stop, reads, writes):
        fw.op("pe", lambda: PE.matmul(out, lhsT=lhsT, rhs=rhs, start=start, stop=stop,
                                      skip_group_check=True), reads, writes)

    def act(out, in_, func, reads, writes, **kw):
        fw.op("act", lambda: A.activation(out=out, in_=in_, func=func, **kw), reads, writes)

    SQkb = [Buf(f"SQ{k}") for k in range(NK)]
    nst = {"presq": False}

    def norm_sq(kc, N):
        act(SQ[:, kc, :N], X[:, kc, :N], AF.Square, [Xb[kc]], [SQkb[kc]])

    def norm_mm(kc, N):
        mm(PS[6][:, :N], ONESB[:], SQ[:, kc, :N], kc == 0, kc == NK - 1, [SQkb[kc], CSTb], [PB[6]])

    def rmsnorm(gidx, N, out_f32=None, out_bufs=None):
        if not nst["presq"]:
            for kc in range(NK):
                norm_sq(kc, N)
                norm_mm(kc, N)
        nst["presq"] = False
        act(RSTD[:, :N], PS[6][:, :N], AF.Sqrt, [PB[6]], [RSTDb], scale=1.0 / D, bias=EPS)
        fw.op("dve", lambda: V.reciprocal(out=RSTD[:, :N], in_=RSTD[:, :N]), [RSTDb], [RSTDb])
        for kc in range(NK):
            if out_f32 is None:
                o, ob = XN[:, kc, :N], [XNb[kc]]
            else:
                o, ob = out_f32[:, kc, :N], [out_bufs[kc]]
            fw.op("dve", lambda o=o, kc=kc: V.scalar_tensor_tensor(
                out=o, in0=X[:, kc, :N], scalar=par(gidx, kc), in1=RSTD[:, :N],
                op0=ALU.mult, op1=ALU.mult), [Xb[kc], RSTDb, PARb], ob)

    def ffn(fi, N):
        rmsnorm(P_FFN0 + fi, N)
        for j in range(11):
            u, ub = use_unit("ffn_in")
            uv = u.rearrange("p (s k c) -> p s k c", s=2, k=NK)
            for mm_ in range(2):
                m = 2 * j + mm_
                pa, pab = PS[m % 2], PB[m % 2]
                pb, pbb = PS[2 + m % 2], PB[2 + m % 2]
                for kc in range(NK):
                    mm(pa[:, :N], uv[:, 0, kc, mm_ * 128:(mm_ + 1) * 128], XN[:, kc, :N],
                       kc == 0, kc == NK - 1, [ub, XNb[kc]], [pab])
                for kc in range(NK):
                    mm(pb[:, :N], uv[:, 1, kc, mm_ * 128:(mm_ + 1) * 128], XN[:, kc, :N],
                       kc == 0, kc == NK - 1, [ub, XNb[kc]], [pbb])
                t, tb = Tt[m % 2], Tb[m % 2]
                act(t[:, :N], pa[:, :N], AF.Silu, [pab], [tb])
                fw.op("dve", lambda t=t, pb=pb, m=m: V.tensor_tensor(
                    out=Hh[:, m, :N], in0=t[:, :N], in1=pb[:, :N], op=ALU.mult), [tb, pbb], [Hb[m]])
        for mo in range(NK):
            u, ub = use_unit("ffn_out")
            uv = u[:, 0:NF * 128].rearrange("p (k c) -> p k c", k=NF)
            py, pyb = PS[4 + mo % 2], PB[4 + mo % 2]
            for kf in range(NF):
                mm(py[:, :N], uv[:, kf, :], Hh[:, kf, :N], kf == 0, kf == NF - 1, [ub, Hb[kf]], [pyb])
            if mo >= 1:
                norm_mm(mo - 1, N)
            fw.op("dve", lambda py=py, mo=mo: V.scalar_tensor_tensor(
                out=X[:, mo, :N], in0=py[:, :N], scalar=0.5, in1=X[:, mo, :N],
                op0=ALU.mult, op1=ALU.add), [pyb, Xb[mo]], [Xb[mo]])
            norm_sq(mo, N)
        norm_mm(NK - 1, N)
        nst["presq"] = True

    def proj_fm(kind, nblk, N, evac):
        u = ub = uv = None
        for blk in range(nblk):
            if blk % 4 == 0:
                u, ub = use_unit(kind)
                uv = u.rearrange("p (k c) -> p k c", k=NK)
            p, pbuf = PS[blk % 2], PB[blk % 2]
            for kc in range(NK):
                mm(p[:, :N], uv[:, kc, (blk % 4) * 128:(blk % 4 + 1) * 128], XN[:, kc, :N],
                   kc == 0, kc == NK - 1, [ub, XNb[kc]], [pbuf])
            evac(blk, p, pbuf)

    def out_proj(kind, N, src, src_bufs):
        u = ub = uv = None
        for mo in range(NK):
            if mo % 4 == 0:
                u, ub = use_unit(kind)
                uv = u.rearrange("p (k c) -> p k c", k=NK)
            py, pyb = PS[4 + mo % 2], PB[4 + mo % 2]
            for kh in range(NK):
                mm(py[:, :N], uv[:, kh, (mo % 4) * 128:(mo % 4 + 1) * 128], src[:, kh, :N],
                   kh == 0, kh == NK - 1, [ub] + src_bufs, [pyb])
            if mo >= 1:
                norm_mm(mo - 1, N)
            fw.op("dve", lambda py=py, mo=mo: V.tensor_tensor(
                out=X[:, mo, :N], in0=py[:, :N], in1=X[:, mo, :N], op=ALU.add), [pyb, Xb[mo]], [Xb[mo]])
            norm_sq(mo, N)
        norm_mm(NK - 1, N)
        nst["presq"] = True

    def mixer_a(N):
        TN = min(N, 128)
        ntile = N // TN
        nb = TN // 16
        rmsnorm(P_MIX0 + 0, N)
        Q, K, LF = A0, A1, A2
        Vv = VB[:].rearrange("p (t c) -> p t c", t=4)

        def ev_q(h, p, pbuf):
            act(Q[:, h, :N], p[:, :N], AF.Silu, [pbuf], [A0b[h]])
        import os
        kstop = os.environ.get("KSTOP", "")
        proj_fm("a_in", 8, N, ev_q)
        if kstop == "q":
            skip(8)
            return

        def ev_f(h, p, pbuf):
            act(K[:, h, :N], p[:, :N], AF.Sigmoid, [pbuf], [A1b[h]])
        proj_fm("a_in", 8, N, ev_f)
        for h in range(8):
            act(LF[:, h, :N], K[:, h, :N], AF.Ln, [A1b[h], DERb], [A2b[h]],
                scale=der(D_OML, h), bias=der(D_LB, h))
            fw.op("dve", lambda h=h: V.tensor_scalar(
                out=K[:, h, :N], in0=K[:, h, :N], scalar1=der(D_NOML, h), scalar2=der(D_OML, h),
                op0=ALU.mult, op1=ALU.add), [A1b[h], DERb], [A1b[h]])
        if kstop == "f":
            skip(6)
            return

        idx = 0
        for c2 in range(2):
            u, ub = use_unit("a_in")
            uv = u.rearrange("p (k c) -> p k c", k=NK)
            for tt in range(ntile):
                p, pbuf = PS[2 + idx % 2], PB[2 + idx % 2]
                for kc in range(NK):
                    mm(p[:TN, :], XN[:, kc, tt * TN:(tt + 1) * TN], uv[:, kc, :],
                       kc == 0, kc == NK - 1, [ub, XNb[kc]], [pbuf])
                if idx % 2 == 0:
                    act(Vv[:TN, tt, c2 * 512:(c2 + 1) * 512], p[:TN, :], AF.Copy, [pbuf], [VBb])
                else:
                    fw.op("dve", lambda p=p, tt=tt, c2=c2: V.tensor_copy(
                        out=Vv[:TN, tt, c2 * 512:(c2 + 1) * 512], in_=p[:TN, :]), [pbuf], [VBb])
                idx += 1

        if kstop == "v":
            skip(4)
            return
        fw.handoff(FFNB, TILEB)
        import os
        sub = int(os.environ.get("KSUB", "99"))
        GP = nc.gpsimd
        for i in range(2 if ntile > 1 else 1):
            fw.op("pool", lambda i=i: GP.memset(G2[i][:], 0.0), [], [G2b[i]])
        opb = [PB[1], PB[2]]

        def prep_stages(tt):
            pr = tt % 2
            G, W, E1, QD, KD, KPT = G2[pr], W2[pr], E12[pr], QD2[pr], KD2[pr], KPT2[pr]
            Gb, Wb, E1b, QDb, KDb, KPTb = G2b[pr], W2b[pr], E12b[pr], QD2b[pr], KD2b[pr], KPT2b[pr]
            tsl = slice(tt * TN, (tt + 1) * TN)

            def v4(ap):
                return ap.rearrange("p h (b j) -> p h b j", j=16)

            def s0():
                for h in range(8):
                    fw.op("dve", lambda h=h: V.tensor_tensor_scan(
                        out=G[:, h, 1:TN + 1], data0=ONESF[:, :TN], data1=LF[:, h, tsl], initial=0.0,
                        op0=ALU.mult, op1=ALU.add), [A2b[h], CSTb], [Gb])

            def s1():
                g1 = v4(G[:, :, 1:TN + 1])
                g0 = v4(G[:, :, 0:TN])[:, :, :, 0:1].to_broadcast([128, 8, nb, 16])
                fw.op("pool", lambda: GP.tensor_tensor(out=v4(W[:, :, :TN]), in0=g1, in1=g0, op=ALU.subtract),
                      [Gb], [Wb])

            def s2():
                act(E1[:, :, :TN], W[:, :, :TN], AF.Exp, [Wb], [E1b])
                act(W[:, :, :TN], W[:, :, :TN], AF.Exp, [Wb], [Wb], scale=-1.0)

            def s3():
                fw.op("pool", lambda: GP.tensor_tensor(out=QD[:, :, :TN], in0=Q[:, :, tsl], in1=E1[:, :, :TN],
                                                       op=ALU.mult), A0b + [E1b], [QDb])
                fw.op("pool", lambda: GP.tensor_tensor(out=W[:, :, :TN], in0=K[:, :, tsl], in1=W[:, :, :TN],
                                                       op=ALU.mult), A1b + [Wb], [Wb])

            def s4():
                act(KD[:, :, :TN], W[:, :, :TN], AF.Copy, [Wb], [KDb])
                e1l = v4(E1[:, :, :TN])[:, :, :, 15:16].to_broadcast([128, 8, nb, 16])
                fw.op("pool", lambda: GP.tensor_tensor(out=v4(W[:, :, :TN]), in0=v4(W[:, :, :TN]), in1=e1l,
                                                       op=ALU.mult), [Wb, E1b], [Wb])

            def s5(hf):
                def f():
                    bank = 7 if hf == 0 else 0
                    for h in range(4 * hf, 4 * hf + 4):
                        fw.op("pe", lambda h=h: PE.transpose(PS[bank][:TN, (h % 4) * 128:(h % 4 + 1) * 128],
                                                             W[:, h, :TN], IDF), [Wb, CSTb], [PB[bank]])
                    nmask = 4 if TN == 128 else 1
                    for m4 in range(nmask):
                        fw.op("dve", lambda m4=m4: V.tensor_scalar(
                            out=KPT[m4][:TN, hf * 512:(hf + 1) * 512], in0=PS[bank][:TN, :],
                            scalar1=PM[:TN, m4:m4 + 1], scalar2=None, op0=ALU.mult),
                            [PB[bank], CSTb], [KPTb[m4]])
                return f
            return [s0, s1, s2, s3, s4, s5(0), s5(1)]

        SLOTS = {0: [0, 1], 2: [2], 3: [3], 5: [4], 6: [5], 7: [6]}

        def front_scores(tt):
            pr = tt % 2
            QD, KD, QDb, KDb = QD2[pr], KD2[pr], QD2b[pr], KD2b[pr]
            for hg in range(2):
                sb = 0 if hg == 0 else 7
                for h in range(hg * 4, hg * 4 + 4):
                    mm(PS[sb][:TN, (h % 4) * TN:(h % 4 + 1) * TN], KD[:, h, :TN], QD[:, h, :TN],
                       h % 4 == 0, h % 4 == 3, [KDb, QDb], [PB[sb]])
                scv = PS[sb][:TN, 0:4 * TN].rearrange("p (h t) -> p h t", h=4)
                mk = MASK[:TN, :TN].unsqueeze(1).to_broadcast([TN, 4, TN])
                fw.op("dve", lambda scv=scv, mk=mk, hg=hg: V.tensor_tensor(
                    out=AT[:TN, hg * 4:hg * 4 + 4, :TN], in0=scv, in1=mk, op=ALU.mult), [PB[sb], CSTb], [ATb])

        def emit_u(tt, n):
            KPT, KPTb = KPT2[tt % 2], KPT2b[tt % 2]
            if TN == 128:
                r0, kr, kp, kpb = 64 * (n // 4), 64, KPT[n % 4], KPTb[n % 4]
            else:
                r0, kr, kp, kpb = 0, 16, KPT[0], KPTb[0]
            bk = 3 + 2 * (n % 2)
            for h in range(8):
                mm(PS[bk + h // 4][:, (h % 4) * 128:(h % 4 + 1) * 128],
                   kp[r0:r0 + kr, h * 128:(h + 1) * 128], Vv[r0:r0 + kr, tt, h * 128:(h + 1) * 128],
                   h % 4 == 0, h % 4 == 3, [kpb, VBb], [PB[bk + h // 4]])

        gst = {"u": None, "done": False}

        def gproj_head(h):
            if h % 4 == 0:
                gst["u"] = use_unit("a_in")
            u, ub = gst["u"]
            uv = u.rearrange("p (k c) -> p k c", k=NK)
            bank = 0 if h % 2 == 0 else 7
            for kc in range(NK):
                mm(PS[bank][:, :N], uv[:, kc, (h % 4) * 128:(h % 4 + 1) * 128], XN[:, kc, :N],
                   kc == 0, kc == NK - 1, [ub, XNb[kc]], [PB[bank]])
            act(LF[:, h, :N], PS[bank][:, :N], AF.Silu, [PB[bank]], [A2b[h]])

        def tile_body(tt, nxt):
            pr = tt % 2
            E1, QD, KD, KPT = E12[pr], QD2[pr], KD2[pr], KPT2[pr]
            E1b, QDb, KDb, KPTb = E12b[pr], QD2b[pr], KD2b[pr], KPT2b[pr]
            tsl = slice(tt * TN, (tt + 1) * TN)
            for hg in range(2):
                for h in range(hg * 4, hg * 4 + 4):
                    mm(PS[1 + hg][:, (h % 4) * TN:(h % 4 + 1) * TN], Vv[:TN, tt, h * 128:(h + 1) * 128],
                       AT[:TN, h, :TN], h % 4 == 0, False, [VBb, ATb], [opb[hg]])

            if tt == 0:
                emit_u(tt, 0)
            for n in range(nb):
                if n + 1 < nb:
                    emit_u(tt, n + 1)
                cs = st["sbf"]
                for h in range(8):
                    mm(PS[1 + h // 4][:, (h % 4) * TN + 16 * n:(h % 4) * TN + 16 * n + 16],
                       SBF[cs][:, h, :], QD[:, h, 16 * n:16 * n + 16], False, False,
                       [SBFb[cs], QDb], [opb[h // 4]])
                bk = 3 + 2 * (n % 2)
                si = st["s"]
                so = 1 - si
                for h in range(8):
                    fw.op("dve", lambda h=h, bk=bk, n=n, si=si, so=so: V.scalar_tensor_tensor(
                        out=SS[so][:, h, :], in0=SS[si][:, h, :], scalar=E1[:, h, 16 * n + 15:16 * n + 16],
                        in1=PS[bk + h // 4][:, (h % 4) * 128:(h % 4 + 1) * 128],
                        op0=ALU.mult, op1=ALU.add), [SSb[si][h], E1b, PB[bk + h // 4]], [SSb[so][h]])
                st["s"] = so
                ns = 1 - cs
                act(SBF[ns][:], SS[so][:], AF.Copy, SSb[so], [SBFb[ns]])
                st["sbf"] = ns
                if nxt:
                    for si_ in SLOTS.get(n, []):
                        nxt[si_]()
                if tt == ntile - 1 and nb == 8 and sub >= 7:
                    gproj_head(n)
                    gst["done"] = True
            for hg in range(2):
                pv = PS[1 + hg][:, 0:4 * TN].rearrange("p (h t) -> p h t", h=4)
                act(OO[:, hg * 4:hg * 4 + 4, :TN], pv, AF.Copy, [opb[hg]], [OOb])
            if tt + 1 < ntile:
                front_scores(tt + 1)
                emit_u(tt + 1, 0)
            fw.op("pool", lambda: GP.tensor_tensor(out=SQO[:, :, :TN], in0=OO[:, :, :TN], in1=OO[:, :, :TN],
                                                   op=ALU.mult), [OOb], [SQOb])
            for h in range(8):
                mm(PS[5 + h // 4][:, (h % 4) * TN:(h % 4 + 1) * TN], ONESB[:], SQO[:, h, :TN],
                   h % 4 == 0, h % 4 == 3, [SQOb, CSTb], [PB[5 + h // 4]])
            for hg in range(2):
                pv = PS[5 + hg][:, 0:4 * TN].rearrange("p (h t) -> p h t", h=4)
                act(RSO[:, hg * 4:hg * 4 + 4, :TN], pv, AF.Ln, [PB[5 + hg]], [RSOb], scale=1.0 / 128, bias=EPS)
            act(RSO[:, :, :TN], RSO[:, :, :TN], AF.Exp, [RSOb], [RSOb], scale=-0.5)
            fw.op("pool", lambda: GP.tensor_tensor(out=Q[:, :, tsl], in0=OO[:, :, :TN], in1=RSO[:, :, :TN],
                                                   op=ALU.mult), [OOb, RSOb], A0b)

        for f in prep_stages(0):
            f()
        front_scores(0)
        for tt in range(ntile):
            nxt = prep_stages(tt + 1) if tt + 1 < ntile else []
            tile_body(tt, nxt)

        fw.handoff(TILEB, FFNB)
        if sub < 7:
            skip(4)
            return

        def ev_g2(h):
            fw.op("dve", lambda: V.scalar_tensor_tensor(
                out=ON[:, h, :N], in0=Q[:, h, :N], scalar=par(P_ONORM, h), in1=LF[:, h, :N],
                op0=ALU.mult, op1=ALU.mult), [A0b[h], A2b[h], PARb], A1b)

        def ev_g(h, p, pbuf):
            act(LF[:, h, :N], p[:, :N], AF.Silu, [pbuf], [A2b[h]])
            fw.op("dve", lambda: V.scalar_tensor_tensor(
                out=ON[:, h, :N], in0=Q[:, h, :N], scalar=par(P_ONORM, h), in1=LF[:, h, :N],
                op0=ALU.mult, op1=ALU.mult), [A0b[h], A2b[h], PARb], A1b)
        if gst["done"]:
            for h in range(8):
                ev_g2(h)
        else:
            proj_fm("a_in", 8, N, ev_g)
        out_proj("a_out", N, ON, A1b)

    def mixer_b(N):
        rmsnorm(P_MIX0 + 1, N)
        XBr, GG, CV = A0, A1, A2
        Y = VB[:].rearrange("p (k t) -> p k t", k=8)

        fw.op("dve", lambda: V.tensor_copy(out=XBr[:, :, 0:3], in_=CVS[:]), [CVSb], A0b)

        def ev_x(n, p, pbuf):
            act(XBr[:, n, 3:3 + N], p[:, :N], AF.Copy, [pbuf], [A0b[n]])
        proj_fm("b_in", 8, N, ev_x)

        def ev_g(n, p, pbuf):
            act(GG[:, n, :N], p[:, :N], AF.Gelu, [pbuf], [A1b[n]])
        proj_fm("b_in", 8, N, ev_g)

        fw.handoff(FFNB, BPHB)
        u, ub = use_unit("b_gate")
        uv = u[:, 0:2048].rearrange("p (s n d) -> p s n d", s=2, n=8)
        fw.handoff([VBb, HSTb], Ynb + HSTnb)

        def stage_a(n):
            i2 = n % 2
            fw.op("pool", lambda n=n: nc.gpsimd.tensor_scalar(
                out=CV[:, n, :N], in0=XBr[:, n, 0:N], scalar1=par(P_CW0 + 0, n), scalar2=par(P_CB, n),
                op0=ALU.mult, op1=ALU.add), [A0b[n], PARb], [A2b[n]])
            for j in range(1, 4):
                fw.op("dve", lambda n=n, j=j: V.scalar_tensor_tensor(
                    out=CV[:, n, :N], in0=XBr[:, n, j:j + N], scalar=par(P_CW0 + j, n), in1=CV[:, n, :N],
                    op0=ALU.mult, op1=ALU.add), [A0b[n], A2b[n], PARb], [A2b[n]])
            act(CBt[i2][:, :N], CV[:, n, :N], AF.Copy, [A2b[n]], [CBb[i2]])
            pr, prb = PS[2 + 2 * i2], PB[2 + 2 * i2]
            pi, pib = PS[3 + 2 * i2], PB[3 + 2 * i2]
            mm(pr[:, :N], uv[:, 0, n, :], CBt[i2][:, :N], True, True, [ub, CBb[i2]], [prb])
            mm(pi[:, :N], uv[:, 1, n, :], CBt[i2][:, :N], True, True, [ub, CBb[i2]], [pib])
            act(RA[:, n, :N], pr[:, :N], AF.Sigmoid, [prb, PARb], [Rb[n]], bias=par(P_BA, n))
            act(IGA[:, n, :N], pi[:, :N], AF.Sigmoid, [pib, PARb], [IGb[n]], bias=par(P_BX, n))

        def stage_b(n):
            i2 = n % 2
            fw.op("dve", lambda i2=i2, n=n: V.tensor_tensor(
                out=Ut[i2][:, :N], in0=IGA[:, n, :N], in1=CV[:, n, :N], op=ALU.mult), [IGb[n], A2b[n]], [Ub[i2]])
            fw.op("dve", lambda i2=i2, n=n: V.tensor_tensor(
                out=Ut[i2][:, :N], in0=Ut[i2][:, :N], in1=RA[:, n, :N], op=ALU.mult), [Ub[i2], Rb[n]], [Ub[i2]])
            fw.op("dve", lambda i2=i2, n=n: V.tensor_tensor_scan(
                out=HNt[i2][:, :N], data0=AA[:, n, :N], data1=Ut[i2][:, :N], initial=HST[:, n:n + 1],
                op0=ALU.mult, op1=ALU.add), [Ab[n], Ub[i2], HSTnb[n]], [HNb[i2]])
            fw.op("dve", lambda i2=i2, n=n: V.tensor_copy(out=HST[:, n:n + 1], in_=HNt[i2][:, N - 1:N]),
                  [HNb[i2]], [HSTnb[n]])
            fw.op("dve", lambda i2=i2, n=n: V.tensor_tensor(
                out=Y[:, n, :N], in0=HNt[i2][:, :N], in1=GG[:, n, :N], op=ALU.mult), [HNb[i2], A1b[n]], [Ynb[n]])

        for n in range(8):
            stage_a(n)
        for half in range(2):
            ns_ = range(4 * half, 4 * half + 4)
            for n in ns_:
                act(AA[:, n, :N], RA[:, n, :N], AF.Exp, [Rb[n], DERb], [Ab[n]], scale=der(D_CL, n))
                act(RA[:, n, :N], RA[:, n, :N], AF.Exp, [Rb[n], DERb], [Rb[n]], scale=der(D_CL2, n))
            for n in ns_:
                act(RA[:, n, :N], RA[:, n, :N], AF.Sqrt, [Rb[n]], [Rb[n]], scale=-1.0, bias=1.0)
        for n in range(8):
            stage_b(n)
        fw.op("dve", lambda: V.tensor_copy(out=CVS[:], in_=XBr[:, :, N:N + 3]), A0b, [CVSb])
        fw.handoff(BPHB, FFNB)
        out_proj("b_out", N, Y, Ynb)
        fw.handoff(Ynb + HSTnb, [VBb, HSTb])

    LEVELS = ["io", "ffn0", "mixa", "ffn1", "ffn2", "mixb", "ffn3", "all"]
    lvl = LEVELS.index(dbg) if dbg else len(LEVELS) - 1

    def skip(k):
        st["next_use"] += k

    def prefetch(src, N):
        fw.dma("pool", A0[:, :, :N], src, owner=XLDb, writes=A0b)

    def chunk(src, dst, N, first, nxt):
        if first:
            prefetch(src, N)
            emit_cast()
        for kc in range(NK):
            act(X[:, kc, :N], A0[:, kc, :N], AF.Copy, [A0b[kc]], [Xb[kc]])
        nst["presq"] = False
        if lvl >= 1:
            ffn(0, N)
        else:
            skip(19)
        if lvl >= 2:
            mixer_a(N)
        else:
            skip(10)
        if lvl >= 3:
            ffn(1, N)
        else:
            skip(19)
        if lvl >= 4:
            ffn(2, N)
        else:
            skip(19)
        if lvl >= 5:
            mixer_b(N)
        else:
            skip(7)
        if nxt is not None:
            prefetch(nxt[0], nxt[1])
        if lvl >= 6:
            ffn(3, N)
        else:
            skip(19)
        if lvl >= 7:
            rmsnorm(P_FIN, N, out_f32=A1, out_bufs=A1b)
        else:
            fw.op("dve", lambda: V.tensor_copy(out=A1[:, :, :N], in_=X[:, :, :N]), Xb, A1b)
        fw.dma("pool", dst, A1[:, :, :N], owner=YOUTb, reads=A1b, writes=[YOUTb])

    def write_states(idx):
        fw.dma("pool", sout[idx], SS[st["s"]][:], owner=STb, reads=SSb[st["s"]], writes=[STb])
        fw.dma("pool", hout[:, idx, :], HST[:], owner=STHb, reads=[HSTb], writes=[STHb])
        fw.dma("pool", cout[:, idx, :, :], CVS[:], owner=STCb, reads=[CVSb], writes=[STCb])

    chunks = []
    pos = 0
    while pos < T:
        n = min(512, T - pos)
        chunks.append((pos, n))
        pos += n
    total_uses[0] = NUNIT * (len(chunks) + NS)

    pm_d = nc.dram_tensor("pm", [128, 4], F32, kind="ExternalInput").ap()
    fw.dma("pool", PM[:], pm_d, owner=CSTb, writes=[CSTb])
    r = [PARb]
    w = [DERb]
    act(der(D_T0), par(P_LB0), AF.Exp, r, w)
    act(der(D_T1), par(P_LB1), AF.Exp, r, w)
    fw.op("dve", lambda: V.tensor_tensor(out=der(D_T2), in0=der(D_T0), in1=der(D_T1), op=ALU.add), [DERb], w)
    fw.op("dve", lambda: V.reciprocal(out=der(D_T2), in_=der(D_T2)), [DERb], w)
    fw.op("dve", lambda: V.tensor_tensor(out=der(D_LB), in0=der(D_T0), in1=der(D_T2), op=ALU.mult), [DERb], w)
    fw.op("dve", lambda: V.tensor_tensor(out=der(D_OML), in0=der(D_T1), in1=der(D_T2), op=ALU.mult), [DERb], w)
    fw.op("dve", lambda: V.tensor_scalar(out=der(D_NOML), in0=der(D_OML), scalar1=-1.0, scalar2=None, op0=ALU.mult), [DERb], w)
    act(der(D_T0), par(P_LAM), AF.Exp, r + [DERb], w, scale=-1.0)
    act(der(D_T0), der(D_T0), AF.Ln, [DERb], w, bias=1.0)
    fw.op("dve", lambda: V.tensor_scalar(out=der(D_CL), in0=der(D_T0), scalar1=-8.0, scalar2=None, op0=ALU.mult), [DERb], w)
    fw.op("dve", lambda: V.tensor_scalar(out=der(D_CL2), in0=der(D_T0), scalar1=-16.0, scalar2=None, op0=ALU.mult), [DERb], w)
    fw.op("dve", lambda: V.tensor_copy(out=IDB[:], in_=IDF), [CSTb], [CSTb])
    fw.op("dve", lambda: V.tensor_copy(out=ONESB[:], in_=ONESF), [CSTb], [CSTb])

    fw.op("dve", lambda: V.memset(SS[0][:], 0.0), [], SSb[0])
    fw.op("dve", lambda: V.memset(SBF[0][:], 0.0), [], [SBFb[0]])
    fw.op("dve", lambda: V.memset(HST[:], 0.0), [], [HSTb])
    fw.op("dve", lambda: V.memset(CVS[:], 0.0), [], [CVSb])
    st["sbf"] = 0
    jobs = [(xseq[:, :, pos:pos + n], yseq[:, :, pos:pos + n], n) for (pos, n) in chunks]
    for s in range(NS):
        jobs.append((xsmp[:, :, s * NSMP:(s + 1) * NSMP], ysmp[:, :, s * NSMP:(s + 1) * NSMP], NSMP))
    npr = len(chunks)
    for ji, (src, dst, n) in enumerate(jobs):
        nxt = (jobs[ji + 1][0], jobs[ji + 1][2]) if ji + 1 < len(jobs) else None
        if ji >= npr:
            sidx = ji - npr
            fw.dma("pool", SS[st["s"]][:], s0_d[sidx], owner=LDSb, reads=[], writes=SSb[st["s"]])
            fw.dma("pool", HST[:], h0_d[:, sidx, :], owner=LDHb, writes=[HSTb])
            fw.dma("pool", CVS[:], c0_d[:, sidx, :, :], owner=LDCb, writes=[CVSb])
            cs = st["sbf"]
            act(SBF[cs][:], SS[st["s"]][:], AF.Copy, SSb[st["s"]], [SBFb[cs]])
        chunk(src, dst, n, ji == 0, nxt)
        if ji == npr - 1:
            write_states(0)
        elif ji >= npr:
            write_states(1 + ji - npr)
    fw.final_wait("pool", [YOUTb, STb, STHb, STCb])
    assert st["next_use"] == total_uses[0], (st["next_use"], total_uses[0])
    return nc, fw


def fm(v):
    return np.ascontiguousarray(np.asarray(v, np.float32).reshape(8, 128).T)


def build_wall(ffn_w_in, ffn_w_out, a_w_in, a_w_out, b_w_in, b_wa, b_wx, b_w_out):
    wall = np.empty((128, WTOT), np.float32)

    def std(wm, u):
        blk = wm[:, u * 512:(u + 1) * 512].reshape(NK, 128, 512)
        return blk.transpose(1, 0, 2).reshape(128, -1)
    for ui, (kind, args, sz) in enumerate(UNITS):
        o = UOFF[ui]
        if kind == "ffn_in":
            fi, j = args
            wm = ffn_w_in[fi // 2, fi % 2].reshape(NK, 128, 2, DFF)[:, :, :, j * 256:(j + 1) * 256]
            blk = wm.transpose(1, 2, 0, 3).reshape(128, -1)
        elif kind == "ffn_out":
            fi, mo = args
            wm = ffn_w_out[fi // 2, fi % 2].reshape(NF, 128, 8, 128)[:, :, mo, :]
            blk = wm.transpose(1, 0, 2).reshape(128, -1)
        elif kind == "a_in":
            blk = std(a_w_in[0], args[0])
        elif kind == "a_out":
            blk = std(a_w_out[0], args[0])
        elif kind == "b_in":
            blk = std(b_w_in[0], args[0])
        elif kind == "b_out":
            blk = std(b_w_out[0], args[0])
        elif kind == "b_gate":
            blk = np.stack([b_wa[0], b_wx[0]], 0).transpose(2, 0, 1, 3).reshape(128, -1)
        assert blk.shape[1] == sz, (kind, blk.shape, sz)
        wall[:, o:o + sz] = blk
    return wall


def build_consts():
    cst = np.zeros((128, 384), np.float32)
    cst[:, 0:128] = np.eye(128, dtype=np.float32)
    s = np.arange(128)[:, None]
    t = np.arange(128)[None, :]
    cst[:, 128:256] = ((s // 16 == t // 16) & (s <= t)).astype(np.float32)
    cst[:, 256:384] = 1.0
    pm = np.zeros((128, 4), np.float32)
    blk = np.arange(128) // 16
    for m4 in range(4):
        pm[:, m4] = (blk % 4 == m4)
    return cst, pm


_CACHE = {}


def kernel(x_prompt, x_sample, state_hgrn, state_rglru, state_conv, meta_tokens,
           ffn_norm, ffn_w_in, ffn_w_out, mix_norm, a_w_in, a_lb, a_onorm, a_w_out,
           b_w_in, b_conv_w, b_conv_b, b_wa, b_ba, b_wx, b_bx, b_lambda, b_w_out, final_norm,
           _ncores=8):
    f32 = np.float32
    x_prompt = np.asarray(x_prompt, f32)
    x_sample = np.asarray(x_sample, f32)
    B, SEQ, _ = x_prompt.shape
    DB, DS, _ = x_sample.shape
    T = SEQ + NMETA
    NS = DB // _ncores
    import os
    dbg = os.environ.get("KDBG") or None
    key = (T, NS, DS, dbg)
    if key not in _CACHE:
        _CACHE[key] = build_program(T, NS, DS, dbg)
    nc, fw = _CACHE[key]

    wall = build_wall(np.asarray(ffn_w_in, f32), np.asarray(ffn_w_out, f32), np.asarray(a_w_in, f32),
                      np.asarray(a_w_out, f32), np.asarray(b_w_in, f32), np.asarray(b_wa, f32),
                      np.asarray(b_wx, f32), np.asarray(b_w_out, f32))
    pv = [None] * NPAR
    fn = np.asarray(ffn_norm, f32)
    for l in range(2):
        for i in range(2):
            pv[P_FFN0 + l * 2 + i] = fm(fn[l, i])
    mn = np.asarray(mix_norm, f32)
    pv[P_MIX0], pv[P_MIX0 + 1] = fm(mn[0]), fm(mn[1])
    pv[P_FIN] = fm(final_norm)
    alb = np.asarray(a_lb, f32)
    pv[P_LB0], pv[P_LB1] = fm(alb[0]), fm(alb[1])
    pv[P_ONORM] = fm(np.asarray(a_onorm, f32)[0])
    cw = np.asarray(b_conv_w, f32)[0]
    for j in range(4):
        pv[P_CW0 + j] = fm(cw[j])
    pv[P_CB] = fm(np.asarray(b_conv_b, f32)[0])
    pv[P_BA] = fm(np.asarray(b_ba, f32)[0])
    pv[P_BX] = fm(np.asarray(b_bx, f32)[0])
    pv[P_LAM] = fm(np.asarray(b_lambda, f32)[0])
    par = np.ascontiguousarray(np.stack(pv, 1).reshape(128, NPAR * 8))
    cst, pm = build_consts()

    meta = np.asarray(meta_tokens, f32)
    sh = np.asarray(state_hgrn, f32)[0]
    sr = np.asarray(state_rglru, f32)[0]
    sc = np.asarray(state_conv, f32)[0]

    owner = {0: 0, 1: 1, 4: 2, 5: 3} if _ncores == 8 else {i: i for i in range(min(B, _ncores))}
    in_maps = []
    for c in range(_ncores):
        if c in owner:
            xs = np.concatenate([meta, x_prompt[owner[c]]], 0)
            xseq = np.ascontiguousarray(xs.reshape(T, NK, 128).transpose(2, 1, 0))
        else:
            xseq = np.zeros((128, NK, T), f32)
        ss = slice(c * NS, (c + 1) * NS)
        xm = x_sample[ss].reshape(NS * DS, NK, 128).transpose(2, 1, 0)
        in_maps.append({
            "wall": wall, "par": par, "cst": cst, "pm": pm,
            "xseq": xseq, "xsmp": np.ascontiguousarray(xm),
            "s0": np.ascontiguousarray(sh[ss].transpose(0, 2, 1, 3)),
            "h0": np.ascontiguousarray(sr[ss].reshape(NS, 8, 128).transpose(2, 0, 1)),
            "c0": np.ascontiguousarray(sc[ss].reshape(NS, 3, 8, 128).transpose(3, 0, 2, 1)),
        })
    res = run_bass_kernel_spmd(nc, in_maps, core_ids=list(range(_ncores)))
    R = res.results

    y_prompt = np.empty((B, SEQ, D), f32)
    y_sample = np.empty((DB, DS, D), f32)
    hg_p = np.empty((1, B, 8, 128, 128), f32)
    hg_s = np.empty((1, DB, 8, 128, 128), f32)
    rg_p = np.empty((1, B, D), f32)
    rg_s = np.empty((1, DB, D), f32)
    cv_p = np.empty((1, B, 3, D), f32)
    cv_s = np.empty((1, DB, 3, D), f32)
    for c in range(_ncores):
        r = R[c]
        so, ho, co = r["sout"], r["hout"], r["cout"]
        if c in owner:
            b = owner[c]
            y_prompt[b] = r["yseq"].transpose(2, 1, 0).reshape(T, D)[NMETA:]
            hg_p[0, b] = so[0].transpose(1, 0, 2)
            rg_p[0, b] = ho[:, 0, :].T.reshape(D)
            cv_p[0, b] = co[:, 0].transpose(2, 1, 0).reshape(3, D)
        ym = r["ysmp"].transpose(2, 1, 0).reshape(NS, DS, D)
        for s in range(NS):
            g = c * NS + s
            y_sample[g] = ym[s]
            hg_s[0, g] = so[1 + s].transpose(1, 0, 2)
            rg_s[0, g] = ho[:, 1 + s, :].T.reshape(D)
            cv_s[0, g] = co[:, 1 + s].transpose(2, 1, 0).reshape(3, D)
    return (y_prompt, y_sample, hg_p, hg_s, rg_p, rg_s, cv_p, cv_s)
```

```python
import numpy as np
import concourse.bass as bass
import concourse.mybir as mybir
from concourse.bass_utils import run_bass_kernel_spmd

F32 = mybir.dt.float32
BF16 = mybir.dt.bfloat16
AF = mybir.ActivationFunctionType
ALU = mybir.AluOpType

D = 1024
NK = 8
DFF = 2816
NF = 22
NMETA = 16
EPS = 1e-6
NSLOT = 5
USZ = 4096


class Buf:
    __slots__ = ("name", "w", "r", "dsem", "dcnt")

    def __init__(self, name):
        self.name = name
        self.w = None
        self.r = {}
        self.dsem = None
        self.dcnt = 0


class FW:
    def __init__(self, nc):
        self.nc = nc
        self.eng = {"pe": nc.tensor, "act": nc.scalar, "dve": nc.vector,
                    "pool": nc.gpsimd, "sp": nc.sync}
        self.sem = {}
        self.cnt = {}
        self.waited = {}
        self.semobj = {}
        for e in self.eng:
            self.sem[e] = nc.alloc_semaphore(name="prog_" + e)
            self.semobj[e] = self.sem[e]
            self.cnt[e] = 0
            self.waited[e] = {}
        self.ninst = 0

    def _need(self, reads, writes):
        need = {}

        def add(clk):
            if clk is None:
                return
            k, v = clk
            if v > need.get(k, 0):
                need[k] = v
        for b in reads:
            add(b.w)
        for b in writes:
            add(b.w)
            for k, v in b.r.items():
                add((k, v))
        return need

    def _emit_waits(self, e, need, skip_self=False):
        eng = self.eng[e]
        wd = self.waited[e]
        for k, v in need.items():
            if skip_self and k == e:
                continue
            if wd.get(k, 0) >= v:
                continue
            eng.wait_ge(self.semobj[k], v)
            wd[k] = v

    def op(self, e, fn, reads=(), writes=()):
        need = self._need(reads, writes)
        self._emit_waits(e, need, skip_self=(e == "pe"))
        ins = fn()
        self.cnt[e] += 1
        c = self.cnt[e]
        ins.then_inc(self.sem[e], 1)
        for b in writes:
            b.w = (e, c)
            b.r = {}
        for b in reads:
            if b not in writes:
                b.r[e] = c
        self.ninst += 1
        return ins

    def dma(self, q, out_ap, in_ap, owner, reads=(), writes=(), **kw):
        need = self._need(reads, writes)
        self._emit_waits(q, need)
        k = "d_" + owner.name
        if owner.dsem is None:
            owner.dsem = self.nc.alloc_semaphore(name=k)
            self.semobj[k] = owner.dsem
        owner.dcnt += 1
        v = 16 * owner.dcnt
        ins = self.eng[q].dma_start(out=out_ap, in_=in_ap, **kw)
        ins.then_inc(owner.dsem, 16)
        for b in writes:
            b.w = (k, v)
            b.r = {}
        for b in reads:
            if b not in writes:
                b.r[k] = v
        self.ninst += 1
        return ins

    def handoff(self, olds, news):
        merged = {}
        for b in olds:
            if b.w is not None and b.w[1] > merged.get(b.w[0], 0):
                merged[b.w[0]] = b.w[1]
            for k, v in b.r.items():
                if v > merged.get(k, 0):
                    merged[k] = v
        for b in news:
            b.w = None
            b.r = dict(merged)

    def final_wait(self, e, bufs):
        self._emit_waits(e, self._need([], bufs))


def unit_table():
    units = []

    def ffn(fi):
        for j in range(11):
            units.append(("ffn_in", (fi, j), 4096))
        for mo in range(8):
            units.append(("ffn_out", (fi, mo), NF * 128))
    ffn(0)
    for u in range(8):
        units.append(("a_in", (u,), 4096))
    for u in range(2):
        units.append(("a_out", (u,), 4096))
    ffn(1)
    ffn(2)
    for u in range(4):
        units.append(("b_in", (u,), 4096))
    units.append(("b_gate", (), 2048))
    for u in range(2):
        units.append(("b_out", (u,), 4096))
    ffn(3)
    offs = []
    o = 0
    for _, _, sz in units:
        offs.append(o)
        o += sz
    return units, offs, o


UNITS, UOFF, WTOT = unit_table()
NUNIT = len(UNITS)

NPAR = 18
(P_FFN0, P_MIX0, P_FIN, P_LB0, P_LB1, P_ONORM, P_CW0, P_CB, P_BA, P_BX, P_LAM) = (0, 4, 6, 7, 8, 9, 10, 14, 15, 16, 17)


def build_program(T, NS=2, NSMP=16, dbg=None):
    nc = bass.Bass("TRN2", target_bir_lowering=False)
    fw = FW(nc)
    V = nc.vector
    A = nc.scalar
    PE = nc.tensor

    wall = nc.dram_tensor("wall", [128, WTOT], F32, kind="ExternalInput").ap()
    par_d = nc.dram_tensor("par", [128, NPAR * 8], F32, kind="ExternalInput").ap()
    cst_d = nc.dram_tensor("cst", [128, 384], F32, kind="ExternalInput").ap()
    xseq = nc.dram_tensor("xseq", [128, NK, T], F32, kind="ExternalInput").ap()
    xsmp = nc.dram_tensor("xsmp", [128, NK, NS * NSMP], F32, kind="ExternalInput").ap()
    s0_d = nc.dram_tensor("s0", [NS, 128, 8, 128], F32, kind="ExternalInput").ap()
    h0_d = nc.dram_tensor("h0", [128, NS, 8], F32, kind="ExternalInput").ap()
    c0_d = nc.dram_tensor("c0", [128, NS, 8, 3], F32, kind="ExternalInput").ap()
    yseq = nc.dram_tensor("yseq", [128, NK, T], F32, kind="ExternalOutput").ap()
    ysmp = nc.dram_tensor("ysmp", [128, NK, NS * NSMP], F32, kind="ExternalOutput").ap()
    sout = nc.dram_tensor("sout", [1 + NS, 128, 8, 128], F32, kind="ExternalOutput").ap()
    hout = nc.dram_tensor("hout", [128, 1 + NS, 8], F32, kind="ExternalOutput").ap()
    cout = nc.dram_tensor("cout", [128, 1 + NS, 8, 3], F32, kind="ExternalOutput").ap()
    wb = nc.dram_tensor("wb", [128, WTOT], BF16, kind="Internal").ap()

    cur = [16512]

    def alloc(name, shape, dt, at=None):
        nbytes = int(np.prod(shape[1:])) * (4 if dt == F32 else 2)
        if at is None:
            off = cur[0]
            cur[0] += (nbytes + 31) // 32 * 32
        else:
            off = at
        return nc.alloc_sbuf_tensor_at(name, list(shape), dt, offset=off)

    X = alloc("X", [128, NK, 512], F32)
    XN = alloc("XN", [128, NK, 512], BF16)
    RING = alloc("RING", [128, NSLOT, USZ], BF16)
    A0 = alloc("A0", [128, 8, 520], F32)
    a1_off = cur[0]
    A1 = alloc("A1", [128, 8, 520], F32)
    ON = alloc("ON", [128, 8, 512], BF16, at=a1_off)
    A2 = alloc("A2", [128, 8, 520], F32)
    VB = alloc("VB", [128, 4096], BF16)
    SQ = alloc("SQ", [128, NK, 512], BF16)
    RSTD = alloc("RSTD", [128, 512], F32)
    SS = [alloc(f"S{i}", [128, 8, 128], F32) for i in range(2)]
    SBF = [alloc(f"SBF{i}", [128, 8, 128], BF16) for i in range(2)]
    HST = alloc("HST", [128, 8], F32)
    CVS = alloc("CVS", [128, 8, 3], F32)
    PAR = alloc("PAR", [128, NPAR, 8], F32)
    DER = alloc("DER", [128, 8, 8], F32)
    CST = alloc("CST", [128, 384], F32)
    IDB = alloc("IDB", [128, 128], BF16)
    ONESB = alloc("ONESB", [128, 128], BF16)
    PM = alloc("PM", [128, 4], F32)
    su0 = cur[0]
    Hh = alloc("H", [128, NF, 512], BF16)
    Tt = [alloc(f"T{i}", [128, 512], F32) for i in range(2)]
    su_ffn_end = cur[0]
    cur[0] = su0
    G2 = [alloc(f"G{i}", [128, 8, 129], F32) for i in range(2)]
    W2 = [alloc(f"W{i}", [128, 8, 128], F32) for i in range(2)]
    E12 = [alloc(f"E1{i}", [128, 8, 128], F32) for i in range(2)]
    QD2 = [alloc(f"QD{i}", [128, 8, 128], BF16) for i in range(2)]
    KD2 = [alloc(f"KD{i}", [128, 8, 128], BF16) for i in range(2)]
    KPT2 = [[alloc(f"KPT{i}_{m}", [128, 1024], BF16) for m in range(4)] for i in range(2)]
    OO = alloc("OO", [128, 8, 128], F32)
    RSO = alloc("RSO", [128, 8, 128], F32)
    AT = alloc("AT", [128, 8, 128], BF16)
    SQO = alloc("SQO", [128, 8, 128], BF16)
    su_a_end = cur[0]
    cur[0] = su0
    CBt = [alloc(f"CB{i}", [128, 512], BF16) for i in range(2)]
    RA = alloc("RA", [128, 8, 512], F32)
    IGA = alloc("IGA", [128, 8, 512], F32)
    AA = alloc("AA", [128, 8, 512], F32)
    Ut = [alloc(f"U{i}", [128, 512], F32) for i in range(2)]
    HNt = [alloc(f"HN{i}", [128, 512], F32) for i in range(2)]
    su_b_end = cur[0]
    cur[0] = max(su_ffn_end, su_a_end, su_b_end)
    assert cur[0] <= 229344, cur[0]

    PS = [nc.alloc_psum_tensor(f"ps{i}", [128, 512], F32) for i in range(8)]

    Xb = [Buf(f"X{k}") for k in range(NK)]
    XNb = [Buf(f"XN{k}") for k in range(NK)]
    RSb = [Buf(f"RS{i}") for i in range(NSLOT)]
    A0b = [Buf(f"A0_{h}") for h in range(8)]
    A1b = [Buf(f"A1_{h}") for h in range(8)]
    A2b = [Buf(f"A2_{h}") for h in range(8)]
    VBb = Buf("VB")
    ONb = Buf("ON")
    SQb = Buf("SQ")
    RSTDb = Buf("RSTD")
    SSb = [[Buf(f"S{i}_{h}") for h in range(8)] for i in range(2)]
    SBFb = [Buf("SBF0"), Buf("SBF1")]
    HSTb = Buf("HST")
    HSTnb = [Buf(f"HST{n}") for n in range(8)]
    Ynb = [Buf(f"Y{n}") for n in range(8)]
    CVSb = Buf("CVS")
    PARb = Buf("PAR")
    DERb = Buf("DER")
    CSTb = Buf("CST")
    Hb = [Buf(f"H{m}") for m in range(NF)]
    Tb = [Buf("T0"), Buf("T1")]
    FFNB = Hb + Tb
    G2b = [Buf("G0"), Buf("G1")]
    W2b = [Buf("W0"), Buf("W1")]
    E12b = [Buf("E10"), Buf("E11")]
    QD2b = [Buf("QD0"), Buf("QD1")]
    KD2b = [Buf("KD0"), Buf("KD1")]
    KPT2b = [[Buf(f"KPT{i}_{m}") for m in range(4)] for i in range(2)]
    OOb, RSOb, ATb, SQOb = Buf("OO"), Buf("RSO"), Buf("AT"), Buf("SQO")
    TILEB = G2b + W2b + E12b + QD2b + KD2b + KPT2b[0] + KPT2b[1] + [OOb, RSOb, ATb, SQOb]
    CBb = [Buf("CB0"), Buf("CB1")]
    Rb = [Buf(f"R{n}") for n in range(8)]
    IGb = [Buf(f"IG{n}") for n in range(8)]
    Ab = [Buf(f"AA{n}") for n in range(8)]
    Ub = [Buf("U0"), Buf("U1")]
    HNb = [Buf("HN0"), Buf("HN1")]
    BPHB = CBb + Rb + IGb + Ab + Ub + HNb
    PB = [Buf(f"PB{i}") for i in range(8)]
    WBb = [Buf(f"WB{u}") for u in range(NUNIT)]
    CGb = [Buf(f"CG{i}") for i in range(8)]
    YOUTb = Buf("YOUT")
    XLDb = Buf("XLD")
    STb = Buf("STIO")
    STHb = Buf("STH")
    STCb = Buf("STC")
    LDSb = Buf("LDS")
    LDHb = Buf("LDH")
    LDCb = Buf("LDC")

    fw.dma("pool", PAR[:].rearrange("p a b -> p (a b)"), par_d, owner=PARb, writes=[PARb])
    fw.dma("pool", CST[:], cst_d, owner=CSTb, writes=[CSTb])

    IDF = CST[:, 0:128]
    MASK = CST[:, 128:256]
    ONESF = CST[:, 256:384]

    def par(i, h=None):
        if h is None:
            return PAR[:, i, :]
        return PAR[:, i, h:h + 1]

    def der(i, h=None):
        if h is None:
            return DER[:, i, :]
        return DER[:, i, h:h + 1]
    D_LB, D_OML, D_NOML, D_CL, D_CL2, D_T0, D_T1, D_T2 = range(8)

    st = {"next_dma": 0, "next_use": 0, "sbf": 0, "s": 0}
    total_uses = [0]

    CAST_AHEAD = 14
    cst_ = {"next": 0}

    def emit_cast_upto(u_hi):
        while cst_["next"] <= min(u_hi, NUNIT - 1):
            u = cst_["next"]
            o, sz = UOFF[u], UNITS[u][2]
            cg = CGb[u % 8]
            fw.dma("pool", wb[:, o:o + sz], wall[:, o:o + sz], owner=cg, writes=[cg, WBb[u]])
            cst_["next"] += 1

    def emit_cast():
        emit_cast_upto(CAST_AHEAD)

    def emit_wload(s):
        u = s % NUNIT
        o, sz = UOFF[u], UNITS[u][2]
        slot = s % NSLOT
        fw.dma("sp", RING[:, slot, 0:sz], wb[:, o:o + sz], owner=RSb[slot],
               reads=[WBb[u]], writes=[RSb[slot]])

    def use_unit(kind):
        s = st["next_use"]
        assert UNITS[s % NUNIT][0] == kind, (UNITS[s % NUNIT], kind)
        if cst_["next"] < NUNIT:
            emit_cast_upto(s + CAST_AHEAD)
        while st["next_dma"] <= min(s + NSLOT - 1, total_uses[0] - 1):
            emit_wload(st["next_dma"])
            st["next_dma"] += 1
        st["next_use"] += 1
        slot = s % NSLOT
        return RING[:, slot, :], RSb[slot]

    def mm(out, lhsT, rhs, start, stop, reads, writes):
        fw.op("pe", lambda: PE.matmul(out, lhsT=lhsT, rhs=rhs, start=start, stop=stop,
                                      skip_group_check=True), reads, writes)

    def act(out, in_, func, reads, writes, **kw):
        fw.op("act", lambda: A.activation(out=out, in_=in_, func=func, **kw), reads, writes)

    SQkb = [Buf(f"SQ{k}") for k in range(NK)]
    nst = {"presq": False}

    def norm_sq(kc, N):
        act(SQ[:, kc, :N], X[:, kc, :N], AF.Square, [Xb[kc]], [SQkb[kc]])

    def norm_mm(kc, N):
        mm(PS[6][:, :N], ONESB[:], SQ[:, kc, :N], kc == 0, kc == NK - 1, [SQkb[kc], CSTb], [PB[6]])

    def rmsnorm(gidx, N, out_f32=None, out_bufs=None):
        if not nst["presq"]:
            for kc in range(NK):
                norm_sq(kc, N)
                norm_mm(kc, N)
        nst["presq"] = False
        act(RSTD[:, :N], PS[6][:, :N], AF.Sqrt, [PB[6]], [RSTDb], scale=1.0 / D, bias=EPS)
        fw.op("dve", lambda: V.reciprocal(out=RSTD[:, :N], in_=RSTD[:, :N]), [RSTDb], [RSTDb])
        for kc in range(NK):
            if out_f32 is None:
                o, ob = XN[:, kc, :N], [XNb[kc]]
            else:
                o, ob = out_f32[:, kc, :N], [out_bufs[kc]]
            fw.op("dve", lambda o=o, kc=kc: V.scalar_tensor_tensor(
                out=o, in0=X[:, kc, :N], scalar=par(gidx, kc), in1=RSTD[:, :N],
                op0=ALU.mult, op1=ALU.mult), [Xb[kc], RSTDb, PARb], ob)

    def ffn(fi, N):
        rmsnorm(P_FFN0 + fi, N)
        for j in range(11):
            u, ub = use_unit("ffn_in")
            uv = u.rearrange("p (s k c) -> p s k c", s=2, k=NK)
            for mm_ in range(2):
                m = 2 * j + mm_
                pa, pab = PS[m % 2], PB[m % 2]
                pb, pbb = PS[2 + m % 2], PB[2 + m % 2]
                for kc in range(NK):
                    mm(pa[:, :N], uv[:, 0, kc, mm_ * 128:(mm_ + 1) * 128], XN[:, kc, :N],
                       kc == 0, kc == NK - 1, [ub, XNb[kc]], [pab])
                for kc in range(NK):
                    mm(pb[:, :N], uv[:, 1, kc, mm_ * 128:(mm_ + 1) * 128], XN[:, kc, :N],
                       kc == 0, kc == NK - 1, [ub, XNb[kc]], [pbb])
                t, tb = Tt[m % 2], Tb[m % 2]
                act(t[:, :N], pa[:, :N], AF.Silu, [pab], [tb])
                fw.op("dve", lambda t=t, pb=pb, m=m: V.tensor_tensor(
                    out=Hh[:, m, :N], in0=t[:, :N], in1=pb[:, :N], op=ALU.mult), [tb, pbb], [Hb[m]])
        for mo in range(NK):
            u, ub = use_unit("ffn_out")
            uv = u[:, 0:NF * 128].rearrange("p (k c) -> p k c", k=NF)
            py, pyb = PS[4 + mo % 2], PB[4 + mo % 2]
            for kf in range(NF):
                mm(py[:, :N], uv[:, kf, :], Hh[:, kf, :N], kf == 0, kf == NF - 1, [ub, Hb[kf]], [pyb])
            if mo >= 1:
                norm_mm(mo - 1, N)
            fw.op("dve", lambda py=py, mo=mo: V.scalar_tensor_tensor(
                out=X[:, mo, :N], in0=py[:, :N], scalar=0.5, in1=X[:, mo, :N],
                op0=ALU.mult, op1=ALU.add), [pyb, Xb[mo]], [Xb[mo]])
            norm_sq(mo, N)
        norm_mm(NK - 1, N)
        nst["presq"] = True

    def proj_fm(kind, nblk, N, evac):
        u = ub = uv = None
        for blk in range(nblk):
            if blk % 4 == 0:
                u, ub = use_unit(kind)
                uv = u.rearrange("p (k c) -> p k c", k=NK)
            p, pbuf = PS[blk % 2], PB[blk % 2]
            for kc in range(NK):
                mm(p[:, :N], uv[:, kc, (blk % 4) * 128:(blk % 4 + 1) * 128], XN[:, kc, :N],
                   kc == 0, kc == NK - 1, [ub, XNb[kc]], [pbuf])
            evac(blk, p, pbuf)

    def out_proj(kind, N, src, src_bufs):
        u = ub = uv = None
        for mo in range(NK):
            if mo % 4 == 0:
                u, ub = use_unit(kind)
                uv = u.rearrange("p (k c) -> p k c", k=NK)
            py, pyb = PS[4 + mo % 2], PB[4 + mo % 2]
            for kh in range(NK):
                mm(py[:, :N], uv[:, kh, (mo % 4) * 128:(mo % 4 + 1) * 128], src[:, kh, :N],
                   kh == 0, kh == NK - 1, [ub] + src_bufs, [pyb])
            if mo >= 1:
                norm_mm(mo - 1, N)
            fw.op("dve", lambda py=py, mo=mo: V.tensor_tensor(
                out=X[:, mo, :N], in0=py[:, :N], in1=X[:, mo, :N], op=ALU.add), [pyb, Xb[mo]], [Xb[mo]])
            norm_sq(mo, N)
        norm_mm(NK - 1, N)
        nst["presq"] = True

    def mixer_a(N):
        TN = min(N, 128)
        ntile = N // TN
        nb = TN // 16
        rmsnorm(P_MIX0 + 0, N)
        Q, K, LF = A0, A1, A2
        Vv = VB[:].rearrange("p (t c) -> p t c", t=4)

        def ev_q(h, p, pbuf):
            act(Q[:, h, :N], p[:, :N], AF.Silu, [pbuf], [A0b[h]])
        import os
        kstop = os.environ.get("KSTOP", "")
        proj_fm("a_in", 8, N, ev_q)
        if kstop == "q":
            skip(8)
            return

        def ev_f(h, p, pbuf):
            act(K[:, h, :N], p[:, :N], AF.Sigmoid, [pbuf], [A1b[h]])
        proj_fm("a_in", 8, N, ev_f)
        for h in range(8):
            act(LF[:, h, :N], K[:, h, :N], AF.Ln, [A1b[h], DERb], [A2b[h]],
                scale=der(D_OML, h), bias=der(D_LB, h))
            fw.op("dve", lambda h=h: V.tensor_scalar(
                out=K[:, h, :N], in0=K[:, h, :N], scalar1=der(D_NOML, h), scalar2=der(D_OML, h),
                op0=ALU.mult, op1=ALU.add), [A1b[h], DERb], [A1b[h]])
        if kstop == "f":
            skip(6)
            return

        idx = 0
        for c2 in range(2):
            u, ub = use_unit("a_in")
            uv = u.rearrange("p (k c) -> p k c", k=NK)
            for tt in range(ntile):
                p, pbuf = PS[2 + idx % 2], PB[2 + idx % 2]
                for kc in range(NK):
                    mm(p[:TN, :], XN[:, kc, tt * TN:(tt + 1) * TN], uv[:, kc, :],
                       kc == 0, kc == NK - 1, [ub, XNb[kc]], [pbuf])
                if idx % 2 == 0:
                    act(Vv[:TN, tt, c2 * 512:(c2 + 1) * 512], p[:TN, :], AF.Copy, [pbuf], [VBb])
                else:
                    fw.op("dve", lambda p=p, tt=tt, c2=c2: V.tensor_copy(
                        out=Vv[:TN, tt, c2 * 512:(c2 + 1) * 512], in_=p[:TN, :]), [pbuf], [VBb])
                idx += 1

        if kstop == "v":
            skip(4)
            return
        fw.handoff(FFNB, TILEB)
        import os
        sub = int(os.environ.get("KSUB", "99"))
        GP = nc.gpsimd
        for i in range(2 if ntile > 1 else 1):
            fw.op("pool", lambda i=i: GP.memset(G2[i][:], 0.0), [], [G2b[i]])
        opb = [PB[1], PB[2]]

        def prep_stages(tt):
            pr = tt % 2
            G, W, E1, QD, KD, KPT = G2[pr], W2[pr], E12[pr], QD2[pr], KD2[pr], KPT2[pr]
            Gb, Wb, E1b, QDb, KDb, KPTb = G2b[pr], W2b[pr], E12b[pr], QD2b[pr], KD2b[pr], KPT2b[pr]
            tsl = slice(tt * TN, (tt + 1) * TN)

            def v4(ap):
                return ap.rearrange("p h (b j) -> p h b j", j=16)

            def s0():
                for h in range(8):
                    fw.op("dve", lambda h=h: V.tensor_tensor_scan(
                        out=G[:, h, 1:TN + 1], data0=ONESF[:, :TN], data1=LF[:, h, tsl], initial=0.0,
                        op0=ALU.mult, op1=ALU.add), [A2b[h], CSTb], [Gb])

            def s1():
                g1 = v4(G[:, :, 1:TN + 1])
                g0 = v4(G[:, :, 0:TN])[:, :, :, 0:1].to_broadcast([128, 8, nb, 16])
                fw.op("pool", lambda: GP.tensor_tensor(out=v4(W[:, :, :TN]), in0=g1, in1=g0, op=ALU.subtract),
                      [Gb], [Wb])

            def s2():
                act(E1[:, :, :TN], W[:, :, :TN], AF.Exp, [Wb], [E1b])
                act(W[:, :, :TN], W[:, :, :TN], AF.Exp, [Wb], [Wb], scale=-1.0)

            def s3():
                fw.op("pool", lambda: GP.tensor_tensor(out=QD[:, :, :TN], in0=Q[:, :, tsl], in1=E1[:, :, :TN],
                                                       op=ALU.mult), A0b + [E1b], [QDb])
                fw.op("pool", lambda: GP.tensor_tensor(out=W[:, :, :TN], in0=K[:, :, tsl], in1=W[:, :, :TN],
                                                       op=ALU.mult), A1b + [Wb], [Wb])

            def s4():
                act(KD[:, :, :TN], W[:, :, :TN], AF.Copy, [Wb], [KDb])
                e1l = v4(E1[:, :, :TN])[:, :, :, 15:16].to_broadcast([128, 8, nb, 16])
                fw.op("pool", lambda: GP.tensor_tensor(out=v4(W[:, :, :TN]), in0=v4(W[:, :, :TN]), in1=e1l,
                                                       op=ALU.mult), [Wb, E1b], [Wb])

            def s5(hf):
                def f():
                    bank = 7 if hf == 0 else 0
                    for h in range(4 * hf, 4 * hf + 4):
                        fw.op("pe", lambda h=h: PE.transpose(PS[bank][:TN, (h % 4) * 128:(h % 4 + 1) * 128],
                                                             W[:, h, :TN], IDF), [Wb, CSTb], [PB[bank]])
                    nmask = 4 if TN == 128 else 1
                    for m4 in range(nmask):
                        fw.op("dve", lambda m4=m4: V.tensor_scalar(
                            out=KPT[m4][:TN, hf * 512:(hf + 1) * 512], in0=PS[bank][:TN, :],
                            scalar1=PM[:TN, m4:m4 + 1], scalar2=None, op0=ALU.mult),
                            [PB[bank], CSTb], [KPTb[m4]])
                return f
            return [s0, s1, s2, s3, s4, s5(0), s5(1)]

        SLOTS = {0: [0, 1], 2: [2], 3: [3], 5: [4], 6: [5], 7: [6]}

        def front_scores(tt):
            pr = tt % 2
            QD, KD, QDb, KDb = QD2[pr], KD2[pr], QD2b[pr], KD2b[pr]
            for hg in range(2):
                sb = 0 if hg == 0 else 7
                for h in range(hg * 4, hg * 4 + 4):
                    mm(PS[sb][:TN, (h % 4) * TN:(h % 4 + 1) * TN], KD[:, h, :TN], QD[:, h, :TN],
                       h % 4 == 0, h % 4 == 3, [KDb, QDb], [PB[sb]])
                scv = PS[sb][:TN, 0:4 * TN].rearrange("p (h t) -> p h t", h=4)
                mk = MASK[:TN, :TN].unsqueeze(1).to_broadcast([TN, 4, TN])
                fw.op("dve", lambda scv=scv, mk=mk, hg=hg: V.tensor_tensor(
                    out=AT[:TN, hg * 4:hg * 4 + 4, :TN], in0=scv, in1=mk, op=ALU.mult), [PB[sb], CSTb], [ATb])

        def emit_u(tt, n):
            KPT, KPTb = KPT2[tt % 2], KPT2b[tt % 2]
            if TN == 128:
                r0, kr, kp, kpb = 64 * (n // 4), 64, KPT[n % 4], KPTb[n % 4]
            else:
                r0, kr, kp, kpb = 0, 16, KPT[0], KPTb[0]
            bk = 3 + 2 * (n % 2)
            for h in range(8):
                mm(PS[bk + h // 4][:, (h % 4) * 128:(h % 4 + 1) * 128],
                   kp[r0:r0 + kr, h * 128:(h + 1) * 128], Vv[r0:r0 + kr, tt, h * 128:(h + 1) * 128],
                   h % 4 == 0, h % 4 == 3, [kpb, VBb], [PB[bk + h // 4]])

        gst = {"u": None, "done": False}

        def gproj_head(h):
            if h % 4 == 0:
                gst["u"] = use_unit("a_in")
            u, ub = gst["u"]
            uv = u.rearrange("p (k c) -> p k c", k=NK)
            bank = 0 if h % 2 == 0 else 7
            for kc in range(NK):
                mm(PS[bank][:, :N], uv[:, kc, (h % 4) * 128:(h % 4 + 1) * 128], XN[:, kc, :N],
                   kc == 0, kc == NK - 1, [ub, XNb[kc]], [PB[bank]])
            act(LF[:, h, :N], PS[bank][:, :N], AF.Silu, [PB[bank]], [A2b[h]])

        def tile_body(tt, nxt):
            pr = tt % 2
            E1, QD, KD, KPT = E12[pr], QD2[pr], KD2[pr], KPT2[pr]
            E1b, QDb, KDb, KPTb = E12b[pr], QD2b[pr], KD2b[pr], KPT2b[pr]
            tsl = slice(tt * TN, (tt + 1) * TN)
            for hg in range(2):
                for h in range(hg * 4, hg * 4 + 4):
                    mm(PS[1 + hg][:, (h % 4) * TN:(h % 4 + 1) * TN], Vv[:TN, tt, h * 128:(h + 1) * 128],
                       AT[:TN, h, :TN], h % 4 == 0, False, [VBb, ATb], [opb[hg]])

            if tt == 0:
                emit_u(tt, 0)
            for n in range(nb):
                if n + 1 < nb:
                    emit_u(tt, n + 1)
                cs = st["sbf"]
                for h in range(8):
                    mm(PS[1 + h // 4][:, (h % 4) * TN + 16 * n:(h % 4) * TN + 16 * n + 16],
                       SBF[cs][:, h, :], QD[:, h, 16 * n:16 * n + 16], False, False,
                       [SBFb[cs], QDb], [opb[h // 4]])
                bk = 3 + 2 * (n % 2)
                si = st["s"]
                so = 1 - si
                for h in range(8):
                    fw.op("dve", lambda h=h, bk=bk, n=n, si=si, so=so: V.scalar_tensor_tensor(
                        out=SS[so][:, h, :], in0=SS[si][:, h, :], scalar=E1[:, h, 16 * n + 15:16 * n + 16],
                        in1=PS[bk + h // 4][:, (h % 4) * 128:(h % 4 + 1) * 128],
                        op0=ALU.mult, op1=ALU.add), [SSb[si][h], E1b, PB[bk + h // 4]], [SSb[so][h]])
                st["s"] = so
                ns = 1 - cs
                act(SBF[ns][:], SS[so][:], AF.Copy, SSb[so], [SBFb[ns]])
                st["sbf"] = ns
                if nxt:
                    for si_ in SLOTS.get(n, []):
                        nxt[si_]()
                if tt == ntile - 1 and nb == 8 and sub >= 7:
                    gproj_head(n)
                    gst["done"] = True
            for hg in range(2):
                pv = PS[1 + hg][:, 0:4 * TN].rearrange("p (h t) -> p h t", h=4)
                act(OO[:, hg * 4:hg * 4 + 4, :TN], pv, AF.Copy, [opb[hg]], [OOb])
            if tt + 1 < ntile:
                front_scores(tt + 1)
                emit_u(tt + 1, 0)
            fw.op("pool", lambda: GP.tensor_tensor(out=SQO[:, :, :TN], in0=OO[:, :, :TN], in1=OO[:, :, :TN],
                                                   op=ALU.mult), [OOb], [SQOb])
            for h in range(8):
                mm(PS[5 + h // 4][:, (h % 4) * TN:(h % 4 + 1) * TN], ONESB[:], SQO[:, h, :TN],
                   h % 4 == 0, h % 4 == 3, [SQOb, CSTb], [PB[5 + h // 4]])
            for hg in range(2):
                pv = PS[5 + hg][:, 0:4 * TN].rearrange("p (h t) -> p h t", h=4)
                act(RSO[:, hg * 4:hg * 4 + 4, :TN], pv, AF.Ln, [PB[5 + hg]], [RSOb], scale=1.0 / 128, bias=EPS)
            act(RSO[:, :, :TN], RSO[:, :, :TN], AF.Exp, [RSOb], [RSOb], scale=-0.5)
            fw.op("pool", lambda: GP.tensor_tensor(out=Q[:, :, tsl], in0=OO[:, :, :TN], in1=RSO[:, :, :TN],
                                                   op=ALU.mult), [OOb, RSOb], A0b)

        for f in prep_stages(0):
            f()
        front_scores(0)
        for tt in range(ntile):
            nxt = prep_stages(tt + 1) if tt + 1 < ntile else []
            tile_body(tt, nxt)

        fw.handoff(TILEB, FFNB)
        if sub < 7:
            skip(4)
            return

        def ev_g2(h):
            fw.op("dve", lambda: V.scalar_tensor_tensor(
                out=ON[:, h, :N], in0=Q[:, h, :N], scalar=par(P_ONORM, h), in1=LF[:, h, :N],
                op0=ALU.mult, op1=ALU.mult), [A0b[h], A2b[h], PARb], A1b)

        def ev_g(h, p, pbuf):
            act(LF[:, h, :N], p[:, :N], AF.Silu, [pbuf], [A2b[h]])
            fw.op("dve", lambda: V.scalar_tensor_tensor(
                out=ON[:, h, :N], in0=Q[:, h, :N], scalar=par(P_ONORM, h), in1=LF[:, h, :N],
                op0=ALU.mult, op1=ALU.mult), [A0b[h], A2b[h], PARb], A1b)
        if gst["done"]:
            for h in range(8):
                ev_g2(h)
        else:
            proj_fm("a_in", 8, N, ev_g)
        out_proj("a_out", N, ON, A1b)

    def mixer_b(N):
        rmsnorm(P_MIX0 + 1, N)
        XBr, GG, CV = A0, A1, A2
        Y = VB[:].rearrange("p (k t) -> p k t", k=8)

        fw.op("dve", lambda: V.tensor_copy(out=XBr[:, :, 0:3], in_=CVS[:]), [CVSb], A0b)

        def ev_x(n, p, pbuf):
            act(XBr[:, n, 3:3 + N], p[:, :N], AF.Copy, [pbuf], [A0b[n]])
        proj_fm("b_in", 8, N, ev_x)

        def ev_g(n, p, pbuf):
            act(GG[:, n, :N], p[:, :N], AF.Gelu, [pbuf], [A1b[n]])
        proj_fm("b_in", 8, N, ev_g)

        fw.handoff(FFNB, BPHB)
        u, ub = use_unit("b_gate")
        uv = u[:, 0:2048].rearrange("p (s n d) -> p s n d", s=2, n=8)
        fw.handoff([VBb, HSTb], Ynb + HSTnb)

        def stage_a(n):
            i2 = n % 2
            fw.op("pool", lambda n=n: nc.gpsimd.tensor_scalar(
                out=CV[:, n, :N], in0=XBr[:, n, 0:N], scalar1=par(P_CW0 + 0, n), scalar2=par(P_CB, n),
                op0=ALU.mult, op1=ALU.add), [A0b[n], PARb], [A2b[n]])
            for j in range(1, 4):
                fw.op("dve", lambda n=n, j=j: V.scalar_tensor_tensor(
                    out=CV[:, n, :N], in0=XBr[:, n, j:j + N], scalar=par(P_CW0 + j, n), in1=CV[:, n, :N],
                    op0=ALU.mult, op1=ALU.add), [A0b[n], A2b[n], PARb], [A2b[n]])
            act(CBt[i2][:, :N], CV[:, n, :N], AF.Copy, [A2b[n]], [CBb[i2]])
            pr, prb = PS[2 + 2 * i2], PB[2 + 2 * i2]
            pi, pib = PS[3 + 2 * i2], PB[3 + 2 * i2]
            mm(pr[:, :N], uv[:, 0, n, :], CBt[i2][:, :N], True, True, [ub, CBb[i2]], [prb])
            mm(pi[:, :N], uv[:, 1, n, :], CBt[i2][:, :N], True, True, [ub, CBb[i2]], [pib])
            act(RA[:, n, :N], pr[:, :N], AF.Sigmoid, [prb, PARb], [Rb[n]], bias=par(P_BA, n))
            act(IGA[:, n, :N], pi[:, :N], AF.Sigmoid, [pib, PARb], [IGb[n]], bias=par(P_BX, n))

        def stage_b(n):
            i2 = n % 2
            fw.op("dve", lambda i2=i2, n=n: V.tensor_tensor(
                out=Ut[i2][:, :N], in0=IGA[:, n, :N], in1=CV[:, n, :N], op=ALU.mult), [IGb[n], A2b[n]], [Ub[i2]])
            fw.op("dve", lambda i2=i2, n=n: V.tensor_tensor(
                out=Ut[i2][:, :N], in0=Ut[i2][:, :N], in1=RA[:, n, :N], op=ALU.mult), [Ub[i2], Rb[n]], [Ub[i2]])
            fw.op("dve", lambda i2=i2, n=n: V.tensor_tensor_scan(
                out=HNt[i2][:, :N], data0=AA[:, n, :N], data1=Ut[i2][:, :N], initial=HST[:, n:n + 1],
                op0=ALU.mult, op1=ALU.add), [Ab[n], Ub[i2], HSTnb[n]], [HNb[i2]])
            fw.op("dve", lambda i2=i2, n=n: V.tensor_copy(out=HST[:, n:n + 1], in_=HNt[i2][:, N - 1:N]),
                  [HNb[i2]], [HSTnb[n]])
            fw.op("dve", lambda i2=i2, n=n: V.tensor_tensor(
                out=Y[:, n, :N], in0=HNt[i2][:, :N], in1=GG[:, n, :N], op=ALU.mult), [HNb[i2], A1b[n]], [Ynb[n]])

        for n in range(8):
            stage_a(n)
        for half in range(2):
            ns_ = range(4 * half, 4 * half + 4)
            for n in ns_:
                act(AA[:, n, :N], RA[:, n, :N], AF.Exp, [Rb[n], DERb], [Ab[n]], scale=der(D_CL, n))
                act(RA[:, n, :N], RA[:, n, :N], AF.Exp, [Rb[n], DERb], [Rb[n]], scale=der(D_CL2, n))
            for n in ns_:
                act(RA[:, n, :N], RA[:, n, :N], AF.Sqrt, [Rb[n]], [Rb[n]], scale=-1.0, bias=1.0)
        for n in range(8):
            stage_b(n)
        fw.op("dve", lambda: V.tensor_copy(out=CVS[:], in_=XBr[:, :, N:N + 3]), A0b, [CVSb])
        fw.handoff(BPHB, FFNB)
        out_proj("b_out", N, Y, Ynb)
        fw.handoff(Ynb + HSTnb, [VBb, HSTb])

    LEVELS = ["io", "ffn0", "mixa", "ffn1", "ffn2", "mixb", "ffn3", "all"]
    lvl = LEVELS.index(dbg) if dbg else len(LEVELS) - 1

    def skip(k):
        st["next_use"] += k

    def prefetch(src, N):
        fw.dma("pool", A0[:, :, :N], src, owner=XLDb, writes=A0b)

    def chunk(src, dst, N, first, nxt):
        if first:
            prefetch(src, N)
            emit_cast()
        for kc in range(NK):
            act(X[:, kc, :N], A0[:, kc, :N], AF.Copy, [A0b[kc]], [Xb[kc]])
        nst["presq"] = False
        if lvl >= 1:
            ffn(0, N)
        else:
            skip(19)
        if lvl >= 2:
            mixer_a(N)
        else:
            skip(10)
        if lvl >= 3:
            ffn(1, N)
        else:
            skip(19)
        if lvl >= 4:
            ffn(2, N)
        else:
            skip(19)
        if lvl >= 5:
            mixer_b(N)
        else:
            skip(7)
        if nxt is not None:
            prefetch(nxt[0], nxt[1])
        if lvl >= 6:
            ffn(3, N)
        else:
            skip(19)
        if lvl >= 7:
            rmsnorm(P_FIN, N, out_f32=A1, out_bufs=A1b)
        else:
            fw.op("dve", lambda: V.tensor_copy(out=A1[:, :, :N], in_=X[:, :, :N]), Xb, A1b)
        fw.dma("pool", dst, A1[:, :, :N], owner=YOUTb, reads=A1b, writes=[YOUTb])

    def write_states(idx):
        fw.dma("pool", sout[idx], SS[st["s"]][:], owner=STb, reads=SSb[st["s"]], writes=[STb])
        fw.dma("pool", hout[:, idx, :], HST[:], owner=STHb, reads=[HSTb], writes=[STHb])
        fw.dma("pool", cout[:, idx, :, :], CVS[:], owner=STCb, reads=[CVSb], writes=[STCb])

    chunks = []
    pos = 0
    while pos < T:
        n = min(512, T - pos)
        chunks.append((pos, n))
        pos += n
    total_uses[0] = NUNIT * (len(chunks) + NS)

    pm_d = nc.dram_tensor("pm", [128, 4], F32, kind="ExternalInput").ap()
    fw.dma("pool", PM[:], pm_d, owner=CSTb, writes=[CSTb])
    r = [PARb]
    w = [DERb]
    act(der(D_T0), par(P_LB0), AF.Exp, r, w)
    act(der(D_T1), par(P_LB1), AF.Exp, r, w)
    fw.op("dve", lambda: V.tensor_tensor(out=der(D_T2), in0=der(D_T0), in1=der(D_T1), op=ALU.add), [DERb], w)
    fw.op("dve", lambda: V.reciprocal(out=der(D_T2), in_=der(D_T2)), [DERb], w)
    fw.op("dve", lambda: V.tensor_tensor(out=der(D_LB), in0=der(D_T0), in1=der(D_T2), op=ALU.mult), [DERb], w)
    fw.op("dve", lambda: V.tensor_tensor(out=der(D_OML), in0=der(D_T1), in1=der(D_T2), op=ALU.mult), [DERb], w)
    fw.op("dve", lambda: V.tensor_scalar(out=der(D_NOML), in0=der(D_OML), scalar1=-1.0, scalar2=None, op0=ALU.mult), [DERb], w)
    act(der(D_T0), par(P_LAM), AF.Exp, r + [DERb], w, scale=-1.0)
    act(der(D_T0), der(D_T0), AF.Ln, [DERb], w, bias=1.0)
    fw.op("dve", lambda: V.tensor_scalar(out=der(D_CL), in0=der(D_T0), scalar1=-8.0, scalar2=None, op0=ALU.mult), [DERb], w)
    fw.op("dve", lambda: V.tensor_scalar(out=der(D_CL2), in0=der(D_T0), scalar1=-16.0, scalar2=None, op0=ALU.mult), [DERb], w)
    fw.op("dve", lambda: V.tensor_copy(out=IDB[:], in_=IDF), [CSTb], [CSTb])
    fw.op("dve", lambda: V.tensor_copy(out=ONESB[:], in_=ONESF), [CSTb], [CSTb])

    fw.op("dve", lambda: V.memset(SS[0][:], 0.0), [], SSb[0])
    fw.op("dve", lambda: V.memset(SBF[0][:], 0.0), [], [SBFb[0]])
    fw.op("dve", lambda: V.memset(HST[:], 0.0), [], [HSTb])
    fw.op("dve", lambda: V.memset(CVS[:], 0.0), [], [CVSb])
    st["sbf"] = 0
    jobs = [(xseq[:, :, pos:pos + n], yseq[:, :, pos:pos + n], n) for (pos, n) in chunks]
    for s in range(NS):
        jobs.append((xsmp[:, :, s * NSMP:(s + 1) * NSMP], ysmp[:, :, s * NSMP:(s + 1) * NSMP], NSMP))
    npr = len(chunks)
    for ji, (src, dst, n) in enumerate(jobs):
        nxt = (jobs[ji + 1][0], jobs[ji + 1][2]) if ji + 1 < len(jobs) else None
        if ji >= npr:
            sidx = ji - npr
            fw.dma("pool", SS[st["s"]][:], s0_d[sidx], owner=LDSb, reads=[], writes=SSb[st["s"]])
            fw.dma("pool", HST[:], h0_d[:, sidx, :], owner=LDHb, writes=[HSTb])
            fw.dma("pool", CVS[:], c0_d[:, sidx, :, :], owner=LDCb, writes=[CVSb])
            cs = st["sbf"]
            act(SBF[cs][:], SS[st["s"]][:], AF.Copy, SSb[st["s"]], [SBFb[cs]])
        chunk(src, dst, n, ji == 0, nxt)
        if ji == npr - 1:
            write_states(0)
        elif ji >= npr:
            write_states(1 + ji - npr)
    fw.final_wait("pool", [YOUTb, STb, STHb, STCb])
    assert st["next_use"] == total_uses[0], (st["next_use"], total_uses[0])
    return nc, fw


def fm(v):
    return np.ascontiguousarray(np.asarray(v, np.float32).reshape(8, 128).T)


def build_wall(ffn_w_in, ffn_w_out, a_w_in, a_w_out, b_w_in, b_wa, b_wx, b_w_out):
    wall = np.empty((128, WTOT), np.float32)

    def std(wm, u):
        blk = wm[:, u * 512:(u + 1) * 512].reshape(NK, 128, 512)
        return blk.transpose(1, 0, 2).reshape(128, -1)
    for ui, (kind, args, sz) in enumerate(UNITS):
        o = UOFF[ui]
        if kind == "ffn_in":
            fi, j = args
            wm = ffn_w_in[fi // 2, fi % 2].reshape(NK, 128, 2, DFF)[:, :, :, j * 256:(j + 1) * 256]
            blk = wm.transpose(1, 2, 0, 3).reshape(128, -1)
        elif kind == "ffn_out":
            fi, mo = args
            wm = ffn_w_out[fi // 2, fi % 2].reshape(NF, 128, 8, 128)[:, :, mo, :]
            blk = wm.transpose(1, 0, 2).reshape(128, -1)
        elif kind == "a_in":
            blk = std(a_w_in[0], args[0])
        elif kind == "a_out":
            blk = std(a_w_out[0], args[0])
        elif kind == "b_in":
            blk = std(b_w_in[0], args[0])
        elif kind == "b_out":
            blk = std(b_w_out[0], args[0])
        elif kind == "b_gate":
            blk = np.stack([b_wa[0], b_wx[0]], 0).transpose(2, 0, 1, 3).reshape(128, -1)
        assert blk.shape[1] == sz, (kind, blk.shape, sz)
        wall[:, o:o + sz] = blk
    return wall


def build_consts():
    cst = np.zeros((128, 384), np.float32)
    cst[:, 0:128] = np.eye(128, dtype=np.float32)
    s = np.arange(128)[:, None]
    t = np.arange(128)[None, :]
    cst[:, 128:256] = ((s // 16 == t // 16) & (s <= t)).astype(np.float32)
    cst[:, 256:384] = 1.0
    pm = np.zeros((128, 4), np.float32)
    blk = np.arange(128) // 16
    for m4 in range(4):
        pm[:, m4] = (blk % 4 == m4)
    return cst, pm


_CACHE = {}


def kernel(x_prompt, x_sample, state_hgrn, state_rglru, state_conv, meta_tokens,
           ffn_norm, ffn_w_in, ffn_w_out, mix_norm, a_w_in, a_lb, a_onorm, a_w_out,
           b_w_in, b_conv_w, b_conv_b, b_wa, b_ba, b_wx, b_bx, b_lambda, b_w_out, final_norm,
           _ncores=8):
    f32 = np.float32
    x_prompt = np.asarray(x_prompt, f32)
    x_sample = np.asarray(x_sample, f32)
    B, SEQ, _ = x_prompt.shape
    DB, DS, _ = x_sample.shape
    T = SEQ + NMETA
    NS = DB // _ncores
    import os
    dbg = os.environ.get("KDBG") or None
    key = (T, NS, DS, dbg)
    if key not in _CACHE:
        _CACHE[key] = build_program(T, NS, DS, dbg)
    nc, fw = _CACHE[key]

    wall = build_wall(np.asarray(ffn_w_in, f32), np.asarray(ffn_w_out, f32), np.asarray(a_w_in, f32),
                      np.asarray(a_w_out, f32), np.asarray(b_w_in, f32), np.asarray(b_wa, f32),
                      np.asarray(b_wx, f32), np.asarray(b_w_out, f32))
    pv = [None] * NPAR
    fn = np.asarray(ffn_norm, f32)
    for l in range(2):
        for i in range(2):
            pv[P_FFN0 + l * 2 + i] = fm(fn[l, i])
    mn = np.asarray(mix_norm, f32)
    pv[P_MIX0], pv[P_MIX0 + 1] = fm(mn[0]), fm(mn[1])
    pv[P_FIN] = fm(final_norm)
    alb = np.asarray(a_lb, f32)
    pv[P_LB0], pv[P_LB1] = fm(alb[0]), fm(alb[1])
    pv[P_ONORM] = fm(np.asarray(a_onorm, f32)[0])
    cw = np.asarray(b_conv_w, f32)[0]
    for j in range(4):
        pv[P_CW0 + j] = fm(cw[j])
    pv[P_CB] = fm(np.asarray(b_conv_b, f32)[0])
    pv[P_BA] = fm(np.asarray(b_ba, f32)[0])
    pv[P_BX] = fm(np.asarray(b_bx, f32)[0])
    pv[P_LAM] = fm(np.asarray(b_lambda, f32)[0])
    par = np.ascontiguousarray(np.stack(pv, 1).reshape(128, NPAR * 8))
    cst, pm = build_consts()

    meta = np.asarray(meta_tokens, f32)
    sh = np.asarray(state_hgrn, f32)[0]
    sr = np.asarray(state_rglru, f32)[0]
    sc = np.asarray(state_conv, f32)[0]

    owner = {0: 0, 1: 1, 4: 2, 5: 3} if _ncores == 8 else {i: i for i in range(min(B, _ncores))}
    in_maps = []
    for c in range(_ncores):
        if c in owner:
            xs = np.concatenate([meta, x_prompt[owner[c]]], 0)
            xseq = np.ascontiguousarray(xs.reshape(T, NK, 128).transpose(2, 1, 0))
        else:
            xseq = np.zeros((128, NK, T), f32)
        ss = slice(c * NS, (c + 1) * NS)
        xm = x_sample[ss].reshape(NS * DS, NK, 128).transpose(2, 1, 0)
        in_maps.append({
            "wall": wall, "par": par, "cst": cst, "pm": pm,
            "xseq": xseq, "xsmp": np.ascontiguousarray(xm),
            "s0": np.ascontiguousarray(sh[ss].transpose(0, 2, 1, 3)),
            "h0": np.ascontiguousarray(sr[ss].reshape(NS, 8, 128).transpose(2, 0, 1)),
            "c0": np.ascontiguousarray(sc[ss].reshape(NS, 3, 8, 128).transpose(3, 0, 2, 1)),
        })
    res = run_bass_kernel_spmd(nc, in_maps, core_ids=list(range(_ncores)))
    R = res.results

    y_prompt = np.empty((B, SEQ, D), f32)
    y_sample = np.empty((DB, DS, D), f32)
    hg_p = np.empty((1, B, 8, 128, 128), f32)
    hg_s = np.empty((1, DB, 8, 128, 128), f32)
    rg_p = np.empty((1, B, D), f32)
    rg_s = np.empty((1, DB, D), f32)
    cv_p = np.empty((1, B, 3, D), f32)
    cv_s = np.empty((1, DB, 3, D), f32)
    for c in range(_ncores):
        r = R[c]
        so, ho, co = r["sout"], r["hout"], r["cout"]
        if c in owner:
            b = owner[c]
            y_prompt[b] = r["yseq"].transpose(2, 1, 0).reshape(T, D)[NMETA:]
            hg_p[0, b] = so[0].transpose(1, 0, 2)
            rg_p[0, b] = ho[:, 0, :].T.reshape(D)
            cv_p[0, b] = co[:, 0].transpose(2, 1, 0).reshape(3, D)
        ym = r["ysmp"].transpose(2, 1, 0).reshape(NS, DS, D)
        for s in range(NS):
            g = c * NS + s
            y_sample[g] = ym[s]
            hg_s[0, g] = so[1 + s].transpose(1, 0, 2)
            rg_s[0, g] = ho[:, 1 + s, :].T.reshape(D)
            cv_s[0, g] = co[:, 1 + s].transpose(2, 1, 0).reshape(3, D)
    return (y_prompt, y_sample, hg_p, hg_s, rg_p, rg_s, cv_p, cv_s)
```

```python
import numpy as np
import concourse.bass as bass
import concourse.mybir as mybir
from concourse.bass_utils import run_bass_kernel_spmd

F32 = mybir.dt.float32
BF16 = mybir.dt.bfloat16
AF = mybir.ActivationFunctionType
ALU = mybir.AluOpType

D = 1024
NK = 8
DFF = 2816
NF = 22
NMETA = 16
EPS = 1e-6
NSLOT = 5
USZ = 4096


class Buf:
    __slots__ = ("name", "w", "r", "dsem", "dcnt")

    def __init__(self, name):
        self.name = name
        self.w = None
        self.r = {}
        self.dsem = None
        self.dcnt = 0


class FW:
    def __init__(self, nc):
        self.nc = nc
        self.eng = {"pe": nc.tensor, "act": nc.scalar, "dve": nc.vector,
                    "pool": nc.gpsimd, "sp": nc.sync}
        self.sem = {}
        self.cnt = {}
        self.waited = {}
        self.semobj = {}
        for e in self.eng:
            self.sem[e] = nc.alloc_semaphore(name="prog_" + e)
            self.semobj[e] = self.sem[e]
            self.cnt[e] = 0
            self.waited[e] = {}
        self.ninst = 0

    def _need(self, reads, writes):
        need = {}

        def add(clk):
            if clk is None:
                return
            k, v = clk
            if v > need.get(k, 0):
                need[k] = v
        for b in reads:
            add(b.w)
        for b in writes:
            add(b.w)
            for k, v in b.r.items():
                add((k, v))
        return need

    def _emit_waits(self, e, need, skip_self=False):
        eng = self.eng[e]
        wd = self.waited[e]
        for k, v in need.items():
            if skip_self and k == e:
                continue
            if wd.get(k, 0) >= v:
                continue
            eng.wait_ge(self.semobj[k], v)
            wd[k] = v

    def op(self, e, fn, reads=(), writes=()):
        need = self._need(reads, writes)
        self._emit_waits(e, need, skip_self=(e == "pe"))
        ins = fn()
        self.cnt[e] += 1
        c = self.cnt[e]
        ins.then_inc(self.sem[e], 1)
        for b in writes:
            b.w = (e, c)
            b.r = {}
        for b in reads:
            if b not in writes:
                b.r[e] = c
        self.ninst += 1
        return ins

    def dma(self, q, out_ap, in_ap, owner, reads=(), writes=(), **kw):
        need = self._need(reads, writes)
        self._emit_waits(q, need)
        k = "d_" + owner.name
        if owner.dsem is None:
            owner.dsem = self.nc.alloc_semaphore(name=k)
            self.semobj[k] = owner.dsem
        owner.dcnt += 1
        v = 16 * owner.dcnt
        ins = self.eng[q].dma_start(out=out_ap, in_=in_ap, **kw)
        ins.then_inc(owner.dsem, 16)
        for b in writes:
            b.w = (k, v)
            b.r = {}
        for b in reads:
            if b not in writes:
                b.r[k] = v
        self.ninst += 1
        return ins

    def handoff(self, olds, news):
        merged = {}
        for b in olds:
            if b.w is not None and b.w[1] > merged.get(b.w[0], 0):
                merged[b.w[0]] = b.w[1]
            for k, v in b.r.items():
                if v > merged.get(k, 0):
                    merged[k] = v
        for b in news:
            b.w = None
            b.r = dict(merged)

    def final_wait(self, e, bufs):
        self._emit_waits(e, self._need([], bufs))


def unit_table():
    units = []

    def ffn(fi):
        for j in range(11):
            units.append(("ffn_in", (fi, j), 4096))
        for mo in range(8):
            units.append(("ffn_out", (fi, mo), NF * 128))
    ffn(0)
    for u in range(8):
        units.append(("a_in", (u,), 4096))
    for u in range(2):
        units.append(("a_out", (u,), 4096))
    ffn(1)
    ffn(2)
    for u in range(4):
        units.append(("b_in", (u,), 4096))
    units.append(("b_gate", (), 2048))
    for u in range(2):
        units.append(("b_out", (u,), 4096))
    ffn(3)
    offs = []
    o = 0
    for _, _, sz in units:
        offs.append(o)
        o += sz
    return units, offs, o


UNITS, UOFF, WTOT = unit_table()
NUNIT = len(UNITS)

NPAR = 18
(P_FFN0, P_MIX0, P_FIN, P_LB0, P_LB1, P_ONORM, P_CW0, P_CB, P_BA, P_BX, P_LAM) = (0, 4, 6, 7, 8, 9, 10, 14, 15, 16, 17)


def build_program(T, NS=2, NSMP=16, dbg=None):
    nc = bass.Bass("TRN2", target_bir_lowering=False)
    fw = FW(nc)
    V = nc.vector
    A = nc.scalar
    PE = nc.tensor

    wall = nc.dram_tensor("wall", [128, WTOT], F32, kind="ExternalInput").ap()
    par_d = nc.dram_tensor("par", [128, NPAR * 8], F32, kind="ExternalInput").ap()
    cst_d = nc.dram_tensor("cst", [128, 384], F32, kind="ExternalInput").ap()
    xseq = nc.dram_tensor("xseq", [128, NK, T], F32, kind="ExternalInput").ap()
    xsmp = nc.dram_tensor("xsmp", [128, NK, NS * NSMP], F32, kind="ExternalInput").ap()
    s0_d = nc.dram_tensor("s0", [NS, 128, 8, 128], F32, kind="ExternalInput").ap()
    h0_d = nc.dram_tensor("h0", [128, NS, 8], F32, kind="ExternalInput").ap()
    c0_d = nc.dram_tensor("c0", [128, NS, 8, 3], F32, kind="ExternalInput").ap()
    yseq = nc.dram_tensor("yseq", [128, NK, T], F32, kind="ExternalOutput").ap()
    ysmp = nc.dram_tensor("ysmp", [128, NK, NS * NSMP], F32, kind="ExternalOutput").ap()
    sout = nc.dram_tensor("sout", [1 + NS, 128, 8, 128], F32, kind="ExternalOutput").ap()
    hout = nc.dram_tensor("hout", [128, 1 + NS, 8], F32, kind="ExternalOutput").ap()
    cout = nc.dram_tensor("cout", [128, 1 + NS, 8, 3], F32, kind="ExternalOutput").ap()
    wb = nc.dram_tensor("wb", [128, WTOT], BF16, kind="Internal").ap()

    cur = [16512]

    def alloc(name, shape, dt, at=None):
        nbytes = int(np.prod(shape[1:])) * (4 if dt == F32 else 2)
        if at is None:
            off = cur[0]
            cur[0] += (nbytes + 31) // 32 * 32
        else:
            off = at
        return nc.alloc_sbuf_tensor_at(name, list(shape), dt, offset=off)

    X = alloc("X", [128, NK, 512], F32)
    XN = alloc("XN", [128, NK, 512], BF16)
    RING = alloc("RING", [128, NSLOT, USZ], BF16)
    A0 = alloc("A0", [128, 8, 520], F32)
    a1_off = cur[0]
    A1 = alloc("A1", [128, 8, 520], F32)
    ON = alloc("ON", [128, 8, 512], BF16, at=a1_off)
    A2 = alloc("A2", [128, 8, 520], F32)
    VB = alloc("VB", [128, 4096], BF16)
    SQ = alloc("SQ", [128, NK, 512], BF16)
    RSTD = alloc("RSTD", [128, 512], F32)
    SS = [alloc(f"S{i}", [128, 8, 128], F32) for i in range(2)]
    SBF = [alloc(f"SBF{i}", [128, 8, 128], BF16) for i in range(2)]
    HST = alloc("HST", [128, 8], F32)
    CVS = alloc("CVS", [128, 8, 3], F32)
    PAR = alloc("PAR", [128, NPAR, 8], F32)
    DER = alloc("DER", [128, 8, 8], F32)
    CST = alloc("CST", [128, 384], F32)
    IDB = alloc("IDB", [128, 128], BF16)
    ONESB = alloc("ONESB", [128, 128], BF16)
    PM = alloc("PM", [128, 4], F32)
    su0 = cur[0]
    Hh = alloc("H", [128, NF, 512], BF16)
    Tt = [alloc(f"T{i}", [128, 512], F32) for i in range(2)]
    su_ffn_end = cur[0]
    cur[0] = su0
    G2 = [alloc(f"G{i}", [128, 8, 129], F32) for i in range(2)]
    W2 = [alloc(f"W{i}", [128, 8, 128], F32) for i in range(2)]
    E12 = [alloc(f"E1{i}", [128, 8, 128], F32) for i in range(2)]
    QD2 = [alloc(f"QD{i}", [128, 8, 128], BF16) for i in range(2)]
    KD2 = [alloc(f"KD{i}", [128, 8, 128], BF16) for i in range(2)]
    KPT2 = [[alloc(f"KPT{i}_{m}", [128, 1024], BF16) for m in range(4)] for i in range(2)]
    OO = alloc("OO", [128, 8, 128], F32)
    RSO = alloc("RSO", [128, 8, 128], F32)
    AT = alloc("AT", [128, 8, 128], BF16)
    SQO = alloc("SQO", [128, 8, 128], BF16)
    su_a_end = cur[0]
    cur[0] = su0
    CBt = [alloc(f"CB{i}", [128, 512], BF16) for i in range(2)]
    RA = alloc("RA", [128, 8, 512], F32)
    IGA = alloc("IGA", [128, 8, 512], F32)
    AA = alloc("AA", [128, 8, 512], F32)
    Ut = [alloc(f"U{i}", [128, 512], F32) for i in range(2)]
    HNt = [alloc(f"HN{i}", [128, 512], F32) for i in range(2)]
    su_b_end = cur[0]
    cur[0] = max(su_ffn_end, su_a_end, su_b_end)
    assert cur[0] <= 229344, cur[0]

    PS = [nc.alloc_psum_tensor(f"ps{i}", [128, 512], F32) for i in range(8)]

    Xb = [Buf(f"X{k}") for k in range(NK)]
    XNb = [Buf(f"XN{k}") for k in range(NK)]
    RSb = [Buf(f"RS{i}") for i in range(NSLOT)]
    A0b = [Buf(f"A0_{h}") for h in range(8)]
    A1b = [Buf(f"A1_{h}") for h in range(8)]
    A2b = [Buf(f"A2_{h}") for h in range(8)]
    VBb = Buf("VB")
    ONb = Buf("ON")
    SQb = Buf("SQ")
    RSTDb = Buf("RSTD")
    SSb = [[Buf(f"S{i}_{h}") for h in range(8)] for i in range(2)]
    SBFb = [Buf("SBF0"), Buf("SBF1")]
    HSTb = Buf("HST")
    HSTnb = [Buf(f"HST{n}") for n in range(8)]
    Ynb = [Buf(f"Y{n}") for n in range(8)]
    CVSb = Buf("CVS")
    PARb = Buf("PAR")
    DERb = Buf("DER")
    CSTb = Buf("CST")
    Hb = [Buf(f"H{m}") for m in range(NF)]
    Tb = [Buf("T0"), Buf("T1")]
    FFNB = Hb + Tb
    G2b = [Buf("G0"), Buf("G1")]
    W2b = [Buf("W0"), Buf("W1")]
    E12b = [Buf("E10"), Buf("E11")]
    QD2b = [Buf("QD0"), Buf("QD1")]
    KD2b = [Buf("KD0"), Buf("KD1")]
    KPT2b = [[Buf(f"KPT{i}_{m}") for m in range(4)] for i in range(2)]
    OOb, RSOb, ATb, SQOb = Buf("OO"), Buf("RSO"), Buf("AT"), Buf("SQO")
    TILEB = G2b + W2b + E12b + QD2b + KD2b + KPT2b[0] + KPT2b[1] + [OOb, RSOb, ATb, SQOb]
    CBb = [Buf("CB0"), Buf("CB1")]
    Rb = [Buf(f"R{n}") for n in range(8)]
    IGb = [Buf(f"IG{n}") for n in range(8)]
    Ab = [Buf(f"AA{n}") for n in range(8)]
    Ub = [Buf("U0"), Buf("U1")]
    HNb = [Buf("HN0"), Buf("HN1")]
    BPHB = CBb + Rb + IGb + Ab + Ub + HNb
    PB = [Buf(f"PB{i}") for i in range(8)]
    WBb = [Buf(f"WB{u}") for u in range(NUNIT)]
    CGb = [Buf(f"CG{i}") for i in range(8)]
    YOUTb = Buf("YOUT")
    XLDb = Buf("XLD")
    STb = Buf("STIO")
    STHb = Buf("STH")
    STCb = Buf("STC")
    LDSb = Buf("LDS")
    LDHb = Buf("LDH")
    LDCb = Buf("LDC")

    fw.dma("pool", PAR[:].rearrange("p a b -> p (a b)"), par_d, owner=PARb, writes=[PARb])
    fw.dma("pool", CST[:], cst_d, owner=CSTb, writes=[CSTb])

    IDF = CST[:, 0:128]
    MASK = CST[:, 128:256]
    ONESF = CST[:, 256:384]

    def par(i, h=None):
        if h is None:
            return PAR[:, i, :]
        return PAR[:, i, h:h + 1]

    def der(i, h=None):
        if h is None:
            return DER[:, i, :]
        return DER[:, i, h:h + 1]
    D_LB, D_OML, D_NOML, D_CL, D_CL2, D_T0, D_T1, D_T2 = range(8)

    st = {"next_dma": 0, "next_use": 0, "sbf": 0, "s": 0}
    total_uses = [0]

    CAST_AHEAD = 14
    cst_ = {"next": 0}

    def emit_cast_upto(u_hi):
        while cst_["next"] <= min(u_hi, NUNIT - 1):
            u = cst_["next"]
            o, sz = UOFF[u], UNITS[u][2]
            cg = CGb[u % 8]
            fw.dma("pool", wb[:, o:o + sz], wall[:, o:o + sz], owner=cg, writes=[cg, WBb[u]])
            cst_["next"] += 1

    def emit_cast():
        emit_cast_upto(CAST_AHEAD)

    def emit_wload(s):
        u = s % NUNIT
        o, sz = UOFF[u], UNITS[u][2]
        slot = s % NSLOT
        fw.dma("sp", RING[:, slot, 0:sz], wb[:, o:o + sz], owner=RSb[slot],
               reads=[WBb[u]], writes=[RSb[slot]])

    def use_unit(kind):
        s = st["next_use"]
        assert UNITS[s % NUNIT][0] == kind, (UNITS[s % NUNIT], kind)
        if cst_["next"] < NUNIT:
            emit_cast_upto(s + CAST_AHEAD)
        while st["next_dma"] <= min(s + NSLOT - 1, total_uses[0] - 1):
            emit_wload(st["next_dma"])
            st["next_dma"] += 1
        st["next_use"] += 1
        slot = s % NSLOT
        return RING[:, slot, :], RSb[slot]

    def mm(out, lhsT, rhs, start, stop, reads, writes):
        fw.op("pe", lambda: PE.matmul(out, lhsT=lhsT, rhs=rhs, start=start, stop=stop,
                                      skip_group_check=True), reads, writes)

    def act(out, in_, func, reads, writes, **kw):
        fw.op("act", lambda: A.activation(out=out, in_=in_, func=func, **kw), reads, writes)

    SQkb = [Buf(f"SQ{k}") for k in range(NK)]
    nst = {"presq": False}

    def norm_sq(kc, N):
        act(SQ[:, kc, :N], X[:, kc, :N], AF.Square, [Xb[kc]], [SQkb[kc]])

    def norm_mm(kc, N):
        mm(PS[6][:, :N], ONESB[:], SQ[:, kc, :N], kc == 0, kc == NK - 1, [SQkb[kc], CSTb], [PB[6]])

    def rmsnorm(gidx, N, out_f32=None, out_bufs=None):
        if not nst["presq"]:
            for kc in range(NK):
                norm_sq(kc, N)
                norm_mm(kc, N)
        nst["presq"] = False
        act(RSTD[:, :N], PS[6][:, :N], AF.Sqrt, [PB[6]], [RSTDb], scale=1.0 / D, bias=EPS)
        fw.op("dve", lambda: V.reciprocal(out=RSTD[:, :N], in_=RSTD[:, :N]), [RSTDb], [RSTDb])
        for kc in range(NK):
            if out_f32 is None:
                o, ob = XN[:, kc, :N], [XNb[kc]]
            else:
                o, ob = out_f32[:, kc, :N], [out_bufs[kc]]
            fw.op("dve", lambda o=o, kc=kc: V.scalar_tensor_tensor(
                out=o, in0=X[:, kc, :N], scalar=par(gidx, kc), in1=RSTD[:, :N],
                op0=ALU.mult, op1=ALU.mult), [Xb[kc], RSTDb, PARb], ob)

    def ffn(fi, N):
        rmsnorm(P_FFN0 + fi, N)
        for j in range(11):
            u, ub = use_unit("ffn_in")
            uv = u.rearrange("p (s k c) -> p s k c", s=2, k=NK)
            for mm_ in range(2):
                m = 2 * j + mm_
                pa, pab = PS[m % 2], PB[m % 2]
                pb, pbb = PS[2 + m % 2], PB[2 + m % 2]
                for kc in range(NK):
                    mm(pa[:, :N], uv[:, 0, kc, mm_ * 128:(mm_ + 1) * 128], XN[:, kc, :N],
                       kc == 0, kc == NK - 1, [ub, XNb[kc]], [pab])
                for kc in range(NK):
                    mm(pb[:, :N], uv[:, 1, kc, mm_ * 128:(mm_ + 1) * 128], XN[:, kc, :N],
                       kc == 0, kc == NK - 1, [ub, XNb[kc]], [pbb])
                t, tb = Tt[m % 2], Tb[m % 2]
                act(t[:, :N], pa[:, :N], AF.Silu, [pab], [tb])
                fw.op("dve", lambda t=t, pb=pb, m=m: V.tensor_tensor(
                    out=Hh[:, m, :N], in0=t[:, :N], in1=pb[:, :N], op=ALU.mult), [tb, pbb], [Hb[m]])
        for mo in range(NK):
            u, ub = use_unit("ffn_out")
            uv = u[:, 0:NF * 128].rearrange("p (k c) -> p k c", k=NF)
            py, pyb = PS[4 + mo % 2], PB[4 + mo % 2]
            for kf in range(NF):
                mm(py[:, :N], uv[:, kf, :], Hh[:, kf, :N], kf == 0, kf == NF - 1, [ub, Hb[kf]], [pyb])
            if mo >= 1:
                norm_mm(mo - 1, N)
            fw.op("dve", lambda py=py, mo=mo: V.scalar_tensor_tensor(
                out=X[:, mo, :N], in0=py[:, :N], scalar=0.5, in1=X[:, mo, :N],
                op0=ALU.mult, op1=ALU.add), [pyb, Xb[mo]], [Xb[mo]])
            norm_sq(mo, N)
        norm_mm(NK - 1, N)
        nst["presq"] = True

    def proj_fm(kind, nblk, N, evac):
        u = ub = uv = None
        for blk in range(nblk):
            if blk % 4 == 0:
                u, ub = use_unit(kind)
                uv = u.rearrange("p (k c) -> p k c", k=NK)
            p, pbuf = PS[blk % 2], PB[blk % 2]
            for kc in range(NK):
                mm(p[:, :N], uv[:, kc, (blk % 4) * 128:(blk % 4 + 1) * 128], XN[:, kc, :N],
                   kc == 0, kc == NK - 1, [ub, XNb[kc]], [pbuf])
            evac(blk, p, pbuf)

    def out_proj(kind, N, src, src_bufs):
        u = ub = uv = None
        for mo in range(NK):
            if mo % 4 == 0:
                u, ub = use_unit(kind)
                uv = u.rearrange("p (k c) -> p k c", k=NK)
            py, pyb = PS[4 + mo % 2], PB[4 + mo % 2]
            for kh in range(NK):
                mm(py[:, :N], uv[:, kh, (mo % 4) * 128:(mo % 4 + 1) * 128], src[:, kh, :N],
                   kh == 0, kh == NK - 1, [ub] + src_bufs, [pyb])
            if mo >= 1:
                norm_mm(mo - 1, N)
            fw.op("dve", lambda py=py, mo=mo: V.tensor_tensor(
                out=X[:, mo, :N], in0=py[:, :N], in1=X[:, mo, :N], op=ALU.add), [pyb, Xb[mo]], [Xb[mo]])
            norm_sq(mo, N)
        norm_mm(NK - 1, N)
        nst["presq"] = True

    def mixer_a(N):
        TN = min(N, 128)
        ntile = N // TN
        nb = TN // 16
        rmsnorm(P_MIX0 + 0, N)
        Q, K, LF = A0, A1, A2
        Vv = VB[:].rearrange("p (t c) -> p t c", t=4)

        def ev_q(h, p, pbuf):
            act(Q[:, h, :N], p[:, :N], AF.Silu, [pbuf], [A0b[h]])
        import os
        kstop = os.environ.get("KSTOP", "")
        proj_fm("a_in", 8, N, ev_q)
        if kstop == "q":
            skip(8)
            return

        def ev_f(h, p, pbuf):
            act(K[:, h, :N], p[:, :N], AF.Sigmoid, [pbuf], [A1b[h]])
        proj_fm("a_in", 8, N, ev_f)
        for h in range(8):
            act(LF[:, h, :N], K[:, h, :N], AF.Ln, [A1b[h], DERb], [A2b[h]],
                scale=der(D_OML, h), bias=der(D_LB, h))
            fw.op("dve", lambda h=h: V.tensor_scalar(
                out=K[:, h, :N], in0=K[:, h, :N], scalar1=der(D_NOML, h), scalar2=der(D_OML, h),
                op0=ALU.mult, op1=ALU.add), [A1b[h], DERb], [A1b[h]])
        if kstop == "f":
            skip(6)
            return

        idx = 0
        for c2 in range(2):
            u, ub = use_unit("a_in")
            uv = u.rearrange("p (k c) -> p k c", k=NK)
            for tt in range(ntile):
                p, pbuf = PS[2 + idx % 2], PB[2 + idx % 2]
                for kc in range(NK):
                    mm(p[:TN, :], XN[:, kc, tt * TN:(tt + 1) * TN], uv[:, kc, :],
                       kc == 0, kc == NK - 1, [ub, XNb[kc]], [pbuf])
                if idx % 2 == 0:
                    act(Vv[:TN, tt, c2 * 512:(c2 + 1) * 512], p[:TN, :], AF.Copy, [pbuf], [VBb])
                else:
                    fw.op("dve", lambda p=p, tt=tt, c2=c2: V.tensor_copy(
                        out=Vv[:TN, tt, c2 * 512:(c2 + 1) * 512], in_=p[:TN, :]), [pbuf], [VBb])
                idx += 1

        if kstop == "v":
            skip(4)
            return
        fw.handoff(FFNB, TILEB)
        import os
        sub = int(os.environ.get("KSUB", "99"))
        GP = nc.gpsimd
        for i in range(2 if ntile > 1 else 1):
            fw.op("pool", lambda i=i: GP.memset(G2[i][:], 0.0), [], [G2b[i]])
        opb = [PB[1], PB[2]]

        def prep_stages(tt):
            pr = tt % 2
            G, W, E1, QD, KD, KPT = G2[pr], W2[pr], E12[pr], QD2[pr], KD2[pr], KPT2[pr]
            Gb, Wb, E1b, QDb, KDb, KPTb = G2b[pr], W2b[pr], E12b[pr], QD2b[pr], KD2b[pr], KPT2b[pr]
            tsl = slice(tt * TN, (tt + 1) * TN)

            def v4(ap):
                return ap.rearrange("p h (b j) -> p h b j", j=16)

            def s0():
                for h in range(8):
                    fw.op("dve", lambda h=h: V.tensor_tensor_scan(
                        out=G[:, h, 1:TN + 1], data0=ONESF[:, :TN], data1=LF[:, h, tsl], initial=0.0,
                        op0=ALU.mult, op1=ALU.add), [A2b[h], CSTb], [Gb])

            def s1():
                g1 = v4(G[:, :, 1:TN + 1])
                g0 = v4(G[:, :, 0:TN])[:, :, :, 0:1].to_broadcast([128, 8, nb, 16])
                fw.op("pool", lambda: GP.tensor_tensor(out=v4(W[:, :, :TN]), in0=g1, in1=g0, op=ALU.subtract),
                      [Gb], [Wb])

            def s2():
                act(E1[:, :, :TN], W[:, :, :TN], AF.Exp, [Wb], [E1b])
                act(W[:, :, :TN], W[:, :, :TN], AF.Exp, [Wb], [Wb], scale=-1.0)

            def s3():
                fw.op("pool", lambda: GP.tensor_tensor(out=QD[:, :, :TN], in0=Q[:, :, tsl], in1=E1[:, :, :TN],
                                                       op=ALU.mult), A0b + [E1b], [QDb])
                fw.op("pool", lambda: GP.tensor_tensor(out=W[:, :, :TN], in0=K[:, :, tsl], in1=W[:, :, :TN],
                                                       op=ALU.mult), A1b + [Wb], [Wb])

            def s4():
                act(KD[:, :, :TN], W[:, :, :TN], AF.Copy, [Wb], [KDb])
                e1l = v4(E1[:, :, :TN])[:, :, :, 15:16].to_broadcast([128, 8, nb, 16])
                fw.op("pool", lambda: GP.tensor_tensor(out=v4(W[:, :, :TN]), in0=v4(W[:, :, :TN]), in1=e1l,
                                                       op=ALU.mult), [Wb, E1b], [Wb])

            def s5(hf):
                def f():
                    bank = 7 if hf == 0 else 0
                    for h in range(4 * hf, 4 * hf + 4):
                        fw.op("pe", lambda h=h: PE.transpose(PS[bank][:TN, (h % 4) * 128:(h % 4 + 1) * 128],
                                                             W[:, h, :TN], IDF), [Wb, CSTb], [PB[bank]])
                    nmask = 4 if TN == 128 else 1
                    for m4 in range(nmask):
                        fw.op("dve", lambda m4=m4: V.tensor_scalar(
                            out=KPT[m4][:TN, hf * 512:(hf + 1) * 512], in0=PS[bank][:TN, :],
                            scalar1=PM[:TN, m4:m4 + 1], scalar2=None, op0=ALU.mult),
                            [PB[bank], CSTb], [KPTb[m4]])
                return f
            return [s0, s1, s2, s3, s4, s5(0), s5(1)]

        SLOTS = {0: [0, 1], 2: [2], 3: [3], 5: [4], 6: [5], 7: [6]}

        def front_scores(tt):
            pr = tt % 2
            QD, KD, QDb, KDb = QD2[pr], KD2[pr], QD2b[pr], KD2b[pr]
            for hg in range(2):
                sb = 0 if hg == 0 else 7
                for h in range(hg * 4, hg * 4 + 4):
                    mm(PS[sb][:TN, (h % 4) * TN:(h % 4 + 1) * TN], KD[:, h, :TN], QD[:, h, :TN],
                       h % 4 == 0, h % 4 == 3, [KDb, QDb], [PB[sb]])
                scv = PS[sb][:TN, 0:4 * TN].rearrange("p (h t) -> p h t", h=4)
                mk = MASK[:TN, :TN].unsqueeze(1).to_broadcast([TN, 4, TN])
                fw.op("dve", lambda scv=scv, mk=mk, hg=hg: V.tensor_tensor(
                    out=AT[:TN, hg * 4:hg * 4 + 4, :TN], in0=scv, in1=mk, op=ALU.mult), [PB[sb], CSTb], [ATb])

        def emit_u(tt, n):
            KPT, KPTb = KPT2[tt % 2], KPT2b[tt % 2]
            if TN == 128:
                r0, kr, kp, kpb = 64 * (n // 4), 64, KPT[n % 4], KPTb[n % 4]
            else:
                r0, kr, kp, kpb = 0, 16, KPT[0], KPTb[0]
            bk = 3 + 2 * (n % 2)
            for h in range(8):
                mm(PS[bk + h // 4][:, (h % 4) * 128:(h % 4 + 1) * 128],
                   kp[r0:r0 + kr, h * 128:(h + 1) * 128], Vv[r0:r0 + kr, tt, h * 128:(h + 1) * 128],
                   h % 4 == 0, h % 4 == 3, [kpb, VBb], [PB[bk + h // 4]])

        gst = {"u": None, "done": False}

        def gproj_head(h):
            if h % 4 == 0:
                gst["u"] = use_unit("a_in")
            u, ub = gst["u"]
            uv = u.rearrange("p (k c) -> p k c", k=NK)
            bank = 0 if h % 2 == 0 else 7
            for kc in range(NK):
                mm(PS[bank][:, :N], uv[:, kc, (h % 4) * 128:(h % 4 + 1) * 128], XN[:, kc, :N],
                   kc == 0, kc == NK - 1, [ub, XNb[kc]], [PB[bank]])
            act(LF[:, h, :N], PS[bank][:, :N], AF.Silu, [PB[bank]], [A2b[h]])

        def tile_body(tt, nxt):
            pr = tt % 2
            E1, QD, KD, KPT = E12[pr], QD2[pr], KD2[pr], KPT2[pr]
            E1b, QDb, KDb, KPTb = E12b[pr], QD2b[pr], KD2b[pr], KPT2b[pr]
            tsl = slice(tt * TN, (tt + 1) * TN)
            for hg in range(2):
                for h in range(hg * 4, hg * 4 + 4):
                    mm(PS[1 + hg][:, (h % 4) * TN:(h % 4 + 1) * TN], Vv[:TN, tt, h * 128:(h + 1) * 128],
                       AT[:TN, h, :TN], h % 4 == 0, False, [VBb, ATb], [opb[hg]])

            if tt == 0:
                emit_u(tt, 0)
            for n in range(nb):
                if n + 1 < nb:
                    emit_u(tt, n + 1)
                cs = st["sbf"]
                for h in range(8):
                    mm(PS[1 + h // 4][:, (h % 4) * TN + 16 * n:(h % 4) * TN + 16 * n + 16],
                       SBF[cs][:, h, :], QD[:, h, 16 * n:16 * n + 16], False, False,
                       [SBFb[cs], QDb], [opb[h // 4]])
                bk = 3 + 2 * (n % 2)
                si = st["s"]
                so = 1 - si
                for h in range(8):
                    fw.op("dve", lambda h=h, bk=bk, n=n, si=si, so=so: V.scalar_tensor_tensor(
                        out=SS[so][:, h, :], in0=SS[si][:, h, :], scalar=E1[:, h, 16 * n + 15:16 * n + 16],
                        in1=PS[bk + h // 4][:, (h % 4) * 128:(h % 4 + 1) * 128],
                        op0=ALU.mult, op1=ALU.add), [SSb[si][h], E1b, PB[bk + h // 4]], [SSb[so][h]])
                st["s"] = so
                ns = 1 - cs
                act(SBF[ns][:], SS[so][:], AF.Copy, SSb[so], [SBFb[ns]])
                st["sbf"] = ns
                if nxt:
                    for si_ in SLOTS.get(n, []):
                        nxt[si_]()
                if tt == ntile - 1 and nb == 8 and sub >= 7:
                    gproj_head(n)
                    gst["done"] = True
            for hg in range(2):
                pv = PS[1 + hg][:, 0:4 * TN].rearrange("p (h t) -> p h t", h=4)
                act(OO[:, hg * 4:hg * 4 + 4, :TN], pv, AF.Copy, [opb[hg]], [OOb])
            if tt + 1 < ntile:
                front_scores(tt + 1)
                emit_u(tt + 1, 0)
            fw.op("pool", lambda: GP.tensor_tensor(out=SQO[:, :, :TN], in0=OO[:, :, :TN], in1=OO[:, :, :TN],
                                                   op=ALU.mult), [OOb], [SQOb])
            for h in range(8):
                mm(PS[5 + h // 4][:, (h % 4) * TN:(h % 4 + 1) * TN], ONESB[:], SQO[:, h, :TN],
                   h % 4 == 0, h % 4 == 3, [SQOb, CSTb], [PB[5 + h // 4]])
            for hg in range(2):
                pv = PS[5 + hg][:, 0:4 * TN].rearrange("p (h t) -> p h t", h=4)
                act(RSO[:, hg * 4:hg * 4 + 4, :TN], pv, AF.Ln, [PB[5 + hg]], [RSOb], scale=1.0 / 128, bias=EPS)
            act(RSO[:, :, :TN], RSO[:, :, :TN], AF.Exp, [RSOb], [RSOb], scale=-0.5)
            fw.op("pool", lambda: GP.tensor_tensor(out=Q[:, :, tsl], in0=OO[:, :, :TN], in1=RSO[:, :, :TN],
                                                   op=ALU.mult), [OOb, RSOb], A0b)

        for f in prep_stages(0):
            f()
        front_scores(0)
        for tt in range(ntile):
            nxt = prep_stages(tt + 1) if tt + 1 < ntile else []
            tile_body(tt, nxt)

        fw.handoff(TILEB, FFNB)
        if sub < 7:
            skip(4)
            return

        def ev_g2(h):
            fw.op("dve", lambda: V.scalar_tensor_tensor(
                out=ON[:, h, :N], in0=Q[:, h, :N], scalar=par(P_ONORM, h), in1=LF[:, h, :N],
                op0=ALU.mult, op1=ALU.mult), [A0b[h], A2b[h], PARb], A1b)

        def ev_g(h, p, pbuf):
            act(LF[:, h, :N], p[:, :N], AF.Silu, [pbuf], [A2b[h]])
            fw.op("dve", lambda: V.scalar_tensor_tensor(
                out=ON[:, h, :N], in0=Q[:, h, :N], scalar=par(P_ONORM, h), in1=LF[:, h, :N],
                op0=ALU.mult, op1=ALU.mult), [A0b[h], A2b[h], PARb], A1b)
        if gst["done"]:
            for h in range(8):
                ev_g2(h)
        else:
            proj_fm("a_in", 8, N, ev_g)
        out_proj("a_out", N, ON, A1b)

    def mixer_b(N):
        rmsnorm(P_MIX0 + 1, N)
        XBr, GG, CV = A0, A1, A2
        Y = VB[:].rearrange("p (k t) -> p k t", k=8)

        fw.op("dve", lambda: V.tensor_copy(out=XBr[:, :, 0:3], in_=CVS[:]), [CVSb], A0b)

        def ev_x(n, p, pbuf):
            act(XBr[:, n, 3:3 + N], p[:, :N], AF.Copy, [pbuf], [A0b[n]])
        proj_fm("b_in", 8, N, ev_x)

        def ev_g(n, p, pbuf):
            act(GG[:, n, :N], p[:, :N], AF.Gelu, [pbuf], [A1b[n]])
        proj_fm("b_in", 8, N, ev_g)

        fw.handoff(FFNB, BPHB)
        u, ub = use_unit("b_gate")
        uv = u[:, 0:2048].rearrange("p (s n d) -> p s n d", s=2, n=8)
        fw.handoff([VBb, HSTb], Ynb + HSTnb)

        def stage_a(n):
            i2 = n % 2
            fw.op("pool", lambda n=n: nc.gpsimd.tensor_scalar(
                out=CV[:, n, :N], in0=XBr[:, n, 0:N], scalar1=par(P_CW0 + 0, n), scalar2=par(P_CB, n),
                op0=ALU.mult, op1=ALU.add), [A0b[n], PARb], [A2b[n]])
            for j in range(1, 4):
                fw.op("dve", lambda n=n, j=j: V.scalar_tensor_tensor(
                    out=CV[:, n, :N], in0=XBr[:, n, j:j + N], scalar=par(P_CW0 + j, n), in1=CV[:, n, :N],
                    op0=ALU.mult, op1=ALU.add), [A0b[n], A2b[n], PARb], [A2b[n]])
            act(CBt[i2][:, :N], CV[:, n, :N], AF.Copy, [A2b[n]], [CBb[i2]])
            pr, prb = PS[2 + 2 * i2], PB[2 + 2 * i2]
            pi, pib = PS[3 + 2 * i2], PB[3 + 2 * i2]
            mm(pr[:, :N], uv[:, 0, n, :], CBt[i2][:, :N], True, True, [ub, CBb[i2]], [prb])
            mm(pi[:, :N], uv[:, 1, n, :], CBt[i2][:, :N], True, True, [ub, CBb[i2]], [pib])
            act(RA[:, n, :N], pr[:, :N], AF.Sigmoid, [prb, PARb], [Rb[n]], bias=par(P_BA, n))
            act(IGA[:, n, :N], pi[:, :N], AF.Sigmoid, [pib, PARb], [IGb[n]], bias=par(P_BX, n))

        def stage_b(n):
            i2 = n % 2
            fw.op("dve", lambda i2=i2, n=n: V.tensor_tensor(
                out=Ut[i2][:, :N], in0=IGA[:, n, :N], in1=RA[:, n, :N], op=ALU.mult), [IGb[n], Rb[n]], [Ub[i2]])
            fw.op("dve", lambda i2=i2, n=n: V.tensor_tensor_scan(
                out=HNt[i2][:, :N], data0=AA[:, n, :N], data1=Ut[i2][:, :N], initial=HST[:, n:n + 1],
                op0=ALU.mult, op1=ALU.add), [Ab[n], Ub[i2], HSTnb[n]], [HNb[i2]])
            fw.op("dve", lambda i2=i2, n=n: V.tensor_copy(out=HST[:, n:n + 1], in_=HNt[i2][:, N - 1:N]),
                  [HNb[i2]], [HSTnb[n]])
            fw.op("dve", lambda i2=i2, n=n: V.tensor_tensor(
                out=Y[:, n, :N], in0=HNt[i2][:, :N], in1=GG[:, n, :N], op=ALU.mult), [HNb[i2], A1b[n]], [Ynb[n]])

        def ig_cv(n):
            fw.op("dve", lambda n=n: V.tensor_tensor(
                out=IGA[:, n, :N], in0=IGA[:, n, :N], in1=CV[:, n, :N], op=ALU.mult), [IGb[n], A2b[n]], [IGb[n]])

        for n in range(8):
            stage_a(n)
            if n >= 2:
                ig_cv(n - 2)
        ig_cv(6)
        ig_cv(7)
        for half in range(2):
            ns_ = range(4 * half, 4 * half + 4)
            for n in ns_:
                act(AA[:, n, :N], RA[:, n, :N], AF.Exp, [Rb[n], DERb], [Ab[n]], scale=der(D_CL, n))
                act(RA[:, n, :N], RA[:, n, :N], AF.Exp, [Rb[n], DERb], [Rb[n]], scale=der(D_CL2, n))
            for n in ns_:
                act(RA[:, n, :N], RA[:, n, :N], AF.Sqrt, [Rb[n]], [Rb[n]], scale=-1.0, bias=1.0)
        for n in range(8):
            stage_b(n)
        fw.op("dve", lambda: V.tensor_copy(out=CVS[:], in_=XBr[:, :, N:N + 3]), A0b, [CVSb])
        fw.handoff(BPHB, FFNB)
        out_proj("b_out", N, Y, Ynb)
        fw.handoff(Ynb + HSTnb, [VBb, HSTb])

    LEVELS = ["io", "ffn0", "mixa", "ffn1", "ffn2", "mixb", "ffn3", "all"]
    lvl = LEVELS.index(dbg) if dbg else len(LEVELS) - 1

    def skip(k):
        st["next_use"] += k

    def prefetch(src, N):
        fw.dma("pool", A0[:, :, :N], src, owner=XLDb, writes=A0b)

    def chunk(src, dst, N, first, nxt):
        if first:
            prefetch(src, N)
            emit_cast()
        for kc in range(NK):
            act(X[:, kc, :N], A0[:, kc, :N], AF.Copy, [A0b[kc]], [Xb[kc]])
        nst["presq"] = False
        if lvl >= 1:
            ffn(0, N)
        else:
            skip(19)
        if lvl >= 2:
            mixer_a(N)
        else:
            skip(10)
        if lvl >= 3:
            ffn(1, N)
        else:
            skip(19)
        if lvl >= 4:
            ffn(2, N)
        else:
            skip(19)
        if lvl >= 5:
            mixer_b(N)
        else:
            skip(7)
        if nxt is not None:
            prefetch(nxt[0], nxt[1])
        if lvl >= 6:
            ffn(3, N)
        else:
            skip(19)
        if lvl >= 7:
            rmsnorm(P_FIN, N, out_f32=A1, out_bufs=A1b)
        else:
            fw.op("dve", lambda: V.tensor_copy(out=A1[:, :, :N], in_=X[:, :, :N]), Xb, A1b)
        fw.dma("pool", dst, A1[:, :, :N], owner=YOUTb, reads=A1b, writes=[YOUTb])

    def write_states(idx):
        fw.dma("pool", sout[idx], SS[st["s"]][:], owner=STb, reads=SSb[st["s"]], writes=[STb])
        fw.dma("pool", hout[:, idx, :], HST[:], owner=STHb, reads=[HSTb], writes=[STHb])
        fw.dma("pool", cout[:, idx, :, :], CVS[:], owner=STCb, reads=[CVSb], writes=[STCb])

    chunks = []
    pos = 0
    while pos < T:
        n = min(512, T - pos)
        chunks.append((pos, n))
        pos += n
    total_uses[0] = NUNIT * (len(chunks) + NS)

    pm_d = nc.dram_tensor("pm", [128, 4], F32, kind="ExternalInput").ap()
    fw.dma("pool", PM[:], pm_d, owner=CSTb, writes=[CSTb])
    r = [PARb]
    w = [DERb]
    act(der(D_T0), par(P_LB0), AF.Exp, r, w)
    act(der(D_T1), par(P_LB1), AF.Exp, r, w)
    fw.op("dve", lambda: V.tensor_tensor(out=der(D_T2), in0=der(D_T0), in1=der(D_T1), op=ALU.add), [DERb], w)
    fw.op("dve", lambda: V.reciprocal(out=der(D_T2), in_=der(D_T2)), [DERb], w)
    fw.op("dve", lambda: V.tensor_tensor(out=der(D_LB), in0=der(D_T0), in1=der(D_T2), op=ALU.mult), [DERb], w)
    fw.op("dve", lambda: V.tensor_tensor(out=der(D_OML), in0=der(D_T1), in1=der(D_T2), op=ALU.mult), [DERb], w)
    fw.op("dve", lambda: V.tensor_scalar(out=der(D_NOML), in0=der(D_OML), scalar1=-1.0, scalar2=None, op0=ALU.mult), [DERb], w)
    act(der(D_T0), par(P_LAM), AF.Exp, r + [DERb], w, scale=-1.0)
    act(der(D_T0), der(D_T0), AF.Ln, [DERb], w, bias=1.0)
    fw.op("dve", lambda: V.tensor_scalar(out=der(D_CL), in0=der(D_T0), scalar1=-8.0, scalar2=None, op0=ALU.mult), [DERb], w)
    fw.op("dve", lambda: V.tensor_scalar(out=der(D_CL2), in0=der(D_T0), scalar1=-16.0, scalar2=None, op0=ALU.mult), [DERb], w)
    fw.op("dve", lambda: V.tensor_copy(out=IDB[:], in_=IDF), [CSTb], [CSTb])
    fw.op("dve", lambda: V.tensor_copy(out=ONESB[:], in_=ONESF), [CSTb], [CSTb])

    fw.op("dve", lambda: V.memset(SS[0][:], 0.0), [], SSb[0])
    fw.op("dve", lambda: V.memset(SBF[0][:], 0.0), [], [SBFb[0]])
    fw.op("dve", lambda: V.memset(HST[:], 0.0), [], [HSTb])
    fw.op("dve", lambda: V.memset(CVS[:], 0.0), [], [CVSb])
    st["sbf"] = 0
    jobs = [(xseq[:, :, pos:pos + n], yseq[:, :, pos:pos + n], n) for (pos, n) in chunks]
    for s in range(NS):
        jobs.append((xsmp[:, :, s * NSMP:(s + 1) * NSMP], ysmp[:, :, s * NSMP:(s + 1) * NSMP], NSMP))
    npr = len(chunks)
    for ji, (src, dst, n) in enumerate(jobs):
        nxt = (jobs[ji + 1][0], jobs[ji + 1][2]) if ji + 1 < len(jobs) else None
        if ji >= npr:
            sidx = ji - npr
            fw.dma("pool", SS[st["s"]][:], s0_d[sidx], owner=LDSb, reads=[], writes=SSb[st["s"]])
            fw.dma("pool", HST[:], h0_d[:, sidx, :], owner=LDHb, writes=[HSTb])
            fw.dma("pool", CVS[:], c0_d[:, sidx, :, :], owner=LDCb, writes=[CVSb])
            cs = st["sbf"]
            act(SBF[cs][:], SS[st["s"]][:], AF.Copy, SSb[st["s"]], [SBFb[cs]])
        chunk(src, dst, n, ji == 0, nxt)
        if ji == npr - 1:
            write_states(0)
        elif ji >= npr:
            write_states(1 + ji - npr)
    fw.final_wait("pool", [YOUTb, STb, STHb, STCb])
    assert st["next_use"] == total_uses[0], (st["next_use"], total_uses[0])
    return nc, fw


def fm(v):
    return np.ascontiguousarray(np.asarray(v, np.float32).reshape(8, 128).T)


def build_wall(ffn_w_in, ffn_w_out, a_w_in, a_w_out, b_w_in, b_wa, b_wx, b_w_out):
    wall = np.empty((128, WTOT), np.float32)

    def std(wm, u):
        blk = wm[:, u * 512:(u + 1) * 512].reshape(NK, 128, 512)
        return blk.transpose(1, 0, 2).reshape(128, -1)
    for ui, (kind, args, sz) in enumerate(UNITS):
        o = UOFF[ui]
        if kind == "ffn_in":
            fi, j = args
            wm = ffn_w_in[fi // 2, fi % 2].reshape(NK, 128, 2, DFF)[:, :, :, j * 256:(j + 1) * 256]
            blk = wm.transpose(1, 2, 0, 3).reshape(128, -1)
        elif kind == "ffn_out":
            fi, mo = args
            wm = ffn_w_out[fi // 2, fi % 2].reshape(NF, 128, 8, 128)[:, :, mo, :]
            blk = wm.transpose(1, 0, 2).reshape(128, -1)
        elif kind == "a_in":
            blk = std(a_w_in[0], args[0])
        elif kind == "a_out":
            blk = std(a_w_out[0], args[0])
        elif kind == "b_in":
            blk = std(b_w_in[0], args[0])
        elif kind == "b_out":
            blk = std(b_w_out[0], args[0])
        elif kind == "b_gate":
            blk = np.stack([b_wa[0], b_wx[0]], 0).transpose(2, 0, 1, 3).reshape(128, -1)
        assert blk.shape[1] == sz, (kind, blk.shape, sz)
        wall[:, o:o + sz] = blk
    return wall


def build_consts():
    cst = np.zeros((128, 384), np.float32)
    cst[:, 0:128] = np.eye(128, dtype=np.float32)
    s = np.arange(128)[:, None]
    t = np.arange(128)[None, :]
    cst[:, 128:256] = ((s // 16 == t // 16) & (s <= t)).astype(np.float32)
    cst[:, 256:384] = 1.0
    pm = np.zeros((128, 4), np.float32)
    blk = np.arange(128) // 16
    for m4 in range(4):
        pm[:, m4] = (blk % 4 == m4)
    return cst, pm


_CACHE = {}


def kernel(x_prompt, x_sample, state_hgrn, state_rglru, state_conv, meta_tokens,
           ffn_norm, ffn_w_in, ffn_w_out, mix_norm, a_w_in, a_lb, a_onorm, a_w_out,
           b_w_in, b_conv_w, b_conv_b, b_wa, b_ba, b_wx, b_bx, b_lambda, b_w_out, final_norm,
           _ncores=8):
    f32 = np.float32
    x_prompt = np.asarray(x_prompt, f32)
    x_sample = np.asarray(x_sample, f32)
    B, SEQ, _ = x_prompt.shape
    DB, DS, _ = x_sample.shape
    T = SEQ + NMETA
    NS = DB // _ncores
    import os
    dbg = os.environ.get("KDBG") or None
    key = (T, NS, DS, dbg)
    if key not in _CACHE:
        _CACHE[key] = build_program(T, NS, DS, dbg)
    nc, fw = _CACHE[key]

    wall = build_wall(np.asarray(ffn_w_in, f32), np.asarray(ffn_w_out, f32), np.asarray(a_w_in, f32),
                      np.asarray(a_w_out, f32), np.asarray(b_w_in, f32), np.asarray(b_wa, f32),
                      np.asarray(b_wx, f32), np.asarray(b_w_out, f32))
    pv = [None] * NPAR
    fn = np.asarray(ffn_norm, f32)
    for l in range(2):
        for i in range(2):
            pv[P_FFN0 + l * 2 + i] = fm(fn[l, i])
    mn = np.asarray(mix_norm, f32)
    pv[P_MIX0], pv[P_MIX0 + 1] = fm(mn[0]), fm(mn[1])
    pv[P_FIN] = fm(final_norm)
    alb = np.asarray(a_lb, f32)
    pv[P_LB0], pv[P_LB1] = fm(alb[0]), fm(alb[1])
    pv[P_ONORM] = fm(np.asarray(a_onorm, f32)[0])
    cw = np.asarray(b_conv_w, f32)[0]
    for j in range(4):
        pv[P_CW0 + j] = fm(cw[j])
    pv[P_CB] = fm(np.asarray(b_conv_b, f32)[0])
    pv[P_BA] = fm(np.asarray(b_ba, f32)[0])
    pv[P_BX] = fm(np.asarray(b_bx, f32)[0])
    pv[P_LAM] = fm(np.asarray(b_lambda, f32)[0])
    par = np.ascontiguousarray(np.stack(pv, 1).reshape(128, NPAR * 8))
    cst, pm = build_consts()

    meta = np.asarray(meta_tokens, f32)
    sh = np.asarray(state_hgrn, f32)[0]
    sr = np.asarray(state_rglru, f32)[0]
    sc = np.asarray(state_conv, f32)[0]

    owner = {0: 0, 1: 1, 4: 2, 5: 3} if _ncores == 8 else {i: i for i in range(min(B, _ncores))}
    in_maps = []
    for c in range(_ncores):
        if c in owner:
            xs = np.concatenate([meta, x_prompt[owner[c]]], 0)
            xseq = np.ascontiguousarray(xs.reshape(T, NK, 128).transpose(2, 1, 0))
        else:
            xseq = np.zeros((128, NK, T), f32)
        ss = slice(c * NS, (c + 1) * NS)
        xm = x_sample[ss].reshape(NS * DS, NK, 128).transpose(2, 1, 0)
        in_maps.append({
            "wall": wall, "par": par, "cst": cst, "pm": pm,
            "xseq": xseq, "xsmp": np.ascontiguousarray(xm),
            "s0": np.ascontiguousarray(sh[ss].transpose(0, 2, 1, 3)),
            "h0": np.ascontiguousarray(sr[ss].reshape(NS, 8, 128).transpose(2, 0, 1)),
            "c0": np.ascontiguousarray(sc[ss].reshape(NS, 3, 8, 128).transpose(3, 0, 2, 1)),
        })
    res = run_bass_kernel_spmd(nc, in_maps, core_ids=list(range(_ncores)))
    R = res.results

    y_prompt = np.empty((B, SEQ, D), f32)
    y_sample = np.empty((DB, DS, D), f32)
    hg_p = np.empty((1, B, 8, 128, 128), f32)
    hg_s = np.empty((1, DB, 8, 128, 128), f32)
    rg_p = np.empty((1, B, D), f32)
    rg_s = np.empty((1, DB, D), f32)
    cv_p = np.empty((1, B, 3, D), f32)
    cv_s = np.empty((1, DB, 3, D), f32)
    for c in range(_ncores):
        r = R[c]
        so, ho, co = r["sout"], r["hout"], r["cout"]
        if c in owner:
            b = owner[c]
            y_prompt[b] = r["yseq"].transpose(2, 1, 0).reshape(T, D)[NMETA:]
            hg_p[0, b] = so[0].transpose(1, 0, 2)
            rg_p[0, b] = ho[:, 0, :].T.reshape(D)
            cv_p[0, b] = co[:, 0].transpose(2, 1, 0).reshape(3, D)
        ym = r["ysmp"].transpose(2, 1, 0).reshape(NS, DS, D)
        for s in range(NS):
            g = c * NS + s
            y_sample[g] = ym[s]
            hg_s[0, g] = so[1 + s].transpose(1, 0, 2)
            rg_s[0, g] = ho[:, 1 + s, :].T.reshape(D)
            cv_s[0, g] = co[:, 1 + s].transpose(2, 1, 0).reshape(3, D)
    return (y_prompt, y_sample, hg_p, hg_s, rg_p, rg_s, cv_p, cv_s)
```
